# Optimizing a Trainium2 kernel written in Bass

```python
import jax, jax.numpy as jnp
from jax import lax
import numpy as np

D_MODEL = 1024
BATCH = 8
SEQ = 4096
DEPTH = 2

GRID_W = 64
CTX_LEN = 256
N_MIXERS = 2
EPS = 1e-6

SSD_EXPAND = 2
D_INNER = SSD_EXPAND * D_MODEL
SSD_HEADDIM = 64
SSD_HEADS = D_INNER // SSD_HEADDIM
SSD_GROUPS = 8
SSD_HPG = SSD_HEADS // SSD_GROUPS
SSD_STATE = 128
SSD_CONV_K = 5
SSD_CHUNK = 64
SSD_GN = SSD_GROUPS * SSD_STATE
SSD_CONV_DIM = D_INNER + 2 * SSD_GN
SSD_PROJ_DIM = D_INNER + SSD_CONV_DIM + 2 * SSD_HEADS

CONF_K = 31
CONF_DIM = D_MODEL
CONF_H = CONF_DIM // 2

FFN_HIDDEN = (((8 * D_MODEL + 2) // 3 + 255) // 256) * 256

N_SSD_LAYERS = (DEPTH + 1) // 2
N_CONF_LAYERS = DEPTH // 2

kernel_name = "hybrid_ssd_conformer_dit_trunk"


def rmsnorm(x, g):
    xf = x.astype(jnp.float32)
    y = xf * lax.rsqrt(jnp.mean(xf * xf, axis=-1, keepdims=True) + EPS)
    return (y * g.astype(jnp.float32)).astype(x.dtype)


def layernorm(x, g, b):
    xf = x.astype(jnp.float32)
    mu = jnp.mean(xf, axis=-1, keepdims=True)
    var = jnp.mean(jnp.square(xf - mu), axis=-1, keepdims=True)
    y = (xf - mu) * lax.rsqrt(var + EPS) * g.astype(jnp.float32) + b.astype(jnp.float32)
    return y.astype(x.dtype)


def modulate(h, shift, scale):
    return h * (1 + scale) + shift


def ada_params(sc, w, b):
    mod = sc @ w + b
    return [t[:, None, :] for t in jnp.split(mod, 6, axis=-1)]


def dwconv1d(x, w, b):
    k = w.shape[0]
    y = lax.conv_general_dilated(
        x, w[:, None, :].astype(x.dtype), window_strides=(1,),
        padding=[(k // 2, k // 2)], dimension_numbers=("NWC", "WIO", "NWC"),
        feature_group_count=x.shape[-1])
    return y + b


def axial_dwconv(u, rows, w, b):
    n, l, ch = u.shape
    g = u.reshape(n, rows, GRID_W, ch)
    hor = dwconv1d(g[..., :CONF_H].reshape(n * rows, GRID_W, CONF_H), w[:, :CONF_H], b[:CONF_H])
    hor = hor.reshape(n, rows, GRID_W, CONF_H)
    cv = ch - CONF_H
    ver_in = jnp.swapaxes(g[..., CONF_H:], 1, 2).reshape(n * GRID_W, rows, cv)
    ver = dwconv1d(ver_in, w[:, CONF_H:], b[CONF_H:]).reshape(n, GRID_W, rows, cv)
    ver = jnp.swapaxes(ver, 1, 2)
    return jnp.concatenate([hor, ver], axis=-1).reshape(n, l, ch)


def ssd_chunked(xh, dt, A, Bm, Cm, h0):
    n, l = xh.shape[:2]
    nc, q = l // SSD_CHUNK, SSD_CHUNK
    x = (xh * dt[..., None]).reshape(n, nc, q, SSD_GROUPS, SSD_HPG, SSD_HEADDIM)
    a_cum = jnp.cumsum((dt * A).reshape(n, nc, q, SSD_GROUPS, SSD_HPG), axis=2)
    Bc = Bm.reshape(n, nc, q, SSD_GROUPS, SSD_STATE)
    Cc = Cm.reshape(n, nc, q, SSD_GROUPS, SSD_STATE)
    idx = jnp.arange(q)
    lower = (idx[:, None] >= idx[None, :])[None, None, :, :, None, None]
    seg = a_cum[:, :, :, None] - a_cum[:, :, None, :]
    decay = jnp.exp(jnp.where(lower, seg, -jnp.inf))
    scores = jnp.einsum("bcqgn,bcsgn->bcqsg", Cc, Bc)
    y_diag = jnp.einsum("bcqsg,bcqsgj,bcsgjp->bcqgjp", scores, decay, x)
    decay_to_end = jnp.exp(a_cum[:, :, -1:] - a_cum)
    states = jnp.einsum("bcsgn,bcsgj,bcsgjp->bcgjpn", Bc, decay_to_end, x)
    chunk_decay = jnp.exp(a_cum[:, :, -1])

    def step(h, inp):
        st, dec = inp
        return h * dec[..., None, None] + st, h

    h_init = h0.reshape(n, SSD_GROUPS, SSD_HPG, SSD_HEADDIM, SSD_STATE)
    h_final, h_prev = lax.scan(step, h_init, (jnp.moveaxis(states, 1, 0), jnp.moveaxis(chunk_decay, 1, 0)))
    h_prev = jnp.moveaxis(h_prev, 0, 1)
    y_off = jnp.einsum("bcqgn,bcgjpn,bcqgj->bcqgjp", Cc, h_prev, jnp.exp(a_cum))
    y = (y_diag + y_off).reshape(n, l, SSD_HEADS, SSD_HEADDIM)
    return y, h_final.reshape(n, SSD_HEADS, SSD_HEADDIM, SSD_STATE)


def ssd_mixer(h, h0_f, h0_b, w_in, conv_w, conv_b, dt_bias_f, dt_bias_b, a_log_f, a_log_b,
              d_skip, norm_w, w_out):
    n, l, _ = h.shape
    f32 = jnp.float32
    zxbcdt = h @ w_in
    z = zxbcdt[..., :D_INNER].astype(f32)
    xbc = jax.nn.silu(dwconv1d(zxbcdt[..., D_INNER:D_INNER + SSD_CONV_DIM], conv_w, conv_b)).astype(f32)
    dt = zxbcdt[..., D_INNER + SSD_CONV_DIM:].astype(f32)
    xs = xbc[..., :D_INNER].reshape(n, l, SSD_HEADS, SSD_HEADDIM)
    Bm = xbc[..., D_INNER:D_INNER + SSD_GN].reshape(n, l, SSD_GROUPS, SSD_STATE)
    Cm = xbc[..., D_INNER + SSD_GN:].reshape(n, l, SSD_GROUPS, SSD_STATE)
    dt_f = jax.nn.softplus(dt[..., :SSD_HEADS] + dt_bias_f.astype(f32))
    dt_b = jax.nn.softplus(dt[..., SSD_HEADS:] + dt_bias_b.astype(f32))
    A_f = -jnp.exp(a_log_f.astype(f32))
    A_b = -jnp.exp(a_log_b.astype(f32))
    y_f, hf = ssd_chunked(xs, dt_f, A_f, Bm, Cm, h0_f)
    flip = lambda t: jnp.flip(t, axis=1)
    y_b, hb = ssd_chunked(flip(xs), flip(dt_b), A_b, flip(Bm), flip(Cm), h0_b)
    y = y_f + flip(y_b) + d_skip.astype(f32)[:, None] * xs
    y = rmsnorm(y.reshape(n, l, D_INNER) * jax.nn.silu(z), norm_w)
    return y.astype(h.dtype) @ w_out, hf, hb


def conformer_conv_module(h, rows, w_pw1, b_pw1, dw_w, dw_b, ln_g, ln_b, w_pw2, b_pw2):
    u = h @ w_pw1 + b_pw1
    u = u[..., :CONF_DIM] * jax.nn.sigmoid(u[..., CONF_DIM:])
    v = dwconv1d(u, dw_w, dw_b) if rows is None else axial_dwconv(u, rows, dw_w, dw_b)
    v = jax.nn.silu(layernorm(v, ln_g, ln_b))
    return v @ w_pw2 + b_pw2


def swiglu(h, w_in, w_out):
    u = h @ w_in
    return (jax.nn.silu(u[..., :FFN_HIDDEN]) * u[..., FFN_HIDDEN:]) @ w_out


def setup_inputs(seed: int = 0) -> dict:
    key = jax.random.key(seed)
    ks = jax.random.split(key, 40)
    f32 = jnp.float32
    nrm = lambda k, shape, s: jax.random.normal(k, shape, f32) * s
    NS, NC, D = N_SSD_LAYERS, N_CONF_LAYERS, D_MODEL
    dt0 = jnp.exp(jax.random.uniform(ks[0], (2, NS, SSD_HEADS), f32,
                                     float(np.log(1e-3)), float(np.log(1e-1))))
    dt_bias = dt0 + jnp.log(-jnp.expm1(-dt0))
    a_log = jnp.log(jax.random.uniform(ks[1], (2, NS, SSD_HEADS), f32, 1.0, 16.0))
    return {
        "x": nrm(ks[2], (BATCH, SEQ, D), 1.0),
        "c": nrm(ks[3], (BATCH, D), 1.0),
        "ctx": nrm(ks[4], (BATCH, CTX_LEN, D), 1.0),
        "c_ctx": nrm(ks[5], (D,), 1.0),
        "ada_w": nrm(ks[6], (DEPTH, D, 6 * D), 0.5 * D ** -0.5),
        "ada_b": nrm(ks[7], (DEPTH, 6 * D), 0.01),
        "norm_mix_g": 1.0 + nrm(ks[8], (DEPTH, D), 0.02),
        "norm_ffn_g": 1.0 + nrm(ks[9], (DEPTH, D), 0.02),
        "final_norm_g": 1.0 + nrm(ks[10], (D,), 0.02),
        "ssd_w_in": nrm(ks[11], (NS, D, SSD_PROJ_DIM), D ** -0.5),
        "ssd_conv_w": nrm(ks[12], (NS, SSD_CONV_K, SSD_CONV_DIM), SSD_CONV_K ** -0.5),
        "ssd_conv_b": nrm(ks[13], (NS, SSD_CONV_DIM), 0.01),
        "ssd_dt_bias_f": dt_bias[0],
        "ssd_dt_bias_b": dt_bias[1],
        "ssd_a_log_f": a_log[0],
        "ssd_a_log_b": a_log[1],
        "ssd_d_skip": 1.0 + nrm(ks[14], (NS, SSD_HEADS), 0.02),
        "ssd_norm_w": 1.0 + nrm(ks[15], (NS, D_INNER), 0.02),
        "ssd_w_out": nrm(ks[16], (NS, D_INNER, D), D_INNER ** -0.5),
        "conf_w_pw1": nrm(ks[17], (NC, D, 2 * CONF_DIM), D ** -0.5),
        "conf_b_pw1": nrm(ks[18], (NC, 2 * CONF_DIM), 0.01),
        "conf_dw_w": nrm(ks[19], (NC, CONF_K, CONF_DIM), CONF_K ** -0.5),
        "conf_dw_b": nrm(ks[20], (NC, CONF_DIM), 0.01),
        "conf_ln_g": 1.0 + nrm(ks[21], (NC, CONF_DIM), 0.02),
        "conf_ln_b": nrm(ks[22], (NC, CONF_DIM), 0.01),
        "conf_w_pw2": nrm(ks[23], (NC, CONF_DIM, D), CONF_DIM ** -0.5),
        "conf_b_pw2": nrm(ks[24], (NC, D), 0.01),
        "ffn_w_in": nrm(ks[25], (DEPTH, D, 2 * FFN_HIDDEN), D ** -0.5),
        "ffn_w_out": nrm(ks[26], (DEPTH, FFN_HIDDEN, D), FFN_HIDDEN ** -0.5),
    }


def reference(x, c, ctx, c_ctx, ada_w, ada_b, norm_mix_g, norm_ffn_g, final_norm_g,
              ssd_w_in, ssd_conv_w, ssd_conv_b, ssd_dt_bias_f, ssd_dt_bias_b, ssd_a_log_f,
              ssd_a_log_b, ssd_d_skip, ssd_norm_w, ssd_w_out,
              conf_w_pw1, conf_b_pw1, conf_dw_w, conf_dw_b, conf_ln_g, conf_ln_b,
              conf_w_pw2, conf_b_pw2, ffn_w_in, ffn_w_out):
    n, l, _ = x.shape
    rows = l // GRID_W
    h_lat, h_ctx = x, ctx
    sc_lat = jax.nn.silu(c)
    sc_ctx = jax.nn.silu(c_ctx)[None, :]
    for i in range(DEPTH):
        need_ctx_out = i < DEPTH - 1
        j = i // N_MIXERS
        sh1, s1, g1, sh2, s2, g2 = ada_params(sc_lat, ada_w[i], ada_b[i])
        xn_lat = modulate(rmsnorm(h_lat, norm_mix_g[i]), sh1, s1)
        if i % N_MIXERS == 0:
            p = (ssd_w_in[j], ssd_conv_w[j], ssd_conv_b[j], ssd_dt_bias_f[j], ssd_dt_bias_b[j],
                 ssd_a_log_f[j], ssd_a_log_b[j], ssd_d_skip[j], ssd_norm_w[j], ssd_w_out[j])
            csh1, cs1, cg1, csh2, cs2, cg2 = ada_params(sc_ctx, ada_w[i], ada_b[i])
            xn_ctx = modulate(rmsnorm(h_ctx, norm_mix_g[i]), csh1, cs1)
            h_zero = jnp.zeros((n, SSD_HEADS, SSD_HEADDIM, SSD_STATE), jnp.float32)
            y_ctx, hf_ctx, hb_ctx = ssd_mixer(xn_ctx, h_zero, h_zero, *p)
            y_lat, _, _ = ssd_mixer(xn_lat, hf_ctx, hb_ctx, *p)
            if need_ctx_out:
                h_ctx = h_ctx + cg1 * y_ctx
        else:
            p = (conf_w_pw1[j], conf_b_pw1[j], conf_dw_w[j], conf_dw_b[j], conf_ln_g[j],
                 conf_ln_b[j], conf_w_pw2[j], conf_b_pw2[j])
            y_lat = conformer_conv_module(xn_lat, rows, *p)
            if need_ctx_out:
                csh1, cs1, cg1, csh2, cs2, cg2 = ada_params(sc_ctx, ada_w[i], ada_b[i])
                xn_ctx = modulate(rmsnorm(h_ctx, norm_mix_g[i]), csh1, cs1)
                h_ctx = h_ctx + cg1 * conformer_conv_module(xn_ctx, None, *p)
        h_lat = h_lat + g1 * y_lat
        h_lat = h_lat + g2 * swiglu(modulate(rmsnorm(h_lat, norm_ffn_g[i]), sh2, s2), ffn_w_in[i], ffn_w_out[i])
        if need_ctx_out:
            h_ctx = h_ctx + cg2 * swiglu(modulate(rmsnorm(h_ctx, norm_ffn_g[i]), csh2, cs2),
                                         ffn_w_in[i], ffn_w_out[i])
    return rmsnorm(h_lat, final_norm_g)
```

```python
import contextlib
from contextlib import ExitStack
import numpy as np
import concourse.bass as bass
import concourse.mybir as mybir
from concourse.bass_utils import run_bass_kernel_spmd

F32 = mybir.dt.float32
BF16 = mybir.dt.bfloat16
AF = mybir.ActivationFunctionType
ALU = mybir.AluOpType
AX = mybir.AxisListType

D = 1024
L = 4096
CTX = 256
LT = L + CTX
NT = LT // 128
DI = 2048
NH = 32
HD = 64
NG = 8
NS = 128
CONVD = 4096
PROJ = 6208
FF = 2816
NFT = FF // 128
EPS = 1e-6
CK = 31

ENGS = ("pe", "act", "dve", "pool", "sp")
DBG = {}
EMIT_LOG = None
N_DMA_SEMS = 44
N_HW_SEMS = 28


class Buf:
    __slots__ = ("name", "w", "r", "rd", "excl")

    def __init__(self, name="", excl=False):
        self.name = name
        self.w = None
        self.r = {}
        self.rd = []
        self.excl = excl


def PBuf():
    return Buf("psum", True)


class Ins:
    __slots__ = ("eng", "fn", "deps", "is_dma", "need_inc", "val", "sem", "idx", "kind")

    def __init__(self, eng, fn, is_dma):
        self.eng = eng
        self.fn = fn
        self.deps = []
        self.is_dma = is_dma
        self.need_inc = False
        self.val = None
        self.sem = None
        self.idx = None
        self.kind = "load"


class Prog:
    def __init__(self, nc):
        self.nc = nc
        self.q = {e: [] for e in ENGS}
        self.dma_rr = 0
        self.dma_rr_sw = 0
        self.dma_last = [None] * N_DMA_SEMS
        self.dma_cnt = [0] * N_DMA_SEMS

    def _collect(self, ins, reads, writes):
        ex = [b for b in reads if b.excl]
        if ex:
            reads = [b for b in reads if not b.excl]
            writes = list(writes) + [b for b in ex if b not in writes]
        deps = {}

        def add(d):
            if d is None or d is ins:
                return
            deps[id(d)] = d
        for b in reads:
            add(b.w)
        for b in writes:
            add(b.w)
            for d in b.r.values():
                add(d)
            for d in b.rd:
                add(d)
        out = []
        for d in deps.values():
            if (not d.is_dma) and (not ins.is_dma) and d.eng == "pe" and ins.eng == "pe":
                continue
            out.append(d)
        ins.deps = out
        for d in out:
            d.need_inc = True
        for b in reads:
            if ins.is_dma:
                b.rd.append(ins)
            else:
                b.r[ins.eng] = ins
        for b in writes:
            b.w = ins
            b.r = {}
            b.rd = []

    budget = None

    def _spend(self):
        if self.budget is None:
            return True
        if self.budget <= 0:
            return False
        self.budget -= 1
        return True

    def op(self, eng, fn, reads=(), writes=()):
        if not self._spend():
            return None
        ins = Ins(eng, fn, False)
        self._collect(ins, reads, writes)
        ins.idx = len(self.q[eng])
        self.q[eng].append(ins)
        return ins

    def dma(self, eng, fn, reads=(), writes=(), kind="load"):
        if not self._spend():
            return None
        ins = Ins(eng, fn, True)
        ins.kind = kind
        self._collect(ins, reads, writes)
        if eng == "pool":
            slot = N_HW_SEMS + self.dma_rr_sw
            self.dma_rr_sw = (self.dma_rr_sw + 1) % (N_DMA_SEMS - N_HW_SEMS)
        else:
            slot = self.dma_rr
            self.dma_rr = (self.dma_rr + 1) % N_HW_SEMS
        prev = self.dma_last[slot]
        if prev is not None:
            ins.deps.append(prev)
        self.dma_cnt[slot] += 1
        ins.sem = slot
        ins.val = 16 * self.dma_cnt[slot]
        ins.need_inc = True
        self.dma_last[slot] = ins
        ins.idx = len(self.q[eng])
        self.q[eng].append(ins)
        return ins

    def barrier(self):
        b = Ins("sp", lambda e: e.nop(), False)
        deps = []
        for e in ENGS:
            for ins in reversed(self.q[e]):
                if not ins.is_dma:
                    deps.append(ins)
                    ins.need_inc = True
                    break
        for d in self.dma_last:
            if d is not None:
                deps.append(d)
        b.deps = deps
        b.need_inc = True
        b.idx = len(self.q["sp"])
        self.q["sp"].append(b)
        for e in ENGS:
            if e == "sp":
                continue
            w = Ins(e, lambda eng: eng.nop(), False)
            w.deps = [b]
            w.idx = len(self.q[e])
            self.q[e].append(w)

    def _hoist_loads(self):
        newq = []
        prev_load_pos = -1
        pos = {id(ins): k for k, ins in enumerate(self.q["sp"])}
        for k, ins in enumerate(self.q["sp"]):
            if ins.is_dma and ins.kind == "load":
                j = len(newq)
                while (j > 0 and newq[j - 1].is_dma and newq[j - 1].kind == "store"
                       and pos[id(newq[j - 1])] > prev_load_pos
                       and all(newq[j - 1] is not d for d in ins.deps) and len(newq) - j < 12):
                    j -= 1
                newq.insert(j, ins)
                prev_load_pos = k
            else:
                newq.append(ins)
        self.q["sp"] = newq

    def emit(self):
        nc = self.nc
        if DBG.get('hoist', True):
            self._hoist_loads()
        for e in ENGS:
            c = 0
            for ins in self.q[e]:
                if ins.is_dma:
                    continue
                if ins.need_inc:
                    c += 1
                    ins.val = c
        with ExitStack() as st:
            esem = {e: st.enter_context(nc.semaphore("s_" + e)) for e in ENGS}
            dsem = [st.enter_context(nc.semaphore("d_%d" % i)) for i in range(N_DMA_SEMS)]
            block = st.enter_context(nc.Block())

            def run(e, engobj):
                seen = {}
                for ins in self.q[e]:
                    for d in ins.deps:
                        if d.is_dma:
                            key = ("d", d.sem)
                            sem = dsem[d.sem]
                        else:
                            key = ("c", d.eng)
                            sem = esem[d.eng]
                        if seen.get(key, 0) >= d.val:
                            continue
                        seen[key] = d.val
                        engobj.wait_ge(sem, d.val)
                        if EMIT_LOG is not None:
                            EMIT_LOG.append((e, ins.idx, "wait", key, d.val))
                    if EMIT_LOG is not None:
                        EMIT_LOG.append((e, ins.idx, "ins", ins.is_dma, ins.val if (ins.need_inc or ins.is_dma) else None, ins.sem))
                    r = ins.fn(engobj)
                    if ins.is_dma:
                        r.then_inc(dsem[ins.sem], 16)
                    elif ins.need_inc:
                        r.then_inc(esem[e], 1)
                if e == "sp":
                    for slot in range(N_DMA_SEMS):
                        if self.dma_cnt[slot]:
                            v = 16 * self.dma_cnt[slot]
                            if seen.get(("d", slot), 0) < v:
                                engobj.wait_ge(dsem[slot], v)

            @block.tensor
            def _(pe):
                run("pe", pe)

            @block.scalar
            def _(act):
                run("act", act)

            @block.vector
            def _(dve):
                run("dve", dve)

            @block.gpsimd
            def _(pool):
                run("pool", pool)

            @block.sync
            def _(sp):
                run("sp", sp)


class Ring:
    def __init__(self, items):
        self.items = items
        self.i = 0

    def next(self):
        it = self.items[self.i % len(self.items)]
        self.i += 1
        return it


def _consts():
    i = np.arange(128)
    c = {}
    c["ident"] = np.eye(128, dtype=np.float32)
    c["Uf"] = (i[:, None] > i[None, :]).astype(np.float32)
    c["Ub"] = (i[:, None] < i[None, :]).astype(np.float32)
    c["Rf"] = (i[:, None] <= i[None, :]).astype(np.float32)
    c["Rb"] = (i[:, None] >= i[None, :]).astype(np.float32)
    c["ones"] = np.ones((128, 128), np.float32)
    return np.stack([c[k] for k in ("ident", "Uf", "Ub", "Rf", "Rb", "ones")], axis=1)


CI_ID, CI_UF, CI_UB, CI_RF, CI_RB, CI_ONE = range(6)


def build_nc(dbg=None, stop_after=None):
    nc = bass.Bass("TRN2", target_bir_lowering=False)
    dbg = dbg or {}

    def din(name, shape, dt=F32):
        return nc.dram_tensor(name, list(shape), dt, kind="ExternalInput").ap()

    def dscr(name, shape, dt):
        return nc.dram_tensor(name, list(shape), dt, kind="Internal").ap()

    xin = din("xin", [LT, D])
    ccT = din("ccT", [128, 2, 8])
    ada_w = din("ada_w", [2, D, 6 * D])
    ada_b = din("ada_b", [1, 2 * 6 * D])
    normg = din("normg", [128, 5, D])
    consts = din("consts", [128, 6, 128])
    w_in = din("ssd_w_in", [D, PROJ])
    convw = din("ssd_convw", [128, 32, 5])
    convb = din("ssd_convb", [128, 32])
    ssdv = din("ssdv", [128, 160])
    ssd_nw = din("ssd_nw", [128, DI])
    w_out = din("ssd_w_out", [DI, D])
    pw1 = din("conf_w_pw1", [D, 2 * D])
    cfv = din("cfv", [128, 8, 37])
    pw2 = din("conf_w_pw2", [D, D])
    bpw2 = din("conf_b_pw2", [1, D])
    ffn_wi = din("ffn_w_in", [2, D, 2 * FF])
    ffn_wo = din("ffn_w_out", [2, FF, D])
    out = nc.dram_tensor("out", [L, D], F32, kind="ExternalOutput").ap()
    dbg_ap = {k: nc.dram_tensor("dbg_" + k, list(s), F32, kind="ExternalOutput").ap() for k, s in dbg.items()}

    XBC = dscr("XBC", [CONVD, LT], BF16)
    ZS = dscr("ZS", [L, DI], BF16)
    YP = dscr("YP", [L, DI], F32)
    SBS = dscr("SBS", [NT, 128, DI], F32)
    HA = dscr("HA", [L, D], F32)
    HB = dscr("HB", [L, D], F32)
    VS = dscr("VS", [D, L], BF16)
    WSC = [dscr("WSC%d" % i, [NFT, 128, 8, 256], BF16) for i in range(2)]

    P = Prog(nc)
    B_xin = [Buf() for _ in range(NT)]
    B_XBC_c = [Buf() for _ in range(32)]
    B_ZS = [Buf() for _ in range(32)]
    B_YP = [Buf() for _ in range(32)]
    B_SBS = [Buf() for _ in range(NT)]
    B_HA = [Buf() for _ in range(32)]
    B_HB = [Buf() for _ in range(32)]
    B_VS = [Buf() for _ in range(8)]
    B_out = [Buf() for _ in range(32)]
    B_w = Buf("weights")
    B_wsc = [[Buf() for _ in range(NFT)] for _ in range(2)]
    B_dbg = Buf("dbg")

    def dump(name, src_ap, src_bufs, dst_slice=None):
        if name not in dbg_ap:
            return
        dst = dbg_ap[name] if dst_slice is None else dst_slice(dbg_ap[name])
        P.dma("pool", lambda e: e.dma_start(out=dst, in_=src_ap), reads=src_bufs, writes=[B_dbg])

    with ExitStack() as top:
        uid = [0]

        def sb(st, name, shape, dt):
            uid[0] += 1
            return st.enter_context(nc.sbuf_tensor("%s_%d" % (name, uid[0]), list(shape), dt))

        def ps(st, name, shape, dt):
            uid[0] += 1
            return st.enter_context(nc.psum_tensor("%s_%d" % (name, uid[0]), list(shape), dt))

        cst_f = sb(top, "cst_f", [128, 6, 128], F32)
        cst_b = sb(top, "cst_b", [128, 6, 128], BF16)
        mod = sb(top, "mod", [128, 6 * D], F32)
        B_cst = Buf("cst")
        B_mod = Buf("mod")
        P.dma("sp", lambda e: e.dma_start(out=cst_f[:], in_=consts), reads=[B_w], writes=[B_cst])
        P.dma("pool", lambda e: e.dma_start(out=cst_b[:], in_=consts), reads=[B_w], writes=[B_cst])
        ident = cst_b[:, CI_ID, :]

        def cast_ffn_weights(li):
            for f in range(NFT):
                for two in range(2):
                    P.dma("pool", lambda e, f=f, two=two: e.dma_start(
                        out=WSC[li][f, :, :, two * 128:(two + 1) * 128],
                        in_=ffn_wi[li][:, two * FF + f * 128:two * FF + (f + 1) * 128].rearrange("(k p) n -> p k n", p=128)),
                        reads=[B_w], writes=[B_wsc[li][f]])

        def mod_pass(li, modc=None, B_modc=None):
            with ExitStack() as st:
                cc = sb(st, "cc", [128, 2, 8], F32)
                scT = sb(st, "scT", [128, 2, 8], F32)
                screp = sb(st, "screp", [128, 2, 8, 128], F32)
                ones1 = sb(st, "ones1", [1, 128], F32)
                wr = Ring([(sb(st, "adw%d" % i, [128, 8, 512], F32), Buf()) for i in range(2)])
                br = Ring([(sb(st, "adb%d" % i, [1, 512], F32), Buf()) for i in range(2)])
                pr = Ring([(ps(st, "adp%d" % i, [128, 512], F32), PBuf()) for i in range(2)])
                gt = sb(st, "gtmp", [128, 2, D], F32)
                B_cc, B_sc, B_rep, B_o1, B_gt = Buf(), Buf(), Buf(), Buf(), Buf()
                P.dma("sp", lambda e: e.dma_start(out=cc[:], in_=ccT), reads=[B_w], writes=[B_cc])
                P.op("act", lambda e: e.activation(out=scT[:], in_=cc[:], func=AF.Silu), reads=[B_cc], writes=[B_sc])
                P.op("dve", lambda e: e.tensor_copy(screp[:], scT[:].unsqueeze(3).to_broadcast([128, 2, 8, 128])),
                     reads=[B_sc], writes=[B_rep])
                P.op("pool", lambda e: e.memset(ones1[:], 1.0), writes=[B_o1])
                P.dma("sp", lambda e: e.dma_start(out=gt[:, 0, :], in_=normg[:, li, :]), reads=[B_w], writes=[B_gt])
                P.dma("sp", lambda e: e.dma_start(out=gt[:, 1, :], in_=normg[:, 2 + li, :]), reads=[B_w], writes=[B_gt])
                for blk in range(12):
                    wt, B_wt = wr.next()
                    bt, B_bt = br.next()
                    P.dma("sp", lambda e, wt=wt, blk=blk: e.dma_start(
                        out=wt[:], in_=ada_w[li, :, blk * 512:(blk + 1) * 512].rearrange("(k p) n -> p k n", p=128)),
                        reads=[B_w], writes=[B_wt])
                    P.dma("sp", lambda e, bt=bt, blk=blk: e.dma_start(
                        out=bt[:], in_=ada_b[:, li * 6 * D + blk * 512: li * 6 * D + (blk + 1) * 512]),
                        reads=[B_w], writes=[B_bt])
                    for who in ((0, 1) if (modc is not None and blk < 4) else (0,)):
                        pt, B_pt = pr.next()
                        for k in range(8):
                            P.op("pe", lambda e, pt=pt, wt=wt, k=k, who=who: e.matmul(
                                pt[:], screp[:, who, k, :], wt[:, k, :], start=(k == 0), stop=False),
                                reads=[B_rep, B_wt], writes=[B_pt])
                        P.op("pe", lambda e, pt=pt, bt=bt: e.matmul(pt[:], ones1[:], bt[:], start=False, stop=True),
                             reads=[B_o1, B_bt], writes=[B_pt])
                        dst = mod if who == 0 else modc
                        Bd = B_mod if who == 0 else B_modc
                        P.op("act", lambda e, pt=pt, dst=dst, blk=blk: e.activation(
                            out=dst[:, blk * 512:(blk + 1) * 512], in_=pt[:], func=AF.Copy),
                            reads=[B_pt], writes=[Bd])
                for (slot, gi) in ((1, 0), (4, 1)):
                    P.op("dve", lambda e, slot=slot, gi=gi: e.scalar_tensor_tensor(
                        out=mod[:, slot * D:(slot + 1) * D], in0=mod[:, slot * D:(slot + 1) * D], scalar=1.0,
                        in1=gt[:, gi, :], op0=ALU.add, op1=ALU.mult), reads=[B_mod, B_gt], writes=[B_mod])
                if modc is not None:
                    P.op("dve", lambda e: e.scalar_tensor_tensor(
                        out=modc[:, D:2 * D], in0=modc[:, D:2 * D], scalar=1.0,
                        in1=gt[:, 0, :], op0=ALU.add, op1=ALU.mult), reads=[B_modc, B_gt], writes=[B_modc])
                P.barrier()

        def norm_mod_T(st_tiles, xt, B_xt, gs_ap, sh_ap, B_g, dstT_ap, B_dst, evac_eng="act", part="both"):
            junk, B_junk, ss, B_ss, xn, B_xn, pT, B_pT = st_tiles
            if part == "back":
                for k in range(8):
                    P.op("pe", lambda e, k=k: e.transpose(pT[:, k, :], xn[:, k * 128:(k + 1) * 128], ident),
                         reads=[B_xn, B_cst], writes=[B_pT])
                if evac_eng == "act":
                    P.op("act", lambda e: e.activation(out=dstT_ap, in_=pT[:], func=AF.Copy), reads=[B_pT], writes=[B_dst])
                else:
                    P.op("dve", lambda e: e.tensor_copy(dstT_ap, pT[:]), reads=[B_pT], writes=[B_dst])
                return
            P.op("act", lambda e: e.activation(out=junk[:], in_=xt, func=AF.Square, accum_out=ss[:, 0:1]),
                 reads=[B_xt], writes=[B_junk, B_ss])
            P.op("act", lambda e: e.activation(out=ss[:, 1:2], in_=ss[:, 0:1], func=AF.Ln, bias=eps_t[:, 0:1], scale=1.0 / D),
                 reads=[B_ss, B_eps], writes=[B_ss])
            P.op("act", lambda e: e.activation(out=ss[:, 2:3], in_=ss[:, 1:2], func=AF.Exp, scale=-0.5),
                 reads=[B_ss], writes=[B_ss])
            P.op("dve", lambda e: e.scalar_tensor_tensor(out=junk[:], in0=xt, scalar=ss[:, 2:3], in1=gs_ap,
                                                         op0=ALU.mult, op1=ALU.mult),
                 reads=[B_xt, B_ss, B_g], writes=[B_junk])
            P.op("dve", lambda e: e.tensor_tensor(out=xn[:], in0=junk[:], in1=sh_ap, op=ALU.add),
                 reads=[B_junk, B_g], writes=[B_xn])
            if part == "front":
                return
            for k in range(8):
                P.op("pe", lambda e, k=k: e.transpose(pT[:, k, :], xn[:, k * 128:(k + 1) * 128], ident),
                     reads=[B_xn, B_cst], writes=[B_pT])
            if evac_eng == "act":
                P.op("act", lambda e: e.activation(out=dstT_ap, in_=pT[:], func=AF.Copy), reads=[B_pT], writes=[B_dst])
            else:
                P.op("dve", lambda e: e.tensor_copy(dstT_ap, pT[:]), reads=[B_pT], writes=[B_dst])

        eps_t = sb(top, "eps_t", [128, 1], F32)
        B_eps = Buf()
        P.op("pool", lambda e: e.memset(eps_t[:], EPS), writes=[B_eps])

        def norm_tiles(st, pfx):
            return (sb(st, pfx + "junk", [128, D], F32), Buf(), sb(st, pfx + "ss", [128, 4], F32), Buf(),
                    sb(st, pfx + "xn", [128, D], BF16), Buf(), ps(st, pfx + "pT", [128, 8, 128], BF16), PBuf())

        def ffn_pass(li, hin, B_hin, hout, B_hout, final):
            TB = 512
            with ExitStack() as st:
                xnTr = [(sb(st, "f_xnT%d" % i, [128, 8, TB], BF16), [Buf() for _ in range(TB // 128)]) for i in range(2)]
                hidT = sb(st, "f_hidT", [128, NFT, TB], BF16)
                B_hid = [Buf() for _ in range(NFT)]
                wo = sb(st, "f_wo", [128, NFT, D], BF16)
                B_wo = Buf()
                hresr = Ring([(sb(st, "f_hres%d" % i, [128, TB // 128, D], F32), [Buf() for _ in range(TB // 128)]) for i in range(2)])
                nts = [norm_tiles(st, "f_a"), norm_tiles(st, "f_b")]
                wr = Ring([(sb(st, "f_wi%d" % i, [128, 8, 256], BF16), Buf()) for i in range(4)])
                pu = Ring([(ps(st, "f_pu%d" % i, [128, 512], F32), PBuf()) for i in range(4)])
                po = Ring([(ps(st, "f_po%d" % i, [128, 512], F32), PBuf()) for i in range(2)])
                sg = Ring([(sb(st, "f_sg%d" % i, [128, 512], BF16), Buf()) for i in range(2)])
                ot = Ring([(sb(st, "f_ot%d" % i, [128, D], F32), Buf()) for i in range(2)])
                fjunk = sb(st, "f_fjunk", [128, D], F32)
                fss = sb(st, "f_fss", [128, 4], F32)
                B_fj, B_fss = Buf(), Buf()
                fg = sb(st, "f_fg", [128, D], F32)
                B_fg = Buf()
                if final:
                    P.dma("sp", lambda e: e.dma_start(out=fg[:], in_=normg[:, 4, :]), reads=[B_w], writes=[B_fg])
                P.dma("pool", lambda e: e.dma_start(out=wo[:], in_=ffn_wo[li].rearrange("(k p) n -> p k n", p=128)),
                      reads=[B_w], writes=[B_wo])
                NB = L // TB
                hres_of = {}

                def norm_tile(tb, j, part="both"):
                    if j == 0 and part != "back":
                        hres_of[tb] = hresr.next()
                    hres, B_hres = hres_of[tb]
                    xnT, B_xnT = xnTr[tb % 2]
                    t = tb * (TB // 128) + j
                    if part != "back":
                        P.dma("sp", lambda e: e.dma_start(out=hres[:, j, :], in_=hin[t * 128:(t + 1) * 128, :]),
                              reads=[B_hin[t]], writes=[B_hres[j]])
                    norm_mod_T(nts[j % 2], hres[:, j, :], B_hres[j], mod[:, 4 * D:5 * D], mod[:, 3 * D:4 * D], B_mod,
                               xnT[:, :, j * 128:(j + 1) * 128], B_xnT[j], evac_eng=("act" if j % 2 else "dve"), part=part)

                def up(tb):
                    xnT, B_xnT = xnTr[tb % 2]
                    for f in range(NFT):
                        wt, B_wt = wr.next()
                        P.dma("sp", lambda e, wt=wt, f=f: e.dma_start(out=wt[:], in_=WSC[li][f]),
                              reads=[B_wsc[li][f]], writes=[B_wt])
                        p1, B_p1 = pu.next()
                        p2, B_p2 = pu.next()
                        for k in range(8):
                            P.op("pe", lambda e, p1=p1, wt=wt, k=k: e.matmul(
                                p1[:], wt[:, k, 0:128], xnT[:, k, :], start=(k == 0), stop=(k == 7)),
                                reads=[B_wt] + B_xnT, writes=[B_p1])
                        for k in range(8):
                            P.op("pe", lambda e, p2=p2, wt=wt, k=k: e.matmul(
                                p2[:], wt[:, k, 128:256], xnT[:, k, :], start=(k == 0), stop=(k == 7)),
                                reads=[B_wt] + B_xnT, writes=[B_p2])
                        s1, B_s1 = sg.next()
                        P.op("act", lambda e, s1=s1, p1=p1: e.activation(out=s1[:], in_=p1[:], func=AF.Silu),
                             reads=[B_p1], writes=[B_s1])
                        P.op("dve", lambda e, s1=s1, p2=p2, f=f: e.tensor_tensor(
                            out=hidT[:, f, :], in0=s1[:], in1=p2[:], op=ALU.mult),
                            reads=[B_s1, B_p2], writes=[B_hid[f]])
                        if f == NFT - 6 and tb + 1 < NB:
                            norm_tile(tb + 1, 0, "front")
                        if f == NFT - 1 and tb + 1 < NB:
                            norm_tile(tb + 1, 0, "back")

                def out_tile(tb, j):
                    hres, B_hres = hres_of[tb]
                    t = tb * (TB // 128) + j
                    o, B_o = ot.next()
                    for cb in range(2):
                        pq, B_pq = po.next()
                        for f in range(NFT):
                            P.op("pe", lambda e, pq=pq, f=f, cb=cb: e.matmul(
                                pq[:], hidT[:, f, j * 128:(j + 1) * 128], wo[:, f, cb * 512:(cb + 1) * 512],
                                start=(f == 0), stop=(f == NFT - 1)), reads=[B_hid[f], B_wo], writes=[B_pq])
                        P.op("dve", lambda e, pq=pq, cb=cb: e.tensor_tensor(
                            out=o[:, cb * 512:(cb + 1) * 512], in0=pq[:], in1=mod[:, 5 * D + cb * 512:5 * D + (cb + 1) * 512],
                            op=ALU.mult), reads=[B_pq, B_mod], writes=[B_o])
                    P.op("pool", lambda e: e.tensor_tensor(out=o[:], in0=o[:], in1=hres[:, j, :], op=ALU.add),
                         reads=[B_o, B_hres[j]], writes=[B_o])
                    if final:
                        P.op("act", lambda e: e.activation(out=fjunk[:], in_=o[:], func=AF.Square, accum_out=fss[:, 0:1]),
                             reads=[B_o], writes=[B_fj, B_fss])
                        P.op("act", lambda e: e.activation(out=fss[:, 1:2], in_=fss[:, 0:1], func=AF.Ln, bias=eps_t[:, 0:1], scale=1.0 / D),
                             reads=[B_fss, B_eps], writes=[B_fss])
                        P.op("act", lambda e: e.activation(out=fss[:, 2:3], in_=fss[:, 1:2], func=AF.Exp, scale=-0.5),
                             reads=[B_fss], writes=[B_fss])
                        P.op("dve", lambda e: e.scalar_tensor_tensor(out=o[:], in0=o[:], scalar=fss[:, 2:3], in1=fg[:],
                                                                     op0=ALU.mult, op1=ALU.mult),
                             reads=[B_o, B_fss, B_fg], writes=[B_o])
                    P.dma("sp", lambda e: e.dma_start(out=hout[t * 128:(t + 1) * 128, :], in_=o[:]),
                          reads=[B_o], writes=[B_hout[t]], kind="store")

                for j in range(TB // 128):
                    norm_tile(0, j)
                for tb in range(NB):
                    up(tb)
                    for j in range(TB // 128):
                        nxt = tb + 1 < NB and j + 1 < TB // 128
                        if nxt:
                            norm_tile(tb + 1, j + 1, "front")
                        out_tile(tb, j)
                        if nxt:
                            norm_tile(tb + 1, j + 1, "back")
                P.barrier()

        def ssd_layer():
            with ExitStack() as s0:
                dt_all = sb(s0, "dt_all", [128, NT, 64], F32)
                a_hl = sb(s0, "a_hl", [128, NT, 2, 64], BF16)
                B_dt = [Buf() for _ in range(NT)]
                sv = sb(s0, "sv", [128, 160], F32)
                B_sv = Buf()
                P.dma("sp", lambda e: e.dma_start(out=sv[:], in_=ssdv), reads=[B_w], writes=[B_sv])
                P.op("act", lambda e: e.activation(out=sv[:, 64:128], in_=sv[:, 64:128], func=AF.Exp), reads=[B_sv], writes=[B_sv])
                P.op("dve", lambda e: e.tensor_scalar(out=sv[:, 64:128], in0=sv[:, 64:128], scalar1=-1.0, scalar2=None, op0=ALU.mult),
                     reads=[B_sv], writes=[B_sv])
                with ExitStack() as st:
                    a_all = sb(st, "a_all", [128, NT, 64], F32)
                    modc = sb(st, "modc", [128, 2 * D], F32)
                    B_modc = Buf()
                    mod_pass(0, modc, B_modc)
                    xnT = sb(st, "xnT", [128, 8, LT], BF16)
                    B_xnT = [Buf() for _ in range(NT)]
                    nts = [norm_tiles(st, "a_a"), norm_tiles(st, "a_b")]
                    xr = Ring([(sb(st, "a_x%d" % i, [128, D], F32), Buf()) for i in range(3)])
                    wdt = sb(st, "a_wdt", [128, 8, 64], BF16)
                    B_wdt = Buf()
                    pdt = ps(st, "a_pdt", [128, 512], F32)
                    B_pdt = PBuf()
                    dttr = Ring([(sb(st, "a_dtt%d" % i, [128, 64], F32), Buf()) for i in range(2)])
                    P.dma("pool", lambda e: e.dma_start(out=wdt[:], in_=w_in[:, DI + CONVD:PROJ].rearrange("(k p) n -> p k n", p=128)),
                          reads=[B_w], writes=[B_wdt])
                    xts = {}

                    def a_norm(t, part):
                        if part != "back":
                            xts[t] = xr.next()
                            xt, B_xt = xts[t]
                            P.dma("sp", lambda e: e.dma_start(out=xt[:], in_=xin[t * 128:(t + 1) * 128, :]),
                                  reads=[B_xin[t]], writes=[B_xt])
                        xt, B_xt = xts[t]
                        if t < 2:
                            norm_mod_T(nts[t % 2], xt[:], B_xt, modc[:, D:2 * D], modc[:, 0:D], B_modc, xnT[:, :, t * 128:(t + 1) * 128], B_xnT[t],
                                       part=part)
                        else:
                            norm_mod_T(nts[t % 2], xt[:], B_xt, mod[:, D:2 * D], mod[:, 0:D], B_mod, xnT[:, :, t * 128:(t + 1) * 128], B_xnT[t],
                                       evac_eng=("act" if t % 2 else "dve"), part=part)

                    def a_dt(t):
                        dtt, B_dtt = dttr.next()
                        for k in range(8):
                            P.op("pe", lambda e, k=k: e.matmul(pdt[:, 0:64], xnT[:, k, t * 128:(t + 1) * 128], wdt[:, k, :],
                                                               start=(k == 0), stop=(k == 7)),
                                 reads=[B_xnT[t], B_wdt], writes=[B_pdt])
                        P.op("dve", lambda e: e.tensor_tensor(out=dtt[:], in0=pdt[:, 0:64], in1=sv[:, 0:64], op=ALU.add),
                             reads=[B_pdt, B_sv], writes=[B_dtt])
                        P.op("act", lambda e: e.activation(out=dtt[:], in_=dtt[:], func=AF.Exp), reads=[B_dtt], writes=[B_dtt])
                        P.op("act", lambda e: e.activation(out=dt_all[:, t, :], in_=dtt[:], func=AF.Ln, bias=1.0, scale=1.0),
                             reads=[B_dtt], writes=[B_dt[t]])
                        P.op("dve", lambda e: e.tensor_tensor(out=a_all[:, t, :], in0=dt_all[:, t, :], in1=sv[:, 64:128], op=ALU.mult),
                             reads=[B_dt[t], B_sv], writes=[B_dt[t]])
                        P.op("dve", lambda e: e.tensor_copy(a_hl[:, t, 0, :], a_all[:, t, :]), reads=[B_dt[t]], writes=[B_dt[t]])
                        P.op("dve", lambda e: e.tensor_tensor(out=a_hl[:, t, 1, :], in0=a_all[:, t, :], in1=a_hl[:, t, 0, :], op=ALU.subtract),
                             reads=[B_dt[t]], writes=[B_dt[t]])

                    a_norm(0, "front")
                    for t in range(NT):
                        if t + 1 < NT:
                            a_norm(t + 1, "front")
                        a_norm(t, "back")
                        if t >= 1:
                            a_dt(t - 1)
                    a_dt(NT - 1)
                    cast_ffn_weights(0)
                    dump("xnT", xnT[:, :, 0:LT], B_xnT)
                    dump("dt_all", dt_all[:], B_dt)
                    with ExitStack() as s2:
                        wz = Ring([(sb(s2, "a_wz%d" % i, [128, 8, 512], BF16), Buf()) for i in range(2)])
                        pz = Ring([(ps(s2, "a_pz%d" % i, [128, 512], F32), PBuf()) for i in range(2)])
                        zt = Ring([(sb(s2, "a_zt%d" % i, [128, 512], BF16), Buf()) for i in range(3)])
                        for cbk in range(4):
                            w, B_wz = wz.next()
                            P.dma("pool", lambda e, w=w, cbk=cbk: e.dma_start(
                                out=w[:], in_=w_in[:, cbk * 512:(cbk + 1) * 512].rearrange("(k p) n -> p k n", p=128)),
                                reads=[B_w], writes=[B_wz])
                            for t in range(32):
                                pzz, B_pz = pz.next()
                                for k in range(8):
                                    P.op("pe", lambda e, pzz=pzz, w=w, k=k, t=t: e.matmul(
                                        pzz[:], xnT[:, k, CTX + t * 128:CTX + (t + 1) * 128], w[:, k, :], start=(k == 0), stop=(k == 7)),
                                        reads=[B_xnT[t + 2], B_wz], writes=[B_pz])
                                z, B_z = zt.next()
                                P.op("act", lambda e, z=z, pzz=pzz: e.activation(out=z[:], in_=pzz[:], func=AF.Silu),
                                     reads=[B_pz], writes=[B_z])
                                P.dma("sp", lambda e, z=z, t=t, cbk=cbk: e.dma_start(
                                    out=ZS[t * 128:(t + 1) * 128, cbk * 512:(cbk + 1) * 512], in_=z[:]),
                                    reads=[B_z], writes=[B_ZS[t]], kind="store")
                        P.barrier()
                    with ExitStack() as s2:
                        cw = sb(s2, "b_cw", [128, 32, 5], F32)
                        cb_ = sb(s2, "b_cb", [128, 32], F32)
                        B_cw = Buf()
                        P.dma("sp", lambda e: e.dma_start(out=cw[:], in_=convw), reads=[B_w], writes=[B_cw])
                        P.dma("sp", lambda e: e.dma_start(out=cb_[:], in_=convb), reads=[B_w], writes=[B_cw])
                        PW = LT + 8
                        pre = Ring([(sb(s2, "b_pre%d" % i, [128, PW], BF16), Buf()) for i in range(2)])
                        post = Ring([(sb(s2, "b_post%d" % i, [128, LT], BF16), Buf()) for i in range(2)])
                        dg = Ring([(sb(s2, "b_dg%d" % i, [128, 5, 128], BF16), Buf()) for i in range(2)])
                        wb = Ring([(sb(s2, "b_w%d" % i, [128, 8, 128], BF16), Buf()) for i in range(3)])
                        pb = Ring([(ps(s2, "b_pb%d" % i, [128, 512], F32), PBuf()) for i in range(3)])
                        pc = Ring([(ps(s2, "b_pc%d" % i, [128, 512], F32), PBuf()) for i in range(2)])
                        for (pt_, B_p) in pre.items:
                            P.op("pool", lambda e, pt_=pt_: e.memset(pt_[:], 0.0), writes=[B_p])
                        segs = [(0, 256, 2)] + [(CTX + i * 512, 512, 262 + i * 512) for i in range(8)]
                        for c in range(32):
                            w, B_wb = wb.next()
                            P.dma("pool", lambda e, w=w, c=c: e.dma_start(
                                out=w[:], in_=w_in[:, DI + c * 128:DI + (c + 1) * 128].rearrange("(k p) n -> p k n", p=128)),
                                reads=[B_w], writes=[B_wb])
                            pr_, B_pre = pre.next()
                            po_, B_post = post.next()
                            d_, B_dg = dg.next()
                            for k in range(5):
                                P.op("pool", lambda e, d_=d_, k=k, c=c: e.tensor_scalar(
                                    out=d_[:, k, :], in0=ident, scalar1=cw[:, c, k:k + 1], scalar2=None, op0=ALU.mult),
                                    reads=[B_cst, B_cw], writes=[B_dg])
                            for si, (t0, n, off) in enumerate(segs):
                                p_, B_pb = pb.next()
                                rb = B_xnT[t0 // 128:(t0 + n) // 128]
                                for k in range(8):
                                    P.op("pe", lambda e, p_=p_, w=w, k=k, t0=t0, n=n: e.matmul(
                                        p_[:, 0:n], w[:, k, :], xnT[:, k, t0:t0 + n], start=(k == 0), stop=(k == 7)),
                                        reads=[B_wb] + rb, writes=[B_pb])
                                if si % 2 == 0:
                                    P.op("dve", lambda e, p_=p_, pr_=pr_, off=off, n=n: e.tensor_copy(pr_[:, off:off + n], p_[:, 0:n]),
                                         reads=[B_pb], writes=[B_pre])
                                else:
                                    P.op("act", lambda e, p_=p_, pr_=pr_, off=off, n=n: e.activation(
                                        out=pr_[:, off:off + n], in_=p_[:, 0:n], func=AF.Copy), reads=[B_pb], writes=[B_pre])
                            for si, (t0, n, off) in enumerate(segs):
                                q_, B_pc = pc.next()
                                for k in range(5):
                                    P.op("pe", lambda e, q_=q_, d_=d_, pr_=pr_, k=k, off=off, n=n: e.matmul(
                                        q_[:, 0:n], d_[:, k, :], pr_[:, off - 2 + k:off - 2 + k + n], start=(k == 0), stop=(k == 4)),
                                        reads=[B_dg, B_pre], writes=[B_pc])
                                P.op("act", lambda e, q_=q_, po_=po_, t0=t0, n=n, c=c: e.activation(
                                    out=po_[:, t0:t0 + n], in_=q_[:, 0:n], func=AF.Silu, bias=cb_[:, c:c + 1], scale=1.0),
                                    reads=[B_pc, B_cw], writes=[B_post])
                            P.dma("sp", lambda e, po_=po_, c=c: e.dma_start(out=XBC[c * 128:(c + 1) * 128, :], in_=po_[:]),
                                  reads=[B_post], writes=[B_XBC_c[c]], kind="store")
                            if c == 0:
                                dump("post0", po_[:], [B_post])
                        P.barrier()
                if stop_after == "B":
                    return
                ssd_scan(dt_all, a_hl, B_dt, sv, B_sv)

        def ssd_scan(dt_all, a_hl, B_dt, sv, B_sv):
            XBCv = XBC.rearrange("(c p) t -> p c t", p=128)
            cast_ffn_weights(1)
            with ExitStack() as st:
                hTb = sb(st, "c_hTb", [128, DI], F32)
                B_hTb = Buf()
                with ExitStack() as s2:
                    xbc = Ring([(sb(s2, "c_xbc%d" % i, [128, 32, 128], BF16), Buf()) for i in range(3)])
                    pTx = Ring([(ps(s2, "c_pTx%d" % i, [128, 1024], BF16), PBuf()) for i in range(2)])
                    pXr = Ring([(ps(s2, "c_pX%d" % i, [128, 512], F32), PBuf()) for i in range(2)])
                    pScr = Ring([(ps(s2, "c_pSc%d" % i, [128, 512], F32), PBuf()) for i in range(1)])
                    pYr = Ring([(ps(s2, "c_pY%d" % i, [128, 512], F32), PBuf()) for i in range(1)])
                    pMr = Ring([(ps(s2, "c_pM%d" % i, [128, 512], F32), PBuf()) for i in range(2)])
                    xdtr = Ring([(sb(s2, "c_xdt%d" % i, [128, 2, DI], BF16), Buf()) for i in range(2)])
                    xddr = Ring([(sb(s2, "c_xdd%d" % i, [128, 2, DI], BF16), [Buf() for _ in range(8)]) for i in range(2)])
                    btmr = Ring([(sb(s2, "c_btm%d" % i, [128, 8, 128], BF16), Buf()) for i in range(2)])
                    expor = Ring([(sb(s2, "c_expo%d" % i, [128, 192], F32), Buf()) for i in range(2)])
                    Lbuf = [[(sb(s2, "c_L%d_%d" % (par, i), [128, 4, 128], BF16), Buf()) for i in range(16)] for par in range(2)]

                    def buildL(tt, i):
                        g, dr = i // 2, i % 2
                        hb = dr * 32 + g * 4
                        L_, B_L = Lbuf[tt % 2][i]
                        P.op("pool", lambda e, L_=L_, dr=dr, hb=hb, tt=tt: e.tensor_tensor(
                            out=L_[:],
                            in0=cst_b[:, CI_UF + dr, :].unsqueeze(1).to_broadcast([128, 4, 128]),
                            in1=a_hl[:, tt, 0, hb:hb + 4].unsqueeze(2).to_broadcast([128, 4, 128]),
                            op=ALU.mult), reads=[B_cst, B_dt[tt]], writes=[B_L])
                    dec = Ring([(sb(s2, "c_dec%d" % i, [128, 4, 128], BF16), Buf()) for i in range(3)])
                    Mt = Ring([(sb(s2, "c_M%d" % i, [128, 4, 128], BF16), Buf()) for i in range(2)])
                    sm = Ring([(sb(s2, "c_sm%d" % i, [128, 2, 128], BF16), Buf()) for i in range(2)])
                    yp = Ring([(sb(s2, "c_yp%d" % i, [128, DI], F32), [Buf() for _ in range(8)]) for i in range(2)])
                    sbst = Ring([(sb(s2, "c_sbst%d" % i, [128, DI], F32), [Buf() for _ in range(8)]) for i in range(2)])
                    hTf = sb(s2, "c_hTf", [128, DI], F32)
                    hTf16 = sb(s2, "c_hTf16", [128, DI], BF16)
                    B_hTf = [Buf() for _ in range(8)]
                    B_hTf16 = [Buf() for _ in range(8)]
                    tmpg = Ring([(sb(s2, "c_tmpg%d" % i, [128, 256], F32), Buf()) for i in range(2)])
                    P.op("pool", lambda e: e.memset(hTf[:], 0.0), writes=B_hTf)
                    P.op("pool", lambda e: e.memset(hTf16[:], 0.0), writes=B_hTf16)
                    for i in range(16):
                        buildL(2, i)
                    def make_tile(t):
                        lat = t >= 2
                        xb_, B_xb = xbc.next()
                        xdt, B_xdt = xdtr.next()
                        xdd, B_xdd = xddr.next()
                        btm, B_btm = btmr.next()
                        expo, B_expo = expor.next()
                        y_, B_y = yp.next() if lat else (None, None)
                        st_dec, st_sm = {}, {}
                        box = {}

                        def loads():
                            for q4 in range(4):
                                P.dma("sp", lambda e, q4=q4: e.dma_start(
                                    out=xb_[:, q4 * 8:(q4 + 1) * 8, :], in_=XBCv[:, q4 * 8:(q4 + 1) * 8, t * 128:(t + 1) * 128]),
                                    reads=B_XBC_c, writes=[B_xb])

                        def stageABC(i):
                            g, dr = i // 2, i % 2
                            L_, B_L = Lbuf[t % 2][i]
                            pX, B_pX = pXr.next()
                            for h4 in range(4):
                                P.op("pe", lambda e, pX=pX, L_=L_, h4=h4, dr=dr: e.matmul(
                                    pX[:, h4 * 128:(h4 + 1) * 128], L_[:, h4, :], cst_b[:, CI_RF + dr, :], start=True, stop=True),
                                    reads=[B_L, B_cst], writes=[B_pX])
                            dc, B_dc = dec.next()
                            P.op("act", lambda e, dc=dc, pX=pX: e.activation(
                                out=dc[:].rearrange("p h q -> p (h q)"), in_=pX[:], func=AF.Exp),
                                reads=[B_pX], writes=[B_dc])
                            st_dec[i] = (dc, B_dc)

                        def stageD(g):
                            pSc, B_pSc = pScr.next()
                            P.op("pe", lambda e, pSc=pSc, g=g: e.matmul(
                                pSc[:, 0:128], xb_[:, 16 + g, :], xb_[:, 24 + g, :], start=True, stop=True),
                                reads=[B_xb], writes=[B_pSc])
                            sm_, B_sm = sm.next()
                            P.op("dve", lambda e, sm_=sm_, pSc=pSc: e.tensor_tensor(
                                out=sm_[:], in0=pSc[:, 0:128].unsqueeze(1).to_broadcast([128, 2, 128]),
                                in1=cst_b[:, CI_RF:CI_RB + 1, :], op=ALU.mult),
                                reads=[B_pSc, B_cst], writes=[B_sm])
                            st_sm[g] = (sm_, B_sm)

                        def pro():
                            pts = []
                            for hf in range(2):
                                pt_, B_pt = pTx.next()
                                for j in range(8):
                                    P.op("pe", lambda e, pt_=pt_, hf=hf, j=j: e.transpose(
                                        pt_[:, j * 128:(j + 1) * 128], xb_[:, hf * 8 + j, :], ident),
                                        reads=[B_xb, B_cst], writes=[B_pt])
                                pts.append((pt_, B_pt))
                            pS, B_pS = pMr.next()
                            sm_specs = [(CI_RF, 0, 0), (CI_RB, 32, 32), (CI_UF, 0, 64), (CI_UB, 32, 96)]
                            for (ci, ac, oc) in sm_specs:
                                for hl in range(2):
                                    P.op("pe", lambda e, ci=ci, ac=ac, oc=oc, hl=hl: e.matmul(
                                        pS[:, oc:oc + 32], cst_b[:, ci, :], a_hl[:, t, hl, ac:ac + 32], start=(hl == 0), stop=(hl == 1)),
                                        reads=[B_cst, B_dt[t]], writes=[B_pS])
                            for hl in range(2):
                                P.op("pe", lambda e, hl=hl: e.matmul(pS[:, 128:192], cst_b[:, CI_ONE, :], a_hl[:, t, hl, :],
                                                                     start=(hl == 0), stop=(hl == 1)),
                                     reads=[B_cst, B_dt[t]], writes=[B_pS])
                            P.op("act", lambda e: e.activation(out=expo[:], in_=pS[:, 0:192], func=AF.Exp),
                                 reads=[B_pS], writes=[B_expo])
                            for dr in range(2):
                                for hf in range(2):
                                    pt_, B_pt = pts[hf]
                                    P.op("dve", lambda e, pt_=pt_, dr=dr, hf=hf: e.tensor_tensor(
                                        out=xdt[:, dr, hf * 1024:(hf + 1) * 1024].rearrange("p (h j) -> p h j", j=64),
                                        in0=pt_[:].rearrange("p (h j) -> p h j", j=64),
                                        in1=dt_all[:, t, dr * 32 + hf * 16:dr * 32 + hf * 16 + 16].unsqueeze(2).to_broadcast([128, 16, 64]),
                                        op=ALU.mult), reads=[B_pt, B_dt[t]], writes=[B_xdt])
                            if lat:
                                for hf in range(2):
                                    pt_, B_pt = pts[hf]
                                    P.op("dve", lambda e, pt_=pt_, hf=hf: e.tensor_tensor(
                                        out=y_[:, hf * 1024:(hf + 1) * 1024].rearrange("p (h j) -> p h j", j=64),
                                        in0=pt_[:].rearrange("p (h j) -> p h j", j=64),
                                        in1=sv[:, 128 + hf * 16:128 + hf * 16 + 16].unsqueeze(2).to_broadcast([128, 16, 64]),
                                        op=ALU.mult), reads=[B_pt, B_sv], writes=B_y[hf * 4:(hf + 1) * 4])
                            ptb, B_ptb = pTx.next()
                            for g in range(8):
                                P.op("pe", lambda e, g=g: e.transpose(ptb[:, g * 128:(g + 1) * 128], xb_[:, 16 + g, :], ident),
                                     reads=[B_xb, B_cst], writes=[B_ptb])
                            P.op("act", lambda e: e.activation(out=btm[:].rearrange("p g n -> p (g n)"), in_=ptb[:], func=AF.Copy),
                                 reads=[B_ptb], writes=[B_btm])

                        def stageXdd(g):
                            gs = slice(g * 256, (g + 1) * 256)
                            for dr in range(2):
                                P.op("pool", lambda e, g=g, gs=gs, dr=dr: e.tensor_tensor(
                                    out=xdd[:, dr, gs].rearrange("p (h j) -> p h j", j=64),
                                    in0=xdt[:, dr, gs].rearrange("p (h j) -> p h j", j=64),
                                    in1=expo[:, 64 + dr * 32 + g * 4:64 + dr * 32 + g * 4 + 4].unsqueeze(2).to_broadcast([128, 4, 64]),
                                    op=ALU.mult), reads=[B_xdt, B_expo], writes=[B_xdd[g]])

                        def stageEF(i):
                            g, dr = i // 2, i % 2
                            if dr == 0:
                                box["pY"] = pYr.next()
                            pY, B_pY = box["pY"]
                            dc, B_dc = st_dec[i]
                            sm_, B_sm = st_sm[g]
                            M_, B_M = Mt.next()
                            P.op("dve", lambda e, M_=M_, dc=dc, sm_=sm_, dr=dr: e.tensor_tensor(
                                out=M_[:], in0=dc[:], in1=sm_[:, dr, :].unsqueeze(1).to_broadcast([128, 4, 128]), op=ALU.mult),
                                reads=[B_dc, B_sm], writes=[B_M])
                            for h4 in range(4):
                                h = g * 4 + h4
                                P.op("pe", lambda e, pY=pY, M_=M_, h4=h4, h=h, dr=dr: e.matmul(
                                    pY[:, h4 * 64:(h4 + 1) * 64], M_[:, h4, :], xdt[:, dr, h * 64:(h + 1) * 64],
                                    start=(dr == 0 and h4 == 0), stop=(dr == 1 and h4 == 3)), reads=[B_M, B_xdt], writes=[B_pY])
                            return pY, B_pY

                        def stageG1(g, pYt):
                            gs = slice(g * 256, (g + 1) * 256)
                            pY, B_pY = pYt
                            P.op("dve", lambda e, pY=pY, gs=gs: e.tensor_tensor(out=y_[:, gs], in0=y_[:, gs], in1=pY[:, 0:256], op=ALU.add),
                                 reads=[B_y[g], B_pY], writes=[B_y[g]])

                        def stageG2(g):
                            sbt, B_sbt = box["sbt"]
                            gs = slice(g * 256, (g + 1) * 256)
                            P.op("pool", lambda e, g=g, gs=gs: e.tensor_tensor(
                                out=hTf[:, gs].rearrange("p (h j) -> p h j", j=64),
                                in0=hTf[:, gs].rearrange("p (h j) -> p h j", j=64),
                                in1=expo[:, 128 + g * 4:128 + g * 4 + 4].unsqueeze(2).to_broadcast([128, 4, 64]), op=ALU.mult),
                                reads=[B_hTf[g], B_expo, B_hTf16[g]], writes=[B_hTf[g]])
                            if lat:
                                pO, B_pO = pMr.next()
                                P.op("pe", lambda e, pO=pO, g=g, gs=gs: e.matmul(
                                    pO[:, 0:256], xb_[:, 24 + g, :], hTf16[:, gs], start=True, stop=True),
                                    reads=[B_xb, B_hTf16[g]], writes=[B_pO])
                            pSt, B_pSt = pMr.next()
                            for dr in range(2):
                                P.op("pe", lambda e, pSt=pSt, g=g, dr=dr, gs=gs: e.matmul(
                                    pSt[:, dr * 256:(dr + 1) * 256], btm[:, g, :], xdd[:, dr, gs], start=True, stop=True),
                                    reads=[B_btm, B_xdd[g]], writes=[B_pSt])
                            if lat:
                                tg, B_tg = tmpg.next()
                                P.op("dve", lambda e, tg=tg, pO=pO, g=g: e.tensor_tensor(
                                    out=tg[:].rearrange("p (h j) -> p h j", j=64),
                                    in0=pO[:, 0:256].rearrange("p (h j) -> p h j", j=64),
                                    in1=expo[:, g * 4:g * 4 + 4].unsqueeze(2).to_broadcast([128, 4, 64]), op=ALU.mult),
                                    reads=[B_pO, B_expo], writes=[B_tg])
                            P.op("dve", lambda e, pSt=pSt, gs=gs: e.tensor_tensor(out=hTf[:, gs], in0=hTf[:, gs], in1=pSt[:, 0:256], op=ALU.add),
                                 reads=[B_hTf[g], B_pSt], writes=[B_hTf[g]])
                            if lat:
                                P.op("pool", lambda e, tg=tg, gs=gs: e.tensor_tensor(out=y_[:, gs], in0=y_[:, gs], in1=tg[:], op=ALU.add),
                                     reads=[B_y[g], B_tg], writes=[B_y[g]])
                            P.op("act", lambda e, pSt=pSt, gs=gs: e.activation(out=sbt[:, gs], in_=pSt[:, 256:512], func=AF.Copy),
                                 reads=[B_pSt], writes=[B_sbt[g]])
                            P.op("act", lambda e, gs=gs: e.activation(out=hTf16[:, gs], in_=hTf[:, gs], func=AF.Copy),
                                 reads=[B_hTf[g]], writes=[B_hTf16[g]])

                        def body(inject):
                            box["sbt"] = sbst.next()
                            sbt, B_sbt = box["sbt"]
                            if lat:
                                stageABC(0)
                                stageABC(1)
                                stageD(0)
                                pend = None
                                for i in range(16):
                                    g, dr = i // 2, i % 2
                                    if i + 2 < 16:
                                        stageABC(i + 2)
                                    if dr == 0 and g + 1 < 8:
                                        stageD(g + 1)
                                    if dr == 0:
                                        stageXdd(g)
                                    pYt = stageEF(i)
                                    if dr == 1:
                                        stageG1(g, pYt)
                                        if pend is not None:
                                            stageG2(pend)
                                            if t + 1 < NT:
                                                buildL(t + 1, 2 * pend)
                                                buildL(t + 1, 2 * pend + 1)
                                        pend = g
                                        if g == 3:
                                            inject()
                                stageG2(pend)
                                if t + 1 < NT:
                                    buildL(t + 1, 2 * pend)
                                    buildL(t + 1, 2 * pend + 1)
                            else:
                                for g in range(8):
                                    stageXdd(g)
                                    stageG2(g)
                                    if g == 3:
                                        inject()
                            P.dma("sp", lambda e: e.dma_start(out=SBS[t], in_=sbt[:]), reads=B_sbt, writes=[B_SBS[t]], kind="store")
                            if lat:
                                P.dma("sp", lambda e: e.dma_start(out=YP[(t - 2) * 128:(t - 1) * 128, :], in_=y_[:]),
                                      reads=B_y, writes=[B_YP[t - 2]], kind="store")
                            if t == 1:
                                dump("hf_ctx", hTf[:], B_hTf)

                        return loads, pro, body

                    tiles = {0: make_tile(0), 1: make_tile(1)}
                    tiles[0][0]()
                    tiles[1][0]()
                    tiles[0][1]()
                    for t in range(NT):
                        def inject(t=t):
                            if t + 1 < NT:
                                tiles[t + 1][1]()
                            if t + 2 < NT:
                                tiles[t + 2] = make_tile(t + 2)
                                tiles[t + 2][0]()
                        tiles[t][2](inject)
                        del tiles[t]
                    P.barrier()
                if stop_after == "C":
                    return
                with ExitStack() as s2:
                    wo = sb(s2, "d_wo", [128, 16, D], BF16)
                    B_wo = Buf()
                    P.dma("pool", lambda e: e.dma_start(out=wo[:], in_=w_out.rearrange("(k p) n -> p k n", p=128)), reads=[B_w], writes=[B_wo])
                    nw = sb(s2, "d_nw", [128, DI], F32)
                    B_nw = Buf()
                    P.dma("sp", lambda e: e.dma_start(out=nw[:], in_=ssd_nw), reads=[B_w], writes=[B_nw])
                    hb16 = sb(s2, "d_hb16", [128, DI], BF16)
                    B_hb16 = Buf()
                    P.op("pool", lambda e: e.memset(hTb[:], 0.0), writes=[B_hTb])
                    P.op("pool", lambda e: e.memset(hb16[:], 0.0), writes=[B_hb16])
                    ct = Ring([(sb(s2, "d_ct%d" % i, [128, 8, 128], BF16), Buf()) for i in range(2)])
                    sbl = Ring([(sb(s2, "d_sbl%d" % i, [128, DI], F32), Buf()) for i in range(2)])
                    ypl = Ring([(sb(s2, "d_ypl%d" % i, [128, DI], F32), [Buf() for _ in range(8)]) for i in range(3)])
                    zl = Ring([(sb(s2, "d_zl%d" % i, [128, DI], BF16), Buf()) for i in range(3)])
                    xl = Ring([(sb(s2, "d_xl%d" % i, [128, D], F32), Buf()) for i in range(3)])
                    pG = Ring([(ps(s2, "d_pG%d" % i, [128, 512], F32), PBuf()) for i in range(6)])
                    pTy = Ring([(ps(s2, "d_pTy%d" % i, [128, 1024], BF16), PBuf()) for i in range(2)])
                    expr = Ring([(sb(s2, "d_expd%d" % i, [128, 64], F32), Buf()) for i in range(2)])
                    tgr = Ring([(sb(s2, "d_tga%d" % i, [128, DI], F32), Buf()) for i in range(1)])
                    ynr = Ring([(sb(s2, "d_yn%d" % i, [128, DI], BF16), Buf()) for i in range(1)])
                    ynTr = Ring([(sb(s2, "d_ynT%d" % i, [128, 16, 128], BF16), Buf()) for i in range(2)])
                    dssr = Ring([(sb(s2, "d_ss%d" % i, [128, 4], F32), Buf()) for i in range(2)])
                    djr = Ring([(sb(s2, "d_junk%d" % i, [128, DI], BF16), Buf()) for i in range(1)])
                    ho = Ring([(sb(s2, "d_ho%d" % i, [128, D], F32), Buf()) for i in range(2)])
                    order = [1, 0] + list(range(NT - 1, 1, -1))

                    def front(t):
                        lat = t >= 2
                        s_, B_s = sbl.next()
                        P.dma("sp", lambda e: e.dma_start(out=s_[:], in_=SBS[t]), reads=[B_SBS[t]], writes=[B_s])
                        pS, B_pS = pG.next()
                        ex, B_ex = expr.next()
                        for ci, oc in ((CI_RB, 0), (CI_ONE, 32)):
                            for hl in range(2):
                                P.op("pe", lambda e, ci=ci, oc=oc, hl=hl: e.matmul(
                                    pS[:, oc:oc + 32], cst_b[:, ci, :], a_hl[:, t, hl, 32:64], start=(hl == 0), stop=(hl == 1)),
                                    reads=[B_cst, B_dt[t]], writes=[B_pS])
                        P.op("act", lambda e: e.activation(out=ex[:, 0:64], in_=pS[:, 0:64], func=AF.Exp), reads=[B_pS], writes=[B_ex])
                        C = None
                        if lat:
                            tl = t - 2
                            c_, B_c = ct.next()
                            P.dma("sp", lambda e: e.dma_start(out=c_[:], in_=XBCv[:, 24:32, t * 128:(t + 1) * 128]),
                                  reads=B_XBC_c, writes=[B_c])
                            y_, B_y = ypl.next()
                            P.dma("sp", lambda e: e.dma_start(out=y_[:], in_=YP[tl * 128:(tl + 1) * 128, :]),
                                  reads=[B_YP[tl]], writes=B_y)
                            z_, B_z = zl.next()
                            P.dma("sp", lambda e: e.dma_start(out=z_[:], in_=ZS[tl * 128:(tl + 1) * 128, :]),
                                  reads=[B_ZS[tl]], writes=[B_z])
                            x_, B_x = xl.next()
                            P.dma("sp", lambda e: e.dma_start(out=x_[:], in_=xin[t * 128:(t + 1) * 128, :]),
                                  reads=[B_xin[t]], writes=[B_x])
                            banks = []
                            for k in range(4):
                                pO, B_pO = pG.next()
                                banks.append((pO, B_pO))
                                for h2 in range(2):
                                    g = 2 * k + h2
                                    gs = slice(g * 256, (g + 1) * 256)
                                    P.op("pe", lambda e, pO=pO, g=g, gs=gs, h2=h2: e.matmul(
                                        pO[:, h2 * 256:(h2 + 1) * 256], c_[:, g, :], hb16[:, gs], start=True, stop=True),
                                        reads=[B_c, B_hb16], writes=[B_pO])
                            C = (tl, y_, B_y, z_, B_z, x_, B_x)
                        P.op("dve", lambda e: e.tensor_tensor(
                            out=hTb[:].rearrange("p (h j) -> p h j", j=64), in0=hTb[:].rearrange("p (h j) -> p h j", j=64),
                            in1=ex[:, 32:64].unsqueeze(2).to_broadcast([128, 32, 64]), op=ALU.mult),
                            reads=[B_hTb, B_ex, B_hb16], writes=[B_hTb])
                        P.op("dve", lambda e: e.tensor_tensor(out=hTb[:], in0=hTb[:], in1=s_[:], op=ALU.add),
                             reads=[B_hTb, B_s], writes=[B_hTb])
                        P.op("act", lambda e: e.activation(out=hb16[:], in_=hTb[:], func=AF.Copy), reads=[B_hTb], writes=[B_hb16])
                        if lat:
                            tga, B_tga = tgr.next()
                            for k in range(4):
                                pO, B_pO = banks[k]
                                P.op("dve", lambda e, pO=pO, k=k: e.tensor_tensor(
                                    out=tga[:, k * 512:(k + 1) * 512].rearrange("p (h j) -> p h j", j=64),
                                    in0=pO[:].rearrange("p (h j) -> p h j", j=64),
                                    in1=ex[:, k * 8:k * 8 + 8].unsqueeze(2).to_broadcast([128, 8, 64]), op=ALU.mult),
                                    reads=[B_pO, B_ex], writes=[B_tga])
                            P.op("dve", lambda e: e.tensor_tensor(out=y_[:], in0=y_[:], in1=tga[:], op=ALU.add),
                                 reads=B_y + [B_tga], writes=B_y)
                        if t == 0:
                            dump("hb_ctx", hTb[:], [B_hTb])
                        return C

                    def back(C):
                        tl, y_, B_y, z_, B_z, x_, B_x = C
                        yn, B_yn = ynr.next()
                        ynT, B_ynT = ynTr.next()
                        djunk, B_dj = djr.next()
                        dss, B_dss = dssr.next()
                        if tl == 5:
                            dump("y5", y_[:], B_y)
                        P.op("dve", lambda e: e.tensor_tensor(out=y_[:], in0=y_[:], in1=z_[:], op=ALU.mult),
                             reads=B_y + [B_z], writes=B_y)
                        P.op("act", lambda e: e.activation(out=djunk[:], in_=y_[:], func=AF.Square, accum_out=dss[:, 0:1]),
                             reads=B_y, writes=[B_dj, B_dss])
                        P.op("act", lambda e: e.activation(out=dss[:, 1:2], in_=dss[:, 0:1], func=AF.Ln, bias=eps_t[:, 0:1], scale=1.0 / DI),
                             reads=[B_dss, B_eps], writes=[B_dss])
                        P.op("act", lambda e: e.activation(out=dss[:, 2:3], in_=dss[:, 1:2], func=AF.Exp, scale=-0.5), reads=[B_dss], writes=[B_dss])
                        P.op("dve", lambda e: e.scalar_tensor_tensor(out=yn[:], in0=y_[:], scalar=dss[:, 2:3], in1=nw[:],
                                                                     op0=ALU.mult, op1=ALU.mult),
                             reads=B_y + [B_dss, B_nw], writes=[B_yn])
                        for hf in range(2):
                            pt_, B_pt = pTy.next()
                            for j in range(8):
                                P.op("pe", lambda e, pt_=pt_, hf=hf, j=j: e.transpose(
                                    pt_[:, j * 128:(j + 1) * 128], yn[:, (hf * 8 + j) * 128:(hf * 8 + j + 1) * 128], ident),
                                    reads=[B_yn, B_cst], writes=[B_pt])
                            if hf == 0:
                                P.op("act", lambda e, pt_=pt_, hf=hf: e.activation(
                                    out=ynT[:, hf * 8:(hf + 1) * 8, :].rearrange("p k t -> p (k t)"), in_=pt_[:], func=AF.Copy),
                                    reads=[B_pt], writes=[B_ynT])
                            else:
                                P.op("dve", lambda e, pt_=pt_, hf=hf: e.tensor_copy(
                                    ynT[:, hf * 8:(hf + 1) * 8, :].rearrange("p k t -> p (k t)"), pt_[:]),
                                    reads=[B_pt], writes=[B_ynT])
                        h_, B_h = ho.next()
                        for cb in range(2):
                            pq, B_pq = pG.next()
                            for k in range(16):
                                P.op("pe", lambda e, pq=pq, k=k, cb=cb: e.matmul(
                                    pq[:], ynT[:, k, :], wo[:, k, cb * 512:(cb + 1) * 512], start=(k == 0), stop=(k == 15)),
                                    reads=[B_ynT, B_wo], writes=[B_pq])
                            P.op("dve", lambda e, pq=pq, cb=cb: e.tensor_tensor(
                                out=h_[:, cb * 512:(cb + 1) * 512], in0=pq[:], in1=mod[:, 2 * D + cb * 512:2 * D + (cb + 1) * 512], op=ALU.mult),
                                reads=[B_pq, B_mod], writes=[B_h])
                        P.op("pool", lambda e: e.tensor_tensor(out=h_[:], in0=h_[:], in1=x_[:], op=ALU.add),
                             reads=[B_h, B_x], writes=[B_h])
                        P.dma("sp", lambda e: e.dma_start(out=HA[tl * 128:(tl + 1) * 128, :], in_=h_[:]),
                              reads=[B_h], writes=[B_HA[tl]], kind="store")

                    pend = None
                    for t in order:
                        C = front(t)
                        if pend is not None:
                            back(pend)
                        pend = C
                    back(pend)
                    P.barrier()

        def conf_layer():
            with ExitStack() as st:
                mod_pass(1)
                xnT = sb(st, "g_xnT", [128, 8, L], BF16)
                B_xnT = [Buf() for _ in range(32)]
                cf = sb(st, "g_cf", [128, 8, 37], F32)
                B_cf = Buf()
                P.dma("sp", lambda e: e.dma_start(out=cf[:], in_=cfv), reads=[B_w], writes=[B_cf])
                with ExitStack() as s2:
                    nts = [norm_tiles(s2, "g_a"), norm_tiles(s2, "g_b")]
                    xr = Ring([(sb(s2, "g_x%d" % i, [128, D], F32), Buf()) for i in range(3)])
                    xts = {}

                    def g_norm(t, part):
                        if part != "back":
                            xts[t] = xr.next()
                            xt, B_xt = xts[t]
                            P.dma("sp", lambda e: e.dma_start(out=xt[:], in_=HB[t * 128:(t + 1) * 128, :]),
                                  reads=[B_HB[t]], writes=[B_xt])
                        xt, B_xt = xts[t]
                        norm_mod_T(nts[t % 2], xt[:], B_xt, mod[:, D:2 * D], mod[:, 0:D], B_mod, xnT[:, :, t * 128:(t + 1) * 128], B_xnT[t],
                                   evac_eng=("act" if t % 2 else "dve"), part=part)

                    g_norm(0, "front")
                    for t in range(32):
                        if t + 1 < 32:
                            g_norm(t + 1, "front")
                        g_norm(t, "back")
                    P.barrier()
                with ExitStack() as s2:
                    HW_ = 94
                    ub = Ring([(sb(s2, "g_u%d" % i, [128, 64 * HW_], BF16), Buf()) for i in range(2)])
                    vb = Ring([(sb(s2, "g_v%d" % i, [128, L], BF16), Buf()) for i in range(2)])
                    dg = Ring([(sb(s2, "g_dg%d" % i, [128, CK, 128], BF16), Buf()) for i in range(2)])
                    wa = Ring([(sb(s2, "g_wa%d" % i, [128, 8, 256], BF16), Buf()) for i in range(2)])
                    pa = Ring([(ps(s2, "g_pa%d" % i, [128, 512], F32), PBuf()) for i in range(4)])
                    pc = Ring([(ps(s2, "g_pc%d" % i, [128, 512], F32), PBuf()) for i in range(3)])
                    sgm = Ring([(sb(s2, "g_sg%d" % i, [128, 512], F32), Buf()) for i in range(2)])
                    for (u_, B_u) in ub.items:
                        P.op("pool", lambda e, u_=u_: e.memset(u_[:], 0.0), writes=[B_u])
                    for c in range(8):
                        hor = c < 4
                        w, B_wa = wa.next()
                        for two in range(2):
                            P.dma("pool", lambda e, w=w, c=c, two=two: e.dma_start(
                                out=w[:, :, two * 128:(two + 1) * 128],
                                in_=pw1[:, two * D + c * 128:two * D + (c + 1) * 128].rearrange("(k p) n -> p k n", p=128)),
                                reads=[B_w], writes=[B_wa])
                        u_, B_u = ub.next()
                        v_, B_v = vb.next()
                        d_, B_dg = dg.next()
                        if c in (4, 5):
                            P.op("pool", lambda e, u_=u_: e.memset(u_[:], 0.0), writes=[B_u])
                        for k in range(CK):
                            P.op("pool" if k % 2 else "dve", lambda e, d_=d_, k=k, c=c: e.tensor_scalar(
                                out=d_[:, k, :], in0=ident, scalar1=cf[:, c, 5 + k:6 + k], scalar2=None, op0=ALU.mult),
                                reads=[B_cst, B_cf], writes=[B_dg])
                        if hor:
                            uv = u_[:].rearrange("p (r w) -> p r w", w=HW_)
                        else:
                            uv = u_[:].rearrange("p (r w) -> p r w", w=64)
                        for i in range(8):
                            p1, B_p1 = pa.next()
                            p2, B_p2 = pa.next()
                            rb = B_xnT[i * 4:(i + 1) * 4]
                            for k in range(8):
                                P.op("pe", lambda e, p1=p1, w=w, k=k, i=i: e.matmul(
                                    p1[:], w[:, k, 0:128], xnT[:, k, i * 512:(i + 1) * 512], start=(k == 0), stop=(k == 7)),
                                    reads=[B_wa] + rb, writes=[B_p1])
                            for k in range(8):
                                P.op("pe", lambda e, p2=p2, w=w, k=k, i=i: e.matmul(
                                    p2[:], w[:, k, 128:256], xnT[:, k, i * 512:(i + 1) * 512], start=(k == 0), stop=(k == 7)),
                                    reads=[B_wa] + rb, writes=[B_p2])
                            s_, B_s = sgm.next()
                            P.op("act", lambda e, s_=s_, p2=p2, c=c: e.activation(out=s_[:], in_=p2[:], func=AF.Sigmoid, bias=cf[:, c, 1:2], scale=1.0),
                                 reads=[B_p2, B_cf], writes=[B_s])
                            if hor:
                                dst = uv[:, i * 8:(i + 1) * 8, 15:79]
                            else:
                                dst = uv[:, 15 + i * 8:15 + (i + 1) * 8, :]
                            P.op("dve", lambda e, dst=dst, p1=p1, s_=s_, c=c: e.scalar_tensor_tensor(
                                out=dst, in0=p1[:].rearrange("p (r w) -> p r w", w=64), scalar=cf[:, c, 0:1],
                                in1=s_[:].rearrange("p (r w) -> p r w", w=64), op0=ALU.add, op1=ALU.mult),
                                reads=[B_p1, B_s, B_cf], writes=[B_u])
                        for i in range(8):
                            q_, B_q = pc.next()
                            for k in range(CK):
                                if hor:
                                    src = uv[:, i * 8:(i + 1) * 8, k:k + 64]
                                else:
                                    src = uv[:, i * 8 + k:i * 8 + k + 8, :]
                                P.op("pe", lambda e, q_=q_, d_=d_, k=k, src=src: e.matmul(
                                    q_[:].rearrange("p (r w) -> p r w", w=64), d_[:, k, :], src, start=(k == 0), stop=(k == CK - 1)),
                                    reads=[B_dg, B_u], writes=[B_q])
                            P.op("act", lambda e, q_=q_, v_=v_, i=i, c=c: e.activation(
                                out=v_[:, i * 512:(i + 1) * 512], in_=q_[:], func=AF.Identity, bias=cf[:, c, 2:3], scale=1.0),
                                reads=[B_q, B_cf], writes=[B_v])
                        P.dma("sp", lambda e, v_=v_, c=c: e.dma_start(out=VS[c * 128:(c + 1) * 128, :], in_=v_[:]),
                              reads=[B_v], writes=[B_VS[c]], kind="store")
                    P.barrier()
                if stop_after == "G":
                    return
                with ExitStack() as s2:
                    VSv = VS.rearrange("(c p) t -> p c t", p=128)
                    w2 = sb(s2, "h_w2", [128, 8, D], BF16)
                    B_w2 = Buf()
                    P.dma("pool", lambda e: e.dma_start(out=w2[:], in_=pw2.rearrange("(k p) n -> p k n", p=128)), reads=[B_w], writes=[B_w2])
                    b2 = sb(s2, "h_b2", [1, D], BF16)
                    ones1 = sb(s2, "h_ones1", [1, 128], BF16)
                    B_b2 = Buf()
                    P.dma("pool", lambda e: e.dma_start(out=b2[:], in_=bpw2), reads=[B_w], writes=[B_b2])
                    P.op("pool", lambda e: e.memset(ones1[:], 1.0), writes=[B_b2])
                    vt = Ring([(sb(s2, "h_vt%d" % i, [128, 8, 512], BF16), Buf()) for i in range(2)])
                    sq = Ring([(sb(s2, "h_sq%d" % i, [128, 512], BF16), Buf()) for i in range(2)])
                    pst = Ring([(ps(s2, "h_pst%d" % i, [128, 512], F32), PBuf()) for i in range(2)])
                    mur = Ring([(sb(s2, "h_mu%d" % i, [128, 512], F32), Buf()) for i in range(2)])
                    rsr = Ring([(sb(s2, "h_rs%d" % i, [128, 512], F32), Buf()) for i in range(2)])
                    tn = Ring([(sb(s2, "h_tn%d" % i, [128, 512], F32), Buf()) for i in range(2)])
                    sTr = Ring([(sb(s2, "h_sT%d" % i, [128, 8, 512], BF16), Buf()) for i in range(2)])
                    po = Ring([(ps(s2, "h_po%d" % i, [128, 512], F32), PBuf()) for i in range(3)])
                    xr = Ring([(sb(s2, "h_x%d" % i, [128, D], F32), Buf()) for i in range(2)])
                    ho = Ring([(sb(s2, "h_ho%d" % i, [128, D], F32), Buf()) for i in range(2)])
                    for i in range(8):
                        v_, B_v = vt.next()
                        mu, B_mu = mur.next()
                        rs, B_rs = rsr.next()
                        sT, B_sT = sTr.next()
                        P.dma("sp", lambda e, v_=v_, i=i, mu=mu, rs=rs, sT=sT: e.dma_start(out=v_[:], in_=VSv[:, :, i * 512:(i + 1) * 512]), reads=B_VS, writes=[B_v])
                        p_s, B_ps = pst.next()
                        p_q, B_pq2 = pst.next()
                        for c in range(8):
                            P.op("pe", lambda e, p_s=p_s, v_=v_, c=c, mu=mu, rs=rs, sT=sT: e.matmul(p_s[:], cst_b[:, CI_ONE, :], v_[:, c, :], start=(c == 0), stop=(c == 7)),
                                 reads=[B_cst, B_v], writes=[B_ps])
                        for c in range(8):
                            s_, B_s = sq.next()
                            P.op("pool" if c % 2 else "dve", lambda e, s_=s_, v_=v_, c=c, mu=mu, rs=rs, sT=sT: e.tensor_tensor(out=s_[:], in0=v_[:, c, :], in1=v_[:, c, :], op=ALU.mult),
                                 reads=[B_v], writes=[B_s])
                            P.op("pe", lambda e, p_q=p_q, s_=s_, c=c, mu=mu, rs=rs, sT=sT: e.matmul(p_q[:], cst_b[:, CI_ONE, :], s_[:], start=(c == 0), stop=(c == 7)),
                                 reads=[B_cst, B_s], writes=[B_pq2])
                        P.op("act", lambda e, p_s=p_s, mu=mu, rs=rs, sT=sT: e.activation(out=mu[:], in_=p_s[:], func=AF.Copy, scale=1.0 / D), reads=[B_ps], writes=[B_mu])
                        P.op("dve", lambda e, mu=mu, rs=rs, sT=sT: e.tensor_tensor(out=rs[:], in0=mu[:], in1=mu[:], op=ALU.mult), reads=[B_mu], writes=[B_rs])
                        P.op("dve", lambda e, p_q=p_q, mu=mu, rs=rs, sT=sT: e.scalar_tensor_tensor(out=rs[:], in0=p_q[:], scalar=1.0 / D, in1=rs[:], op0=ALU.mult, op1=ALU.subtract),
                             reads=[B_pq2, B_rs], writes=[B_rs])
                        P.op("act", lambda e, mu=mu, rs=rs, sT=sT: e.activation(out=rs[:], in_=rs[:], func=AF.Ln, bias=eps_t[:, 0:1], scale=1.0), reads=[B_rs, B_eps], writes=[B_rs])
                        P.op("act", lambda e, mu=mu, rs=rs, sT=sT: e.activation(out=rs[:], in_=rs[:], func=AF.Exp, scale=-0.5), reads=[B_rs], writes=[B_rs])
                        for c in range(8):
                            t_, B_t = tn.next()
                            P.op("dve", lambda e, t_=t_, v_=v_, c=c, mu=mu, rs=rs, sT=sT: e.tensor_tensor(out=t_[:], in0=v_[:, c, :], in1=mu[:], op=ALU.subtract),
                                 reads=[B_v, B_mu], writes=[B_t])
                            P.op("pool", lambda e, t_=t_, mu=mu, rs=rs, sT=sT: e.tensor_tensor(out=t_[:], in0=t_[:], in1=rs[:], op=ALU.mult), reads=[B_t, B_rs], writes=[B_t])
                            P.op("act", lambda e, t_=t_, c=c, mu=mu, rs=rs, sT=sT: e.activation(out=sT[:, c, :], in_=t_[:], func=AF.Silu, bias=cf[:, c, 4:5], scale=cf[:, c, 3:4]),
                                 reads=[B_t, B_cf], writes=[B_sT])
                        for j in range(4):
                            t = i * 4 + j
                            xt, B_xt = xr.next()
                            P.dma("sp", lambda e, xt=xt, t=t, mu=mu, rs=rs, sT=sT: e.dma_start(out=xt[:], in_=HB[t * 128:(t + 1) * 128, :]), reads=[B_HB[t]], writes=[B_xt])
                            h_, B_h = ho.next()
                            for cb in range(2):
                                pq, B_pq = po.next()
                                for c in range(8):
                                    P.op("pe", lambda e, pq=pq, c=c, j=j, cb=cb, mu=mu, rs=rs, sT=sT: e.matmul(
                                        pq[:], sT[:, c, j * 128:(j + 1) * 128], w2[:, c, cb * 512:(cb + 1) * 512], start=(c == 0), stop=False),
                                        reads=[B_sT, B_w2], writes=[B_pq])
                                P.op("pe", lambda e, pq=pq, cb=cb, mu=mu, rs=rs, sT=sT: e.matmul(pq[:], ones1[:], b2[:, cb * 512:(cb + 1) * 512], start=False, stop=True),
                                     reads=[B_b2], writes=[B_pq])
                                P.op("dve", lambda e, h_=h_, pq=pq, cb=cb, mu=mu, rs=rs, sT=sT: e.tensor_tensor(
                                    out=h_[:, cb * 512:(cb + 1) * 512], in0=pq[:], in1=mod[:, 2 * D + cb * 512:2 * D + (cb + 1) * 512], op=ALU.mult),
                                    reads=[B_pq, B_mod], writes=[B_h])
                            P.op("pool", lambda e, h_=h_, xt=xt, mu=mu, rs=rs, sT=sT: e.tensor_tensor(out=h_[:], in0=h_[:], in1=xt[:], op=ALU.add),
                                 reads=[B_h, B_xt], writes=[B_h])
                            P.dma("sp", lambda e, h_=h_, t=t, mu=mu, rs=rs, sT=sT: e.dma_start(out=HA[t * 128:(t + 1) * 128, :], in_=h_[:]), reads=[B_h], writes=[B_HA[t]], kind="store")
                    P.barrier()

        ssd_layer()
        if stop_after in (None, "D", "E", "G", "H"):
            if stop_after != "D":
                ffn_pass(0, HA, B_HA, HB, B_HB, final=False)
            if stop_after in (None, "G", "H"):
                conf_layer()
            if stop_after is None:
                ffn_pass(1, HA, B_HA, out, B_out, final=True)
        for name, (src, bufs) in {"HA": (HA, B_HA), "HB": (HB, B_HB)}.items():
            if name in dbg_ap:
                P.dma("sp", lambda e, name=name, src=src: e.dma_start(out=dbg_ap[name], in_=src), reads=bufs, writes=[B_dbg])
        P.emit()
    build_nc.last_prog = P
    return nc


def _prep_inputs(inp, b):
    f = np.float32
    rep = lambda v: np.ascontiguousarray(np.broadcast_to(np.asarray(v, f).reshape(1, -1), (128, np.asarray(v).size)))
    m = {}
    m["xin"] = np.ascontiguousarray(np.concatenate([inp["ctx"][b], inp["x"][b]], axis=0), dtype=f)
    cc = np.stack([inp["c"][b].reshape(8, 128).T, inp["c_ctx"].reshape(8, 128).T], axis=1)
    m["ccT"] = np.ascontiguousarray(cc, dtype=f)
    m["ada_w"] = inp["ada_w"]
    m["ada_b"] = np.ascontiguousarray(inp["ada_b"].reshape(1, -1), dtype=f)
    ng = np.stack([inp["norm_mix_g"][0], inp["norm_mix_g"][1], inp["norm_ffn_g"][0], inp["norm_ffn_g"][1], inp["final_norm_g"]], axis=0)
    m["normg"] = np.ascontiguousarray(np.broadcast_to(ng[None], (128, 5, D)), dtype=f)
    m["consts"] = _consts()
    m["ssd_w_in"] = inp["ssd_w_in"][0]
    m["ssd_convw"] = np.ascontiguousarray(inp["ssd_conv_w"][0].reshape(5, 32, 128).transpose(2, 1, 0), dtype=f)
    m["ssd_convb"] = np.ascontiguousarray(inp["ssd_conv_b"][0].reshape(32, 128).T, dtype=f)
    sv = np.concatenate([inp["ssd_dt_bias_f"][0], inp["ssd_dt_bias_b"][0], inp["ssd_a_log_f"][0], inp["ssd_a_log_b"][0], inp["ssd_d_skip"][0]])
    m["ssdv"] = rep(sv)
    m["ssd_nw"] = rep(inp["ssd_norm_w"][0])
    m["ssd_w_out"] = inp["ssd_w_out"][0]
    m["conf_w_pw1"] = inp["conf_w_pw1"][0]
    cfv = np.zeros((128, 8, 37), f)
    col = lambda v: np.asarray(v, f).reshape(8, 128).T
    cfv[:, :, 0] = col(inp["conf_b_pw1"][0][:D])
    cfv[:, :, 1] = col(inp["conf_b_pw1"][0][D:])
    cfv[:, :, 2] = col(inp["conf_dw_b"][0])
    cfv[:, :, 3] = col(inp["conf_ln_g"][0])
    cfv[:, :, 4] = col(inp["conf_ln_b"][0])
    cfv[:, :, 5:36] = inp["conf_dw_w"][0].reshape(CK, 8, 128).transpose(2, 1, 0)
    m["cfv"] = cfv
    m["conf_w_pw2"] = inp["conf_w_pw2"][0]
    m["conf_b_pw2"] = np.ascontiguousarray(inp["conf_b_pw2"][0].reshape(1, -1), dtype=f)
    m["ffn_w_in"] = inp["ffn_w_in"]
    m["ffn_w_out"] = inp["ffn_w_out"]
    return m


def kernel(**inputs):
    inp = {k: np.asarray(v) for k, v in inputs.items()}
    nc = build_nc()
    in_maps = [_prep_inputs(inp, b) for b in range(8)]
    res = run_bass_kernel_spmd(nc, in_maps, core_ids=list(range(8)))
    return np.stack([r["out"] for r in res.results], axis=0).astype(np.float32)
```

```python
import contextlib
from contextlib import ExitStack
import numpy as np
import concourse.bass as bass
import concourse.mybir as mybir
from concourse.bass_utils import run_bass_kernel_spmd

F32 = mybir.dt.float32
BF16 = mybir.dt.bfloat16
AF = mybir.ActivationFunctionType
ALU = mybir.AluOpType
AX = mybir.AxisListType

D = 1024
L = 4096
CTX = 256
LT = L + CTX
NT = LT // 128
DI = 2048
NH = 32
HD = 64
NG = 8
NS = 128
CONVD = 4096
PROJ = 6208
FF = 2816
NFT = FF // 128
EPS = 1e-6
CK = 31

ENGS = ("pe", "act", "dve", "pool", "sp")
DBG = {}
EMIT_LOG = None
N_DMA_SEMS = 44
N_HW_SEMS = 28


class Buf:
    __slots__ = ("name", "w", "r", "rd", "excl")

    def __init__(self, name="", excl=False):
        self.name = name
        self.w = None
        self.r = {}
        self.rd = []
        self.excl = excl


def PBuf():
    return Buf("psum", True)


class Ins:
    __slots__ = ("eng", "fn", "deps", "is_dma", "need_inc", "val", "sem", "idx", "kind")

    def __init__(self, eng, fn, is_dma):
        self.eng = eng
        self.fn = fn
        self.deps = []
        self.is_dma = is_dma
        self.need_inc = False
        self.val = None
        self.sem = None
        self.idx = None
        self.kind = "load"


class Prog:
    def __init__(self, nc):
        self.nc = nc
        self.q = {e: [] for e in ENGS}
        self.dma_rr = 0
        self.dma_rr_sw = 0
        self.dma_last = [None] * N_DMA_SEMS
        self.dma_cnt = [0] * N_DMA_SEMS

    def _collect(self, ins, reads, writes):
        ex = [b for b in reads if b.excl]
        if ex:
            reads = [b for b in reads if not b.excl]
            writes = list(writes) + [b for b in ex if b not in writes]
        deps = {}

        def add(d):
            if d is None or d is ins:
                return
            deps[id(d)] = d
        for b in reads:
            add(b.w)
        for b in writes:
            add(b.w)
            for d in b.r.values():
                add(d)
            for d in b.rd:
                add(d)
        out = []
        for d in deps.values():
            if (not d.is_dma) and (not ins.is_dma) and d.eng == "pe" and ins.eng == "pe":
                continue
            out.append(d)
        ins.deps = out
        for d in out:
            d.need_inc = True
        for b in reads:
            if ins.is_dma:
                b.rd.append(ins)
            else:
                b.r[ins.eng] = ins
        for b in writes:
            b.w = ins
            b.r = {}
            b.rd = []

    budget = None

    def _spend(self):
        if self.budget is None:
            return True
        if self.budget <= 0:
            return False
        self.budget -= 1
        return True

    def op(self, eng, fn, reads=(), writes=()):
        if not self._spend():
            return None
        ins = Ins(eng, fn, False)
        self._collect(ins, reads, writes)
        ins.idx = len(self.q[eng])
        self.q[eng].append(ins)
        return ins

    def dma(self, eng, fn, reads=(), writes=(), kind="load"):
        if not self._spend():
            return None
        ins = Ins(eng, fn, True)
        ins.kind = kind
        self._collect(ins, reads, writes)
        if eng == "pool":
            slot = N_HW_SEMS + self.dma_rr_sw
            self.dma_rr_sw = (self.dma_rr_sw + 1) % (N_DMA_SEMS - N_HW_SEMS)
        else:
            slot = self.dma_rr
            self.dma_rr = (self.dma_rr + 1) % N_HW_SEMS
        prev = self.dma_last[slot]
        if prev is not None:
            ins.deps.append(prev)
        self.dma_cnt[slot] += 1
        ins.sem = slot
        ins.val = 16 * self.dma_cnt[slot]
        ins.need_inc = True
        self.dma_last[slot] = ins
        ins.idx = len(self.q[eng])
        self.q[eng].append(ins)
        return ins

    def barrier(self):
        b = Ins("sp", lambda e: e.nop(), False)
        deps = []
        for e in ENGS:
            for ins in reversed(self.q[e]):
                if not ins.is_dma:
                    deps.append(ins)
                    ins.need_inc = True
                    break
        for d in self.dma_last:
            if d is not None:
                deps.append(d)
        b.deps = deps
        b.need_inc = True
        b.idx = len(self.q["sp"])
        self.q["sp"].append(b)
        for e in ENGS:
            if e == "sp":
                continue
            w = Ins(e, lambda eng: eng.nop(), False)
            w.deps = [b]
            w.idx = len(self.q[e])
            self.q[e].append(w)

    def _hoist_loads(self):
        newq = []
        prev_load_pos = -1
        pos = {id(ins): k for k, ins in enumerate(self.q["sp"])}
        for k, ins in enumerate(self.q["sp"]):
            if ins.is_dma and ins.kind == "load":
                j = len(newq)
                while (j > 0 and newq[j - 1].is_dma and newq[j - 1].kind == "store"
                       and pos[id(newq[j - 1])] > prev_load_pos
                       and all(newq[j - 1] is not d for d in ins.deps) and len(newq) - j < 12):
                    j -= 1
                newq.insert(j, ins)
                prev_load_pos = k
            else:
                newq.append(ins)
        self.q["sp"] = newq

    def emit(self):
        nc = self.nc
        if DBG.get('hoist', True):
            self._hoist_loads()
        for e in ENGS:
            c = 0
            for ins in self.q[e]:
                if ins.is_dma:
                    continue
                if ins.need_inc:
                    c += 1
                    ins.val = c
        with ExitStack() as st:
            esem = {e: st.enter_context(nc.semaphore("s_" + e)) for e in ENGS}
            dsem = [st.enter_context(nc.semaphore("d_%d" % i)) for i in range(N_DMA_SEMS)]
            block = st.enter_context(nc.Block())

            def run(e, engobj):
                seen = {}
                for ins in self.q[e]:
                    for d in ins.deps:
                        if d.is_dma:
                            key = ("d", d.sem)
                            sem = dsem[d.sem]
                        else:
                            key = ("c", d.eng)
                            sem = esem[d.eng]
                        if seen.get(key, 0) >= d.val:
                            continue
                        seen[key] = d.val
                        engobj.wait_ge(sem, d.val)
                        if EMIT_LOG is not None:
                            EMIT_LOG.append((e, ins.idx, "wait", key, d.val))
                    if EMIT_LOG is not None:
                        EMIT_LOG.append((e, ins.idx, "ins", ins.is_dma, ins.val if (ins.need_inc or ins.is_dma) else None, ins.sem))
                    r = ins.fn(engobj)
                    if ins.is_dma:
                        r.then_inc(dsem[ins.sem], 16)
                    elif ins.need_inc:
                        r.then_inc(esem[e], 1)
                if e == "sp":
                    for slot in range(N_DMA_SEMS):
                        if self.dma_cnt[slot]:
                            v = 16 * self.dma_cnt[slot]
                            if seen.get(("d", slot), 0) < v:
                                engobj.wait_ge(dsem[slot], v)

            @block.tensor
            def _(pe):
                run("pe", pe)

            @block.scalar
            def _(act):
                run("act", act)

            @block.vector
            def _(dve):
                run("dve", dve)

            @block.gpsimd
            def _(pool):
                run("pool", pool)

            @block.sync
            def _(sp):
                run("sp", sp)


class Ring:
    def __init__(self, items):
        self.items = items
        self.i = 0

    def next(self):
        it = self.items[self.i % len(self.items)]
        self.i += 1
        return it


def _consts():
    i = np.arange(128)
    c = {}
    c["ident"] = np.eye(128, dtype=np.float32)
    c["Uf"] = (i[:, None] > i[None, :]).astype(np.float32)
    c["Ub"] = (i[:, None] < i[None, :]).astype(np.float32)
    c["Rf"] = (i[:, None] <= i[None, :]).astype(np.float32)
    c["Rb"] = (i[:, None] >= i[None, :]).astype(np.float32)
    c["ones"] = np.ones((128, 128), np.float32)
    return np.stack([c[k] for k in ("ident", "Uf", "Ub", "Rf", "Rb", "ones")], axis=1)


CI_ID, CI_UF, CI_UB, CI_RF, CI_RB, CI_ONE = range(6)


def build_nc(dbg=None, stop_after=None):
    nc = bass.Bass("TRN2", target_bir_lowering=False)
    dbg = dbg or {}

    def din(name, shape, dt=F32):
        return nc.dram_tensor(name, list(shape), dt, kind="ExternalInput").ap()

    def dscr(name, shape, dt):
        return nc.dram_tensor(name, list(shape), dt, kind="Internal").ap()

    xin = din("xin", [LT, D])
    ccT = din("ccT", [128, 2, 8])
    ada_w = din("ada_w", [2, D, 6 * D])
    ada_b = din("ada_b", [1, 2 * 6 * D])
    normg = din("normg", [128, 5, D])
    consts = din("consts", [128, 6, 128])
    w_in = din("ssd_w_in", [D, PROJ])
    convw = din("ssd_convw", [128, 32, 5])
    convb = din("ssd_convb", [128, 32])
    ssdv = din("ssdv", [128, 160])
    ssd_nw = din("ssd_nw", [128, DI])
    w_out = din("ssd_w_out", [DI, D])
    pw1 = din("conf_w_pw1", [D, 2 * D])
    cfv = din("cfv", [128, 8, 37])
    pw2 = din("conf_w_pw2", [D, D])
    bpw2 = din("conf_b_pw2", [1, D])
    ffn_wi = din("ffn_w_in", [2, D, 2 * FF])
    ffn_wo = din("ffn_w_out", [2, FF, D])
    out = nc.dram_tensor("out", [L, D], F32, kind="ExternalOutput").ap()
    dbg_ap = {k: nc.dram_tensor("dbg_" + k, list(s), F32, kind="ExternalOutput").ap() for k, s in dbg.items()}

    XBC = dscr("XBC", [CONVD, LT], BF16)
    ZS = dscr("ZS", [L, DI], BF16)
    YP = dscr("YP", [L, DI], F32)
    SBS = dscr("SBS", [NT, 128, DI], F32)
    HA = dscr("HA", [L, D], F32)
    HB = dscr("HB", [L, D], F32)
    VS = dscr("VS", [D, L], BF16)
    WSC = [dscr("WSC%d" % i, [NFT, 128, 8, 256], BF16) for i in range(2)]

    P = Prog(nc)
    B_xin = [Buf() for _ in range(NT)]
    B_XBC_c = [Buf() for _ in range(32)]
    B_ZS = [Buf() for _ in range(32)]
    B_YP = [Buf() for _ in range(32)]
    B_SBS = [Buf() for _ in range(NT)]
    B_HA = [Buf() for _ in range(32)]
    B_HB = [Buf() for _ in range(32)]
    B_VS = [Buf() for _ in range(8)]
    B_out = [Buf() for _ in range(32)]
    B_w = Buf("weights")
    B_wsc = [[Buf() for _ in range(NFT)] for _ in range(2)]
    B_dbg = Buf("dbg")

    def dump(name, src_ap, src_bufs, dst_slice=None):
        if name not in dbg_ap:
            return
        dst = dbg_ap[name] if dst_slice is None else dst_slice(dbg_ap[name])
        P.dma("pool", lambda e: e.dma_start(out=dst, in_=src_ap), reads=src_bufs, writes=[B_dbg])

    with ExitStack() as top:
        uid = [0]

        def sb(st, name, shape, dt):
            uid[0] += 1
            return st.enter_context(nc.sbuf_tensor("%s_%d" % (name, uid[0]), list(shape), dt))

        def ps(st, name, shape, dt):
            uid[0] += 1
            return st.enter_context(nc.psum_tensor("%s_%d" % (name, uid[0]), list(shape), dt))

        cst_f = sb(top, "cst_f", [128, 6, 128], F32)
        cst_b = sb(top, "cst_b", [128, 6, 128], BF16)
        mod = sb(top, "mod", [128, 6 * D], F32)
        B_cst = Buf("cst")
        B_mod = Buf("mod")
        P.dma("sp", lambda e: e.dma_start(out=cst_f[:], in_=consts), reads=[B_w], writes=[B_cst])
        P.dma("pool", lambda e: e.dma_start(out=cst_b[:], in_=consts), reads=[B_w], writes=[B_cst])
        ident = cst_b[:, CI_ID, :]

        def cast_ffn_weights(li):
            for f in range(NFT):
                for two in range(2):
                    P.dma("pool", lambda e, f=f, two=two: e.dma_start(
                        out=WSC[li][f, :, :, two * 128:(two + 1) * 128],
                        in_=ffn_wi[li][:, two * FF + f * 128:two * FF + (f + 1) * 128].rearrange("(k p) n -> p k n", p=128)),
                        reads=[B_w], writes=[B_wsc[li][f]])

        def mod_pass(li, modc=None, B_modc=None):
            with ExitStack() as st:
                cc = sb(st, "cc", [128, 2, 8], F32)
                scT = sb(st, "scT", [128, 2, 8], F32)
                screp = sb(st, "screp", [128, 2, 8, 128], F32)
                ones1 = sb(st, "ones1", [1, 128], F32)
                wr = Ring([(sb(st, "adw%d" % i, [128, 8, 512], F32), Buf()) for i in range(2)])
                br = Ring([(sb(st, "adb%d" % i, [1, 512], F32), Buf()) for i in range(2)])
                pr = Ring([(ps(st, "adp%d" % i, [128, 512], F32), PBuf()) for i in range(2)])
                gt = sb(st, "gtmp", [128, 2, D], F32)
                B_cc, B_sc, B_rep, B_o1, B_gt = Buf(), Buf(), Buf(), Buf(), Buf()
                P.dma("sp", lambda e: e.dma_start(out=cc[:], in_=ccT), reads=[B_w], writes=[B_cc])
                P.op("act", lambda e: e.activation(out=scT[:], in_=cc[:], func=AF.Silu), reads=[B_cc], writes=[B_sc])
                P.op("dve", lambda e: e.tensor_copy(screp[:], scT[:].unsqueeze(3).to_broadcast([128, 2, 8, 128])),
                     reads=[B_sc], writes=[B_rep])
                P.op("pool", lambda e: e.memset(ones1[:], 1.0), writes=[B_o1])
                P.dma("sp", lambda e: e.dma_start(out=gt[:, 0, :], in_=normg[:, li, :]), reads=[B_w], writes=[B_gt])
                P.dma("sp", lambda e: e.dma_start(out=gt[:, 1, :], in_=normg[:, 2 + li, :]), reads=[B_w], writes=[B_gt])
                for blk in range(12):
                    wt, B_wt = wr.next()
                    bt, B_bt = br.next()
                    P.dma("sp", lambda e, wt=wt, blk=blk: e.dma_start(
                        out=wt[:], in_=ada_w[li, :, blk * 512:(blk + 1) * 512].rearrange("(k p) n -> p k n", p=128)),
                        reads=[B_w], writes=[B_wt])
                    P.dma("sp", lambda e, bt=bt, blk=blk: e.dma_start(
                        out=bt[:], in_=ada_b[:, li * 6 * D + blk * 512: li * 6 * D + (blk + 1) * 512]),
                        reads=[B_w], writes=[B_bt])
                    for who in ((0, 1) if (modc is not None and blk < 4) else (0,)):
                        pt, B_pt = pr.next()
                        for k in range(8):
                            P.op("pe", lambda e, pt=pt, wt=wt, k=k, who=who: e.matmul(
                                pt[:], screp[:, who, k, :], wt[:, k, :], start=(k == 0), stop=False),
                                reads=[B_rep, B_wt], writes=[B_pt])
                        P.op("pe", lambda e, pt=pt, bt=bt: e.matmul(pt[:], ones1[:], bt[:], start=False, stop=True),
                             reads=[B_o1, B_bt], writes=[B_pt])
                        dst = mod if who == 0 else modc
                        Bd = B_mod if who == 0 else B_modc
                        P.op("act", lambda e, pt=pt, dst=dst, blk=blk: e.activation(
                            out=dst[:, blk * 512:(blk + 1) * 512], in_=pt[:], func=AF.Copy),
                            reads=[B_pt], writes=[Bd])
                for (slot, gi) in ((1, 0), (4, 1)):
                    P.op("dve", lambda e, slot=slot, gi=gi: e.scalar_tensor_tensor(
                        out=mod[:, slot * D:(slot + 1) * D], in0=mod[:, slot * D:(slot + 1) * D], scalar=1.0,
                        in1=gt[:, gi, :], op0=ALU.add, op1=ALU.mult), reads=[B_mod, B_gt], writes=[B_mod])
                if modc is not None:
                    P.op("dve", lambda e: e.scalar_tensor_tensor(
                        out=modc[:, D:2 * D], in0=modc[:, D:2 * D], scalar=1.0,
                        in1=gt[:, 0, :], op0=ALU.add, op1=ALU.mult), reads=[B_modc, B_gt], writes=[B_modc])
                P.barrier()

        def norm_mod_T(st_tiles, xt, B_xt, gs_ap, sh_ap, B_g, dstT_ap, B_dst, evac_eng="act", part="both"):
            junk, B_junk, ss, B_ss, xn, B_xn, pT, B_pT = st_tiles
            if part == "back":
                for k in range(8):
                    P.op("pe", lambda e, k=k: e.transpose(pT[:, k, :], xn[:, k * 128:(k + 1) * 128], ident),
                         reads=[B_xn, B_cst], writes=[B_pT])
                if evac_eng == "act":
                    P.op("act", lambda e: e.activation(out=dstT_ap, in_=pT[:], func=AF.Copy), reads=[B_pT], writes=[B_dst])
                else:
                    P.op("dve", lambda e: e.tensor_copy(dstT_ap, pT[:]), reads=[B_pT], writes=[B_dst])
                return
            P.op("act", lambda e: e.activation(out=junk[:], in_=xt, func=AF.Square, accum_out=ss[:, 0:1]),
                 reads=[B_xt], writes=[B_junk, B_ss])
            P.op("act", lambda e: e.activation(out=ss[:, 1:2], in_=ss[:, 0:1], func=AF.Ln, bias=eps_t[:, 0:1], scale=1.0 / D),
                 reads=[B_ss, B_eps], writes=[B_ss])
            P.op("act", lambda e: e.activation(out=ss[:, 2:3], in_=ss[:, 1:2], func=AF.Exp, scale=-0.5),
                 reads=[B_ss], writes=[B_ss])
            P.op("dve", lambda e: e.scalar_tensor_tensor(out=junk[:], in0=xt, scalar=ss[:, 2:3], in1=gs_ap,
                                                         op0=ALU.mult, op1=ALU.mult),
                 reads=[B_xt, B_ss, B_g], writes=[B_junk])
            P.op("dve", lambda e: e.tensor_tensor(out=xn[:], in0=junk[:], in1=sh_ap, op=ALU.add),
                 reads=[B_junk, B_g], writes=[B_xn])
            if part == "front":
                return
            for k in range(8):
                P.op("pe", lambda e, k=k: e.transpose(pT[:, k, :], xn[:, k * 128:(k + 1) * 128], ident),
                     reads=[B_xn, B_cst], writes=[B_pT])
            if evac_eng == "act":
                P.op("act", lambda e: e.activation(out=dstT_ap, in_=pT[:], func=AF.Copy), reads=[B_pT], writes=[B_dst])
            else:
                P.op("dve", lambda e: e.tensor_copy(dstT_ap, pT[:]), reads=[B_pT], writes=[B_dst])

        eps_t = sb(top, "eps_t", [128, 1], F32)
        B_eps = Buf()
        P.op("pool", lambda e: e.memset(eps_t[:], EPS), writes=[B_eps])

        def norm_tiles(st, pfx):
            return (sb(st, pfx + "junk", [128, D], F32), Buf(), sb(st, pfx + "ss", [128, 4], F32), Buf(),
                    sb(st, pfx + "xn", [128, D], BF16), Buf(), ps(st, pfx + "pT", [128, 8, 128], BF16), PBuf())

        def ffn_pass(li, hin, B_hin, hout, B_hout, final):
            TB = 512
            with ExitStack() as st:
                xnTr = [(sb(st, "f_xnT%d" % i, [128, 8, TB], BF16), [Buf() for _ in range(TB // 128)]) for i in range(2)]
                hidT = sb(st, "f_hidT", [128, NFT, TB], BF16)
                B_hid = [Buf() for _ in range(NFT)]
                wo = sb(st, "f_wo", [128, NFT, D], BF16)
                B_wo = Buf()
                hresr = Ring([(sb(st, "f_hres%d" % i, [128, TB // 128, D], F32), [Buf() for _ in range(TB // 128)]) for i in range(2)])
                nts = [norm_tiles(st, "f_a"), norm_tiles(st, "f_b")]
                wr = Ring([(sb(st, "f_wi%d" % i, [128, 8, 256], BF16), Buf()) for i in range(4)])
                pu = Ring([(ps(st, "f_pu%d" % i, [128, 512], F32), PBuf()) for i in range(4)])
                po = Ring([(ps(st, "f_po%d" % i, [128, 512], F32), PBuf()) for i in range(2)])
                sg = Ring([(sb(st, "f_sg%d" % i, [128, 512], BF16), Buf()) for i in range(2)])
                ot = Ring([(sb(st, "f_ot%d" % i, [128, D], F32), Buf()) for i in range(2)])
                fjunk = sb(st, "f_fjunk", [128, D], F32)
                fss = sb(st, "f_fss", [128, 4], F32)
                B_fj, B_fss = Buf(), Buf()
                fg = sb(st, "f_fg", [128, D], F32)
                B_fg = Buf()
                if final:
                    P.dma("sp", lambda e: e.dma_start(out=fg[:], in_=normg[:, 4, :]), reads=[B_w], writes=[B_fg])
                P.dma("pool", lambda e: e.dma_start(out=wo[:], in_=ffn_wo[li].rearrange("(k p) n -> p k n", p=128)),
                      reads=[B_w], writes=[B_wo])
                NB = L // TB
                hres_of = {}

                def norm_tile(tb, j, part="both"):
                    if j == 0 and part != "back":
                        hres_of[tb] = hresr.next()
                    hres, B_hres = hres_of[tb]
                    xnT, B_xnT = xnTr[tb % 2]
                    t = tb * (TB // 128) + j
                    if part != "back":
                        P.dma("sp", lambda e: e.dma_start(out=hres[:, j, :], in_=hin[t * 128:(t + 1) * 128, :]),
                              reads=[B_hin[t]], writes=[B_hres[j]])
                    norm_mod_T(nts[j % 2], hres[:, j, :], B_hres[j], mod[:, 4 * D:5 * D], mod[:, 3 * D:4 * D], B_mod,
                               xnT[:, :, j * 128:(j + 1) * 128], B_xnT[j], evac_eng=("act" if j % 2 else "dve"), part=part)

                def up(tb):
                    xnT, B_xnT = xnTr[tb % 2]
                    for f in range(NFT):
                        wt, B_wt = wr.next()
                        P.dma("sp", lambda e, wt=wt, f=f: e.dma_start(out=wt[:], in_=WSC[li][f]),
                              reads=[B_wsc[li][f]], writes=[B_wt])
                        p1, B_p1 = pu.next()
                        p2, B_p2 = pu.next()
                        for k in range(8):
                            P.op("pe", lambda e, p1=p1, wt=wt, k=k: e.matmul(
                                p1[:], wt[:, k, 0:128], xnT[:, k, :], start=(k == 0), stop=(k == 7)),
                                reads=[B_wt] + B_xnT, writes=[B_p1])
                        for k in range(8):
                            P.op("pe", lambda e, p2=p2, wt=wt, k=k: e.matmul(
                                p2[:], wt[:, k, 128:256], xnT[:, k, :], start=(k == 0), stop=(k == 7)),
                                reads=[B_wt] + B_xnT, writes=[B_p2])
                        s1, B_s1 = sg.next()
                        P.op("act", lambda e, s1=s1, p1=p1: e.activation(out=s1[:], in_=p1[:], func=AF.Silu),
                             reads=[B_p1], writes=[B_s1])
                        P.op("dve", lambda e, s1=s1, p2=p2, f=f: e.tensor_tensor(
                            out=hidT[:, f, :], in0=s1[:], in1=p2[:], op=ALU.mult),
                            reads=[B_s1, B_p2], writes=[B_hid[f]])
                        if f == NFT - 6 and tb + 1 < NB:
                            norm_tile(tb + 1, 0, "front")
                        if f == NFT - 1 and tb + 1 < NB:
                            norm_tile(tb + 1, 0, "back")

                def out_tile(tb, j):
                    hres, B_hres = hres_of[tb]
                    t = tb * (TB // 128) + j
                    o, B_o = ot.next()
                    for cb in range(2):
                        pq, B_pq = po.next()
                        for f in range(NFT):
                            P.op("pe", lambda e, pq=pq, f=f, cb=cb: e.matmul(
                                pq[:], hidT[:, f, j * 128:(j + 1) * 128], wo[:, f, cb * 512:(cb + 1) * 512],
                                start=(f == 0), stop=(f == NFT - 1)), reads=[B_hid[f], B_wo], writes=[B_pq])
                        P.op("dve", lambda e, pq=pq, cb=cb: e.tensor_tensor(
                            out=o[:, cb * 512:(cb + 1) * 512], in0=pq[:], in1=mod[:, 5 * D + cb * 512:5 * D + (cb + 1) * 512],
                            op=ALU.mult), reads=[B_pq, B_mod], writes=[B_o])
                    P.op("pool", lambda e: e.tensor_tensor(out=o[:], in0=o[:], in1=hres[:, j, :], op=ALU.add),
                         reads=[B_o, B_hres[j]], writes=[B_o])
                    if final:
                        P.op("act", lambda e: e.activation(out=fjunk[:], in_=o[:], func=AF.Square, accum_out=fss[:, 0:1]),
                             reads=[B_o], writes=[B_fj, B_fss])
                        P.op("act", lambda e: e.activation(out=fss[:, 1:2], in_=fss[:, 0:1], func=AF.Ln, bias=eps_t[:, 0:1], scale=1.0 / D),
                             reads=[B_fss, B_eps], writes=[B_fss])
                        P.op("act", lambda e: e.activation(out=fss[:, 2:3], in_=fss[:, 1:2], func=AF.Exp, scale=-0.5),
                             reads=[B_fss], writes=[B_fss])
                        P.op("dve", lambda e: e.scalar_tensor_tensor(out=o[:], in0=o[:], scalar=fss[:, 2:3], in1=fg[:],
                                                                     op0=ALU.mult, op1=ALU.mult),
                             reads=[B_o, B_fss, B_fg], writes=[B_o])
                    P.dma("sp", lambda e: e.dma_start(out=hout[t * 128:(t + 1) * 128, :], in_=o[:]),
                          reads=[B_o], writes=[B_hout[t]], kind="store")

                for j in range(TB // 128):
                    norm_tile(0, j)
                for tb in range(NB):
                    up(tb)
                    for j in range(TB // 128):
                        nxt = tb + 1 < NB and j + 1 < TB // 128
                        if nxt:
                            norm_tile(tb + 1, j + 1, "front")
                        out_tile(tb, j)
                        if nxt:
                            norm_tile(tb + 1, j + 1, "back")
                P.barrier()

        def ssd_layer():
            with ExitStack() as s0:
                dt_all = sb(s0, "dt_all", [128, NT, 64], F32)
                a_hl = sb(s0, "a_hl", [128, NT, 2, 64], BF16)
                B_dt = [Buf() for _ in range(NT)]
                sv = sb(s0, "sv", [128, 160], F32)
                B_sv = Buf()
                P.dma("sp", lambda e: e.dma_start(out=sv[:], in_=ssdv), reads=[B_w], writes=[B_sv])
                P.op("act", lambda e: e.activation(out=sv[:, 64:128], in_=sv[:, 64:128], func=AF.Exp), reads=[B_sv], writes=[B_sv])
                P.op("dve", lambda e: e.tensor_scalar(out=sv[:, 64:128], in0=sv[:, 64:128], scalar1=-1.0, scalar2=None, op0=ALU.mult),
                     reads=[B_sv], writes=[B_sv])
                with ExitStack() as st:
                    a_all = sb(st, "a_all", [128, NT, 64], F32)
                    modc = sb(st, "modc", [128, 2 * D], F32)
                    B_modc = Buf()
                    mod_pass(0, modc, B_modc)
                    xnT = sb(st, "xnT", [128, 8, LT], BF16)
                    B_xnT = [Buf() for _ in range(NT)]
                    nts = [norm_tiles(st, "a_a"), norm_tiles(st, "a_b")]
                    xr = Ring([(sb(st, "a_x%d" % i, [128, D], F32), Buf()) for i in range(3)])
                    wdt = sb(st, "a_wdt", [128, 8, 64], BF16)
                    B_wdt = Buf()
                    pdt = ps(st, "a_pdt", [128, 512], F32)
                    B_pdt = PBuf()
                    dttr = Ring([(sb(st, "a_dtt%d" % i, [128, 64], F32), Buf()) for i in range(2)])
                    P.dma("pool", lambda e: e.dma_start(out=wdt[:], in_=w_in[:, DI + CONVD:PROJ].rearrange("(k p) n -> p k n", p=128)),
                          reads=[B_w], writes=[B_wdt])
                    xts = {}

                    def a_norm(t, part):
                        if part != "back":
                            xts[t] = xr.next()
                            xt, B_xt = xts[t]
                            P.dma("sp", lambda e: e.dma_start(out=xt[:], in_=xin[t * 128:(t + 1) * 128, :]),
                                  reads=[B_xin[t]], writes=[B_xt])
                        xt, B_xt = xts[t]
                        if t < 2:
                            norm_mod_T(nts[t % 2], xt[:], B_xt, modc[:, D:2 * D], modc[:, 0:D], B_modc, xnT[:, :, t * 128:(t + 1) * 128], B_xnT[t],
                                       part=part)
                        else:
                            norm_mod_T(nts[t % 2], xt[:], B_xt, mod[:, D:2 * D], mod[:, 0:D], B_mod, xnT[:, :, t * 128:(t + 1) * 128], B_xnT[t],
                                       evac_eng=("act" if t % 2 else "dve"), part=part)

                    def a_dt(t):
                        dtt, B_dtt = dttr.next()
                        for k in range(8):
                            P.op("pe", lambda e, k=k: e.matmul(pdt[:, 0:64], xnT[:, k, t * 128:(t + 1) * 128], wdt[:, k, :],
                                                               start=(k == 0), stop=(k == 7)),
                                 reads=[B_xnT[t], B_wdt], writes=[B_pdt])
                        P.op("dve", lambda e: e.tensor_tensor(out=dtt[:], in0=pdt[:, 0:64], in1=sv[:, 0:64], op=ALU.add),
                             reads=[B_pdt, B_sv], writes=[B_dtt])
                        P.op("act", lambda e: e.activation(out=dtt[:], in_=dtt[:], func=AF.Exp), reads=[B_dtt], writes=[B_dtt])
                        P.op("act", lambda e: e.activation(out=dt_all[:, t, :], in_=dtt[:], func=AF.Ln, bias=1.0, scale=1.0),
                             reads=[B_dtt], writes=[B_dt[t]])
                        P.op("dve", lambda e: e.tensor_tensor(out=a_all[:, t, :], in0=dt_all[:, t, :], in1=sv[:, 64:128], op=ALU.mult),
                             reads=[B_dt[t], B_sv], writes=[B_dt[t]])
                        P.op("dve", lambda e: e.tensor_copy(a_hl[:, t, 0, :], a_all[:, t, :]), reads=[B_dt[t]], writes=[B_dt[t]])
                        P.op("dve", lambda e: e.tensor_tensor(out=a_hl[:, t, 1, :], in0=a_all[:, t, :], in1=a_hl[:, t, 0, :], op=ALU.subtract),
                             reads=[B_dt[t]], writes=[B_dt[t]])

                    a_norm(0, "front")
                    for t in range(NT):
                        if t + 1 < NT:
                            a_norm(t + 1, "front")
                        a_norm(t, "back")
                        if t >= 1:
                            a_dt(t - 1)
                    a_dt(NT - 1)
                    cast_ffn_weights(0)
                    dump("xnT", xnT[:, :, 0:LT], B_xnT)
                    dump("dt_all", dt_all[:], B_dt)
                    with ExitStack() as s2:
                        wz = Ring([(sb(s2, "a_wz%d" % i, [128, 8, 512], BF16), Buf()) for i in range(2)])
                        pz = Ring([(ps(s2, "a_pz%d" % i, [128, 512], F32), PBuf()) for i in range(2)])
                        zt = Ring([(sb(s2, "a_zt%d" % i, [128, 512], BF16), Buf()) for i in range(3)])
                        for cbk in range(4):
                            w, B_wz = wz.next()
                            P.dma("pool", lambda e, w=w, cbk=cbk: e.dma_start(
                                out=w[:], in_=w_in[:, cbk * 512:(cbk + 1) * 512].rearrange("(k p) n -> p k n", p=128)),
                                reads=[B_w], writes=[B_wz])
                            for t in range(32):
                                pzz, B_pz = pz.next()
                                for k in range(8):
                                    P.op("pe", lambda e, pzz=pzz, w=w, k=k, t=t: e.matmul(
                                        pzz[:], xnT[:, k, CTX + t * 128:CTX + (t + 1) * 128], w[:, k, :], start=(k == 0), stop=(k == 7)),
                                        reads=[B_xnT[t + 2], B_wz], writes=[B_pz])
                                z, B_z = zt.next()
                                P.op("act", lambda e, z=z, pzz=pzz: e.activation(out=z[:], in_=pzz[:], func=AF.Silu),
                                     reads=[B_pz], writes=[B_z])
                                P.dma("sp", lambda e, z=z, t=t, cbk=cbk: e.dma_start(
                                    out=ZS[t * 128:(t + 1) * 128, cbk * 512:(cbk + 1) * 512], in_=z[:]),
                                    reads=[B_z], writes=[B_ZS[t]], kind="store")
                        P.barrier()
                    with ExitStack() as s2:
                        cw = sb(s2, "b_cw", [128, 32, 5], F32)
                        cb_ = sb(s2, "b_cb", [128, 32], F32)
                        B_cw = Buf()
                        P.dma("sp", lambda e: e.dma_start(out=cw[:], in_=convw), reads=[B_w], writes=[B_cw])
                        P.dma("sp", lambda e: e.dma_start(out=cb_[:], in_=convb), reads=[B_w], writes=[B_cw])
                        PW = LT + 8
                        pre = Ring([(sb(s2, "b_pre%d" % i, [128, PW], BF16), Buf()) for i in range(2)])
                        post = Ring([(sb(s2, "b_post%d" % i, [128, LT], BF16), Buf()) for i in range(2)])
                        dg = Ring([(sb(s2, "b_dg%d" % i, [128, 5, 128], BF16), Buf()) for i in range(2)])
                        wb = Ring([(sb(s2, "b_w%d" % i, [128, 8, 128], BF16), Buf()) for i in range(3)])
                        pb = Ring([(ps(s2, "b_pb%d" % i, [128, 512], F32), PBuf()) for i in range(3)])
                        pc = Ring([(ps(s2, "b_pc%d" % i, [128, 512], F32), PBuf()) for i in range(2)])
                        for (pt_, B_p) in pre.items:
                            P.op("pool", lambda e, pt_=pt_: e.memset(pt_[:], 0.0), writes=[B_p])
                        segs = [(0, 256, 2)] + [(CTX + i * 512, 512, 262 + i * 512) for i in range(8)]
                        for c in range(32):
                            w, B_wb = wb.next()
                            P.dma("pool", lambda e, w=w, c=c: e.dma_start(
                                out=w[:], in_=w_in[:, DI + c * 128:DI + (c + 1) * 128].rearrange("(k p) n -> p k n", p=128)),
                                reads=[B_w], writes=[B_wb])
                            pr_, B_pre = pre.next()
                            po_, B_post = post.next()
                            d_, B_dg = dg.next()
                            for k in range(5):
                                P.op("pool", lambda e, d_=d_, k=k, c=c: e.tensor_scalar(
                                    out=d_[:, k, :], in0=ident, scalar1=cw[:, c, k:k + 1], scalar2=None, op0=ALU.mult),
                                    reads=[B_cst, B_cw], writes=[B_dg])
                            for si, (t0, n, off) in enumerate(segs):
                                p_, B_pb = pb.next()
                                rb = B_xnT[t0 // 128:(t0 + n) // 128]
                                for k in range(8):
                                    P.op("pe", lambda e, p_=p_, w=w, k=k, t0=t0, n=n: e.matmul(
                                        p_[:, 0:n], w[:, k, :], xnT[:, k, t0:t0 + n], start=(k == 0), stop=(k == 7)),
                                        reads=[B_wb] + rb, writes=[B_pb])
                                if si % 2 == 0:
                                    P.op("dve", lambda e, p_=p_, pr_=pr_, off=off, n=n: e.tensor_copy(pr_[:, off:off + n], p_[:, 0:n]),
                                         reads=[B_pb], writes=[B_pre])
                                else:
                                    P.op("act", lambda e, p_=p_, pr_=pr_, off=off, n=n: e.activation(
                                        out=pr_[:, off:off + n], in_=p_[:, 0:n], func=AF.Copy), reads=[B_pb], writes=[B_pre])
                            for si, (t0, n, off) in enumerate(segs):
                                q_, B_pc = pc.next()
                                for k in range(5):
                                    P.op("pe", lambda e, q_=q_, d_=d_, pr_=pr_, k=k, off=off, n=n: e.matmul(
                                        q_[:, 0:n], d_[:, k, :], pr_[:, off - 2 + k:off - 2 + k + n], start=(k == 0), stop=(k == 4)),
                                        reads=[B_dg, B_pre], writes=[B_pc])
                                P.op("act", lambda e, q_=q_, po_=po_, t0=t0, n=n, c=c: e.activation(
                                    out=po_[:, t0:t0 + n], in_=q_[:, 0:n], func=AF.Silu, bias=cb_[:, c:c + 1], scale=1.0),
                                    reads=[B_pc, B_cw], writes=[B_post])
                            P.dma("sp", lambda e, po_=po_, c=c: e.dma_start(out=XBC[c * 128:(c + 1) * 128, :], in_=po_[:]),
                                  reads=[B_post], writes=[B_XBC_c[c]], kind="store")
                            if c == 0:
                                dump("post0", po_[:], [B_post])
                        P.barrier()
                if stop_after == "B":
                    return
                ssd_scan(dt_all, a_hl, B_dt, sv, B_sv)

        def ssd_scan(dt_all, a_hl, B_dt, sv, B_sv):
            XBCv = XBC.rearrange("(c p) t -> p c t", p=128)
            cast_ffn_weights(1)
            with ExitStack() as st:
                hTb = sb(st, "c_hTb", [128, DI], F32)
                B_hTb = Buf()
                with ExitStack() as s2:
                    xbc = Ring([(sb(s2, "c_xbc%d" % i, [128, 32, 128], BF16), Buf()) for i in range(3)])
                    pTx = Ring([(ps(s2, "c_pTx%d" % i, [128, 1024], BF16), PBuf()) for i in range(2)])
                    pXr = Ring([(ps(s2, "c_pX%d" % i, [128, 512], F32), PBuf()) for i in range(2)])
                    pScr = Ring([(ps(s2, "c_pSc%d" % i, [128, 512], F32), PBuf()) for i in range(1)])
                    pYr = Ring([(ps(s2, "c_pY%d" % i, [128, 512], F32), PBuf()) for i in range(1)])
                    pMr = Ring([(ps(s2, "c_pM%d" % i, [128, 512], F32), PBuf()) for i in range(2)])
                    xdtr = Ring([(sb(s2, "c_xdt%d" % i, [128, 2, DI], BF16), Buf()) for i in range(2)])
                    xddr = Ring([(sb(s2, "c_xdd%d" % i, [128, 2, DI], BF16), [Buf() for _ in range(8)]) for i in range(2)])
                    btmr = Ring([(sb(s2, "c_btm%d" % i, [128, 8, 128], BF16), Buf()) for i in range(2)])
                    expor = Ring([(sb(s2, "c_expo%d" % i, [128, 192], F32), Buf()) for i in range(2)])
                    Lbuf = [[(sb(s2, "c_L%d_%d" % (par, i), [128, 4, 128], BF16), Buf()) for i in range(16)] for par in range(2)]

                    def buildL(tt, i):
                        g, dr = i // 2, i % 2
                        hb = dr * 32 + g * 4
                        L_, B_L = Lbuf[tt % 2][i]
                        P.op("pool", lambda e, L_=L_, dr=dr, hb=hb, tt=tt: e.tensor_tensor(
                            out=L_[:],
                            in0=cst_b[:, CI_UF + dr, :].unsqueeze(1).to_broadcast([128, 4, 128]),
                            in1=a_hl[:, tt, 0, hb:hb + 4].unsqueeze(2).to_broadcast([128, 4, 128]),
                            op=ALU.mult), reads=[B_cst, B_dt[tt]], writes=[B_L])
                    dec = Ring([(sb(s2, "c_dec%d" % i, [128, 4, 128], BF16), Buf()) for i in range(3)])
                    Mt = Ring([(sb(s2, "c_M%d" % i, [128, 4, 128], BF16), Buf()) for i in range(2)])
                    sm = Ring([(sb(s2, "c_sm%d" % i, [128, 2, 128], BF16), Buf()) for i in range(2)])
                    yp = Ring([(sb(s2, "c_yp%d" % i, [128, DI], F32), [Buf() for _ in range(8)]) for i in range(2)])
                    sbst = Ring([(sb(s2, "c_sbst%d" % i, [128, DI], F32), [Buf() for _ in range(8)]) for i in range(2)])
                    hTf = sb(s2, "c_hTf", [128, DI], F32)
                    hTf16 = sb(s2, "c_hTf16", [128, DI], BF16)
                    B_hTf = [Buf() for _ in range(8)]
                    B_hTf16 = [Buf() for _ in range(8)]
                    tmpg = Ring([(sb(s2, "c_tmpg%d" % i, [128, 256], F32), Buf()) for i in range(2)])
                    P.op("pool", lambda e: e.memset(hTf[:], 0.0), writes=B_hTf)
                    P.op("pool", lambda e: e.memset(hTf16[:], 0.0), writes=B_hTf16)
                    for i in range(16):
                        buildL(2, i)
                    def make_tile(t):
                        lat = t >= 2
                        xb_, B_xb = xbc.next()
                        xdt, B_xdt = xdtr.next()
                        xdd, B_xdd = xddr.next()
                        btm, B_btm = btmr.next()
                        expo, B_expo = expor.next()
                        y_, B_y = yp.next() if lat else (None, None)
                        st_dec, st_sm = {}, {}
                        box = {}

                        def loads():
                            for q4 in range(4):
                                P.dma("sp", lambda e, q4=q4: e.dma_start(
                                    out=xb_[:, q4 * 8:(q4 + 1) * 8, :], in_=XBCv[:, q4 * 8:(q4 + 1) * 8, t * 128:(t + 1) * 128]),
                                    reads=B_XBC_c, writes=[B_xb])

                        def stageABC(i):
                            g, dr = i // 2, i % 2
                            L_, B_L = Lbuf[t % 2][i]
                            pX, B_pX = pXr.next()
                            for h4 in range(4):
                                P.op("pe", lambda e, pX=pX, L_=L_, h4=h4, dr=dr: e.matmul(
                                    pX[:, h4 * 128:(h4 + 1) * 128], L_[:, h4, :], cst_b[:, CI_RF + dr, :], start=True, stop=True),
                                    reads=[B_L, B_cst], writes=[B_pX])
                            dc, B_dc = dec.next()
                            P.op("act", lambda e, dc=dc, pX=pX: e.activation(
                                out=dc[:].rearrange("p h q -> p (h q)"), in_=pX[:], func=AF.Exp),
                                reads=[B_pX], writes=[B_dc])
                            st_dec[i] = (dc, B_dc)

                        def stageD(g):
                            pSc, B_pSc = pScr.next()
                            P.op("pe", lambda e, pSc=pSc, g=g: e.matmul(
                                pSc[:, 0:128], xb_[:, 16 + g, :], xb_[:, 24 + g, :], start=True, stop=True),
                                reads=[B_xb], writes=[B_pSc])
                            sm_, B_sm = sm.next()
                            P.op("dve", lambda e, sm_=sm_, pSc=pSc: e.tensor_tensor(
                                out=sm_[:], in0=pSc[:, 0:128].unsqueeze(1).to_broadcast([128, 2, 128]),
                                in1=cst_b[:, CI_RF:CI_RB + 1, :], op=ALU.mult),
                                reads=[B_pSc, B_cst], writes=[B_sm])
                            st_sm[g] = (sm_, B_sm)

                        def pro():
                            pts = []
                            for hf in range(2):
                                pt_, B_pt = pTx.next()
                                for j in range(8):
                                    P.op("pe", lambda e, pt_=pt_, hf=hf, j=j: e.transpose(
                                        pt_[:, j * 128:(j + 1) * 128], xb_[:, hf * 8 + j, :], ident),
                                        reads=[B_xb, B_cst], writes=[B_pt])
                                pts.append((pt_, B_pt))
                            pS, B_pS = pMr.next()
                            sm_specs = [(CI_RF, 0, 0), (CI_RB, 32, 32), (CI_UF, 0, 64), (CI_UB, 32, 96)]
                            for (ci, ac, oc) in sm_specs:
                                for hl in range(2):
                                    P.op("pe", lambda e, ci=ci, ac=ac, oc=oc, hl=hl: e.matmul(
                                        pS[:, oc:oc + 32], cst_b[:, ci, :], a_hl[:, t, hl, ac:ac + 32], start=(hl == 0), stop=(hl == 1)),
                                        reads=[B_cst, B_dt[t]], writes=[B_pS])
                            for hl in range(2):
                                P.op("pe", lambda e, hl=hl: e.matmul(pS[:, 128:192], cst_b[:, CI_ONE, :], a_hl[:, t, hl, :],
                                                                     start=(hl == 0), stop=(hl == 1)),
                                     reads=[B_cst, B_dt[t]], writes=[B_pS])
                            P.op("act", lambda e: e.activation(out=expo[:], in_=pS[:, 0:192], func=AF.Exp),
                                 reads=[B_pS], writes=[B_expo])
                            for dr in range(2):
                                for hf in range(2):
                                    pt_, B_pt = pts[hf]
                                    P.op("dve", lambda e, pt_=pt_, dr=dr, hf=hf: e.tensor_tensor(
                                        out=xdt[:, dr, hf * 1024:(hf + 1) * 1024].rearrange("p (h j) -> p h j", j=64),
                                        in0=pt_[:].rearrange("p (h j) -> p h j", j=64),
                                        in1=dt_all[:, t, dr * 32 + hf * 16:dr * 32 + hf * 16 + 16].unsqueeze(2).to_broadcast([128, 16, 64]),
                                        op=ALU.mult), reads=[B_pt, B_dt[t]], writes=[B_xdt])
                            if lat:
                                for hf in range(2):
                                    pt_, B_pt = pts[hf]
                                    P.op("dve", lambda e, pt_=pt_, hf=hf: e.tensor_tensor(
                                        out=y_[:, hf * 1024:(hf + 1) * 1024].rearrange("p (h j) -> p h j", j=64),
                                        in0=pt_[:].rearrange("p (h j) -> p h j", j=64),
                                        in1=sv[:, 128 + hf * 16:128 + hf * 16 + 16].unsqueeze(2).to_broadcast([128, 16, 64]),
                                        op=ALU.mult), reads=[B_pt, B_sv], writes=B_y[hf * 4:(hf + 1) * 4])
                            ptb, B_ptb = pTx.next()
                            for g in range(8):
                                P.op("pe", lambda e, g=g: e.transpose(ptb[:, g * 128:(g + 1) * 128], xb_[:, 16 + g, :], ident),
                                     reads=[B_xb, B_cst], writes=[B_ptb])
                            P.op("act", lambda e: e.activation(out=btm[:].rearrange("p g n -> p (g n)"), in_=ptb[:], func=AF.Copy),
                                 reads=[B_ptb], writes=[B_btm])

                        def stageXdd(g):
                            gs = slice(g * 256, (g + 1) * 256)
                            for dr in range(2):
                                P.op("pool", lambda e, g=g, gs=gs, dr=dr: e.tensor_tensor(
                                    out=xdd[:, dr, gs].rearrange("p (h j) -> p h j", j=64),
                                    in0=xdt[:, dr, gs].rearrange("p (h j) -> p h j", j=64),
                                    in1=expo[:, 64 + dr * 32 + g * 4:64 + dr * 32 + g * 4 + 4].unsqueeze(2).to_broadcast([128, 4, 64]),
                                    op=ALU.mult), reads=[B_xdt, B_expo], writes=[B_xdd[g]])

                        def stageEF(i):
                            g, dr = i // 2, i % 2
                            if dr == 0:
                                box["pY"] = pYr.next()
                            pY, B_pY = box["pY"]
                            dc, B_dc = st_dec[i]
                            sm_, B_sm = st_sm[g]
                            M_, B_M = Mt.next()
                            P.op("dve", lambda e, M_=M_, dc=dc, sm_=sm_, dr=dr: e.tensor_tensor(
                                out=M_[:], in0=dc[:], in1=sm_[:, dr, :].unsqueeze(1).to_broadcast([128, 4, 128]), op=ALU.mult),
                                reads=[B_dc, B_sm], writes=[B_M])
                            for h4 in range(4):
                                h = g * 4 + h4
                                P.op("pe", lambda e, pY=pY, M_=M_, h4=h4, h=h, dr=dr: e.matmul(
                                    pY[:, h4 * 64:(h4 + 1) * 64], M_[:, h4, :], xdt[:, dr, h * 64:(h + 1) * 64],
                                    start=(dr == 0 and h4 == 0), stop=(dr == 1 and h4 == 3)), reads=[B_M, B_xdt], writes=[B_pY])
                            return pY, B_pY

                        def stageG1(g, pYt):
                            gs = slice(g * 256, (g + 1) * 256)
                            pY, B_pY = pYt
                            P.op("dve", lambda e, pY=pY, gs=gs: e.tensor_tensor(out=y_[:, gs], in0=y_[:, gs], in1=pY[:, 0:256], op=ALU.add),
                                 reads=[B_y[g], B_pY], writes=[B_y[g]])

                        def stageG2(g):
                            sbt, B_sbt = box["sbt"]
                            gs = slice(g * 256, (g + 1) * 256)
                            P.op("pool", lambda e, g=g, gs=gs: e.tensor_tensor(
                                out=hTf[:, gs].rearrange("p (h j) -> p h j", j=64),
                                in0=hTf[:, gs].rearrange("p (h j) -> p h j", j=64),
                                in1=expo[:, 128 + g * 4:128 + g * 4 + 4].unsqueeze(2).to_broadcast([128, 4, 64]), op=ALU.mult),
                                reads=[B_hTf[g], B_expo, B_hTf16[g]], writes=[B_hTf[g]])
                            if lat:
                                pO, B_pO = pMr.next()
                                P.op("pe", lambda e, pO=pO, g=g, gs=gs: e.matmul(
                                    pO[:, 0:256], xb_[:, 24 + g, :], hTf16[:, gs], start=True, stop=True),
                                    reads=[B_xb, B_hTf16[g]], writes=[B_pO])
                            pSt, B_pSt = pMr.next()
                            for dr in range(2):
                                P.op("pe", lambda e, pSt=pSt, g=g, dr=dr, gs=gs: e.matmul(
                                    pSt[:, dr * 256:(dr + 1) * 256], btm[:, g, :], xdd[:, dr, gs], start=True, stop=True),
                                    reads=[B_btm, B_xdd[g]], writes=[B_pSt])
                            if lat:
                                tg, B_tg = tmpg.next()
                                P.op("dve", lambda e, tg=tg, pO=pO, g=g: e.tensor_tensor(
                                    out=tg[:].rearrange("p (h j) -> p h j", j=64),
                                    in0=pO[:, 0:256].rearrange("p (h j) -> p h j", j=64),
                                    in1=expo[:, g * 4:g * 4 + 4].unsqueeze(2).to_broadcast([128, 4, 64]), op=ALU.mult),
                                    reads=[B_pO, B_expo], writes=[B_tg])
                            P.op("dve", lambda e, pSt=pSt, gs=gs: e.tensor_tensor(out=hTf[:, gs], in0=hTf[:, gs], in1=pSt[:, 0:256], op=ALU.add),
                                 reads=[B_hTf[g], B_pSt], writes=[B_hTf[g]])
                            if lat:
                                P.op("pool", lambda e, tg=tg, gs=gs: e.tensor_tensor(out=y_[:, gs], in0=y_[:, gs], in1=tg[:], op=ALU.add),
                                     reads=[B_y[g], B_tg], writes=[B_y[g]])
                            P.op("act", lambda e, pSt=pSt, gs=gs: e.activation(out=sbt[:, gs], in_=pSt[:, 256:512], func=AF.Copy),
                                 reads=[B_pSt], writes=[B_sbt[g]])
                            P.op("act", lambda e, gs=gs: e.activation(out=hTf16[:, gs], in_=hTf[:, gs], func=AF.Copy),
                                 reads=[B_hTf[g]], writes=[B_hTf16[g]])

                        def body(inject):
                            box["sbt"] = sbst.next()
                            sbt, B_sbt = box["sbt"]
                            if lat:
                                stageABC(0)
                                stageABC(1)
                                stageD(0)
                                pend = None
                                for i in range(16):
                                    g, dr = i // 2, i % 2
                                    if i + 2 < 16:
                                        stageABC(i + 2)
                                    if dr == 0 and g + 1 < 8:
                                        stageD(g + 1)
                                    if dr == 0:
                                        stageXdd(g)
                                    pYt = stageEF(i)
                                    if dr == 1:
                                        stageG1(g, pYt)
                                        if pend is not None:
                                            stageG2(pend)
                                            if t + 1 < NT:
                                                buildL(t + 1, 2 * pend)
                                                buildL(t + 1, 2 * pend + 1)
                                        pend = g
                                        if g == 3:
                                            inject()
                                stageG2(pend)
                                if t + 1 < NT:
                                    buildL(t + 1, 2 * pend)
                                    buildL(t + 1, 2 * pend + 1)
                            else:
                                for g in range(8):
                                    stageXdd(g)
                                    stageG2(g)
                                    if g == 3:
                                        inject()
                            P.dma("sp", lambda e: e.dma_start(out=SBS[t], in_=sbt[:]), reads=B_sbt, writes=[B_SBS[t]], kind="store")
                            if lat:
                                P.dma("sp", lambda e: e.dma_start(out=YP[(t - 2) * 128:(t - 1) * 128, :], in_=y_[:]),
                                      reads=B_y, writes=[B_YP[t - 2]], kind="store")
                            if t == 1:
                                dump("hf_ctx", hTf[:], B_hTf)

                        return loads, pro, body

                    tiles = {0: make_tile(0), 1: make_tile(1)}
                    tiles[0][0]()
                    tiles[1][0]()
                    tiles[0][1]()
                    for t in range(NT):
                        def inject(t=t):
                            if t + 1 < NT:
                                tiles[t + 1][1]()
                            if t + 2 < NT:
                                tiles[t + 2] = make_tile(t + 2)
                                tiles[t + 2][0]()
                        tiles[t][2](inject)
                        del tiles[t]
                    P.barrier()
                if stop_after == "C":
                    return
                with ExitStack() as s2:
                    wo = sb(s2, "d_wo", [128, 16, D], BF16)
                    B_wo = Buf()
                    P.dma("pool", lambda e: e.dma_start(out=wo[:], in_=w_out.rearrange("(k p) n -> p k n", p=128)), reads=[B_w], writes=[B_wo])
                    nw = sb(s2, "d_nw", [128, DI], F32)
                    B_nw = Buf()
                    P.dma("sp", lambda e: e.dma_start(out=nw[:], in_=ssd_nw), reads=[B_w], writes=[B_nw])
                    hb16 = sb(s2, "d_hb16", [128, DI], BF16)
                    B_hb16 = Buf()
                    P.op("pool", lambda e: e.memset(hTb[:], 0.0), writes=[B_hTb])
                    P.op("pool", lambda e: e.memset(hb16[:], 0.0), writes=[B_hb16])
                    ct = Ring([(sb(s2, "d_ct%d" % i, [128, 8, 128], BF16), Buf()) for i in range(2)])
                    sbl = Ring([(sb(s2, "d_sbl%d" % i, [128, DI], F32), Buf()) for i in range(2)])
                    ypl = Ring([(sb(s2, "d_ypl%d" % i, [128, DI], F32), [Buf() for _ in range(8)]) for i in range(3)])
                    zl = Ring([(sb(s2, "d_zl%d" % i, [128, DI], BF16), Buf()) for i in range(3)])
                    xl = Ring([(sb(s2, "d_xl%d" % i, [128, D], F32), Buf()) for i in range(3)])
                    pG = Ring([(ps(s2, "d_pG%d" % i, [128, 512], F32), PBuf()) for i in range(6)])
                    pTy = Ring([(ps(s2, "d_pTy%d" % i, [128, 1024], BF16), PBuf()) for i in range(2)])
                    expr = Ring([(sb(s2, "d_expd%d" % i, [128, 64], F32), Buf()) for i in range(2)])
                    tgr = Ring([(sb(s2, "d_tga%d" % i, [128, DI], F32), Buf()) for i in range(1)])
                    ynr = Ring([(sb(s2, "d_yn%d" % i, [128, DI], BF16), Buf()) for i in range(1)])
                    ynTr = Ring([(sb(s2, "d_ynT%d" % i, [128, 16, 128], BF16), Buf()) for i in range(2)])
                    dssr = Ring([(sb(s2, "d_ss%d" % i, [128, 4], F32), Buf()) for i in range(2)])
                    djr = Ring([(sb(s2, "d_junk%d" % i, [128, DI], BF16), Buf()) for i in range(1)])
                    ho = Ring([(sb(s2, "d_ho%d" % i, [128, D], F32), Buf()) for i in range(2)])
                    order = [1, 0] + list(range(NT - 1, 1, -1))

                    def front(t):
                        lat = t >= 2
                        s_, B_s = sbl.next()
                        P.dma("sp", lambda e: e.dma_start(out=s_[:], in_=SBS[t]), reads=[B_SBS[t]], writes=[B_s])
                        pS, B_pS = pG.next()
                        ex, B_ex = expr.next()
                        for ci, oc in ((CI_RB, 0), (CI_ONE, 32)):
                            for hl in range(2):
                                P.op("pe", lambda e, ci=ci, oc=oc, hl=hl: e.matmul(
                                    pS[:, oc:oc + 32], cst_b[:, ci, :], a_hl[:, t, hl, 32:64], start=(hl == 0), stop=(hl == 1)),
                                    reads=[B_cst, B_dt[t]], writes=[B_pS])
                        P.op("act", lambda e: e.activation(out=ex[:, 0:64], in_=pS[:, 0:64], func=AF.Exp), reads=[B_pS], writes=[B_ex])
                        C = None
                        if lat:
                            tl = t - 2
                            c_, B_c = ct.next()
                            P.dma("sp", lambda e: e.dma_start(out=c_[:], in_=XBCv[:, 24:32, t * 128:(t + 1) * 128]),
                                  reads=B_XBC_c, writes=[B_c])
                            y_, B_y = ypl.next()
                            P.dma("sp", lambda e: e.dma_start(out=y_[:], in_=YP[tl * 128:(tl + 1) * 128, :]),
                                  reads=[B_YP[tl]], writes=B_y)
                            z_, B_z = zl.next()
                            P.dma("sp", lambda e: e.dma_start(out=z_[:], in_=ZS[tl * 128:(tl + 1) * 128, :]),
                                  reads=[B_ZS[tl]], writes=[B_z])
                            x_, B_x = xl.next()
                            P.dma("sp", lambda e: e.dma_start(out=x_[:], in_=xin[t * 128:(t + 1) * 128, :]),
                                  reads=[B_xin[t]], writes=[B_x])
                            banks = []
                            for k in range(4):
                                pO, B_pO = pG.next()
                                banks.append((pO, B_pO))
                                for h2 in range(2):
                                    g = 2 * k + h2
                                    gs = slice(g * 256, (g + 1) * 256)
                                    P.op("pe", lambda e, pO=pO, g=g, gs=gs, h2=h2: e.matmul(
                                        pO[:, h2 * 256:(h2 + 1) * 256], c_[:, g, :], hb16[:, gs], start=True, stop=True),
                                        reads=[B_c, B_hb16], writes=[B_pO])
                            C = (tl, y_, B_y, z_, B_z, x_, B_x)
                        P.op("dve", lambda e: e.tensor_tensor(
                            out=hTb[:].rearrange("p (h j) -> p h j", j=64), in0=hTb[:].rearrange("p (h j) -> p h j", j=64),
                            in1=ex[:, 32:64].unsqueeze(2).to_broadcast([128, 32, 64]), op=ALU.mult),
                            reads=[B_hTb, B_ex, B_hb16], writes=[B_hTb])
                        P.op("dve", lambda e: e.tensor_tensor(out=hTb[:], in0=hTb[:], in1=s_[:], op=ALU.add),
                             reads=[B_hTb, B_s], writes=[B_hTb])
                        P.op("act", lambda e: e.activation(out=hb16[:], in_=hTb[:], func=AF.Copy), reads=[B_hTb], writes=[B_hb16])
                        if lat:
                            tga, B_tga = tgr.next()
                            for k in range(4):
                                pO, B_pO = banks[k]
                                P.op("dve", lambda e, pO=pO, k=k: e.tensor_tensor(
                                    out=tga[:, k * 512:(k + 1) * 512].rearrange("p (h j) -> p h j", j=64),
                                    in0=pO[:].rearrange("p (h j) -> p h j", j=64),
                                    in1=ex[:, k * 8:k * 8 + 8].unsqueeze(2).to_broadcast([128, 8, 64]), op=ALU.mult),
                                    reads=[B_pO, B_ex], writes=[B_tga])
                            P.op("dve", lambda e: e.tensor_tensor(out=y_[:], in0=y_[:], in1=tga[:], op=ALU.add),
                                 reads=B_y + [B_tga], writes=B_y)
                        if t == 0:
                            dump("hb_ctx", hTb[:], [B_hTb])
                        return C

                    def back(C):
                        tl, y_, B_y, z_, B_z, x_, B_x = C
                        yn, B_yn = ynr.next()
                        ynT, B_ynT = ynTr.next()
                        djunk, B_dj = djr.next()
                        dss, B_dss = dssr.next()
                        if tl == 5:
                            dump("y5", y_[:], B_y)
                        P.op("dve", lambda e: e.tensor_tensor(out=y_[:], in0=y_[:], in1=z_[:], op=ALU.mult),
                             reads=B_y + [B_z], writes=B_y)
                        P.op("act", lambda e: e.activation(out=djunk[:], in_=y_[:], func=AF.Square, accum_out=dss[:, 0:1]),
                             reads=B_y, writes=[B_dj, B_dss])
                        P.op("act", lambda e: e.activation(out=dss[:, 1:2], in_=dss[:, 0:1], func=AF.Ln, bias=eps_t[:, 0:1], scale=1.0 / DI),
                             reads=[B_dss, B_eps], writes=[B_dss])
                        P.op("act", lambda e: e.activation(out=dss[:, 2:3], in_=dss[:, 1:2], func=AF.Exp, scale=-0.5), reads=[B_dss], writes=[B_dss])
                        P.op("dve", lambda e: e.scalar_tensor_tensor(out=yn[:], in0=y_[:], scalar=dss[:, 2:3], in1=nw[:],
                                                                     op0=ALU.mult, op1=ALU.mult),
                             reads=B_y + [B_dss, B_nw], writes=[B_yn])
                        for hf in range(2):
                            pt_, B_pt = pTy.next()
                            for j in range(8):
                                P.op("pe", lambda e, pt_=pt_, hf=hf, j=j: e.transpose(
                                    pt_[:, j * 128:(j + 1) * 128], yn[:, (hf * 8 + j) * 128:(hf * 8 + j + 1) * 128], ident),
                                    reads=[B_yn, B_cst], writes=[B_pt])
                            if hf == 0:
                                P.op("act", lambda e, pt_=pt_, hf=hf: e.activation(
                                    out=ynT[:, hf * 8:(hf + 1) * 8, :].rearrange("p k t -> p (k t)"), in_=pt_[:], func=AF.Copy),
                                    reads=[B_pt], writes=[B_ynT])
                            else:
                                P.op("dve", lambda e, pt_=pt_, hf=hf: e.tensor_copy(
                                    ynT[:, hf * 8:(hf + 1) * 8, :].rearrange("p k t -> p (k t)"), pt_[:]),
                                    reads=[B_pt], writes=[B_ynT])
                        h_, B_h = ho.next()
                        for cb in range(2):
                            pq, B_pq = pG.next()
                            for k in range(16):
                                P.op("pe", lambda e, pq=pq, k=k, cb=cb: e.matmul(
                                    pq[:], ynT[:, k, :], wo[:, k, cb * 512:(cb + 1) * 512], start=(k == 0), stop=(k == 15)),
                                    reads=[B_ynT, B_wo], writes=[B_pq])
                            P.op("dve", lambda e, pq=pq, cb=cb: e.tensor_tensor(
                                out=h_[:, cb * 512:(cb + 1) * 512], in0=pq[:], in1=mod[:, 2 * D + cb * 512:2 * D + (cb + 1) * 512], op=ALU.mult),
                                reads=[B_pq, B_mod], writes=[B_h])
                        P.op("pool", lambda e: e.tensor_tensor(out=h_[:], in0=h_[:], in1=x_[:], op=ALU.add),
                             reads=[B_h, B_x], writes=[B_h])
                        P.dma("sp", lambda e: e.dma_start(out=HA[tl * 128:(tl + 1) * 128, :], in_=h_[:]),
                              reads=[B_h], writes=[B_HA[tl]], kind="store")

                    pend = None
                    for t in order:
                        C = front(t)
                        if pend is not None:
                            back(pend)
                        pend = C
                    back(pend)
                    P.barrier()

        def conf_layer():
            with ExitStack() as st:
                mod_pass(1)
                xnT = sb(st, "g_xnT", [128, 8, L], BF16)
                B_xnT = [Buf() for _ in range(32)]
                cf = sb(st, "g_cf", [128, 8, 37], F32)
                B_cf = Buf()
                P.dma("sp", lambda e: e.dma_start(out=cf[:], in_=cfv), reads=[B_w], writes=[B_cf])
                with ExitStack() as s2:
                    nts = [norm_tiles(s2, "g_a"), norm_tiles(s2, "g_b")]
                    xr = Ring([(sb(s2, "g_x%d" % i, [128, D], F32), Buf()) for i in range(3)])
                    xts = {}

                    def g_norm(t, part):
                        if part != "back":
                            xts[t] = xr.next()
                            xt, B_xt = xts[t]
                            P.dma("sp", lambda e: e.dma_start(out=xt[:], in_=HB[t * 128:(t + 1) * 128, :]),
                                  reads=[B_HB[t]], writes=[B_xt])
                        xt, B_xt = xts[t]
                        norm_mod_T(nts[t % 2], xt[:], B_xt, mod[:, D:2 * D], mod[:, 0:D], B_mod, xnT[:, :, t * 128:(t + 1) * 128], B_xnT[t],
                                   evac_eng=("act" if t % 2 else "dve"), part=part)

                    g_norm(0, "front")
                    for t in range(32):
                        if t + 1 < 32:
                            g_norm(t + 1, "front")
                        g_norm(t, "back")
                    P.barrier()
                with ExitStack() as s2:
                    HW_ = 94
                    ub = Ring([(sb(s2, "g_u%d" % i, [128, 64 * HW_], BF16), Buf()) for i in range(2)])
                    vb = Ring([(sb(s2, "g_v%d" % i, [128, L], BF16), Buf()) for i in range(2)])
                    dg = Ring([(sb(s2, "g_dg%d" % i, [128, CK, 128], BF16), Buf()) for i in range(2)])
                    wa = Ring([(sb(s2, "g_wa%d" % i, [128, 8, 256], BF16), Buf()) for i in range(2)])
                    pa = Ring([(ps(s2, "g_pa%d" % i, [128, 512], F32), PBuf()) for i in range(4)])
                    pc = Ring([(ps(s2, "g_pc%d" % i, [128, 512], F32), PBuf()) for i in range(3)])
                    sgm = Ring([(sb(s2, "g_sg%d" % i, [128, 512], F32), Buf()) for i in range(2)])
                    for (u_, B_u) in ub.items:
                        P.op("pool", lambda e, u_=u_: e.memset(u_[:], 0.0), writes=[B_u])
                    for c in range(8):
                        hor = c < 4
                        w, B_wa = wa.next()
                        for two in range(2):
                            P.dma("pool", lambda e, w=w, c=c, two=two: e.dma_start(
                                out=w[:, :, two * 128:(two + 1) * 128],
                                in_=pw1[:, two * D + c * 128:two * D + (c + 1) * 128].rearrange("(k p) n -> p k n", p=128)),
                                reads=[B_w], writes=[B_wa])
                        u_, B_u = ub.next()
                        v_, B_v = vb.next()
                        d_, B_dg = dg.next()
                        if c in (4, 5):
                            P.op("pool", lambda e, u_=u_: e.memset(u_[:], 0.0), writes=[B_u])
                        for k in range(CK):
                            P.op("pool" if k % 2 else "dve", lambda e, d_=d_, k=k, c=c: e.tensor_scalar(
                                out=d_[:, k, :], in0=ident, scalar1=cf[:, c, 5 + k:6 + k], scalar2=None, op0=ALU.mult),
                                reads=[B_cst, B_cf], writes=[B_dg])
                        if hor:
                            uv = u_[:].rearrange("p (r w) -> p r w", w=HW_)
                        else:
                            uv = u_[:].rearrange("p (r w) -> p r w", w=64)
                        for i in range(8):
                            p1, B_p1 = pa.next()
                            p2, B_p2 = pa.next()
                            rb = B_xnT[i * 4:(i + 1) * 4]
                            for k in range(8):
                                P.op("pe", lambda e, p1=p1, w=w, k=k, i=i: e.matmul(
                                    p1[:], w[:, k, 0:128], xnT[:, k, i * 512:(i + 1) * 512], start=(k == 0), stop=(k == 7)),
                                    reads=[B_wa] + rb, writes=[B_p1])
                            for k in range(8):
                                P.op("pe", lambda e, p2=p2, w=w, k=k, i=i: e.matmul(
                                    p2[:], w[:, k, 128:256], xnT[:, k, i * 512:(i + 1) * 512], start=(k == 0), stop=(k == 7)),
                                    reads=[B_wa] + rb, writes=[B_p2])
                            s_, B_s = sgm.next()
                            P.op("act", lambda e, s_=s_, p2=p2, c=c: e.activation(out=s_[:], in_=p2[:], func=AF.Sigmoid, bias=cf[:, c, 1:2], scale=1.0),
                                 reads=[B_p2, B_cf], writes=[B_s])
                            if hor:
                                dst = uv[:, i * 8:(i + 1) * 8, 15:79]
                            else:
                                dst = uv[:, 15 + i * 8:15 + (i + 1) * 8, :]
                            P.op("dve", lambda e, dst=dst, p1=p1, s_=s_, c=c: e.scalar_tensor_tensor(
                                out=dst, in0=p1[:].rearrange("p (r w) -> p r w", w=64), scalar=cf[:, c, 0:1],
                                in1=s_[:].rearrange("p (r w) -> p r w", w=64), op0=ALU.add, op1=ALU.mult),
                                reads=[B_p1, B_s, B_cf], writes=[B_u])
                        for i in range(8):
                            q_, B_q = pc.next()
                            for k in range(CK):
                                if hor:
                                    src = uv[:, i * 8:(i + 1) * 8, k:k + 64]
                                else:
                                    src = uv[:, i * 8 + k:i * 8 + k + 8, :]
                                P.op("pe", lambda e, q_=q_, d_=d_, k=k, src=src: e.matmul(
                                    q_[:].rearrange("p (r w) -> p r w", w=64), d_[:, k, :], src, start=(k == 0), stop=(k == CK - 1)),
                                    reads=[B_dg, B_u], writes=[B_q])
                            P.op("act", lambda e, q_=q_, v_=v_, i=i, c=c: e.activation(
                                out=v_[:, i * 512:(i + 1) * 512], in_=q_[:], func=AF.Identity, bias=cf[:, c, 2:3], scale=1.0),
                                reads=[B_q, B_cf], writes=[B_v])
                        P.dma("sp", lambda e, v_=v_, c=c: e.dma_start(out=VS[c * 128:(c + 1) * 128, :], in_=v_[:]),
                              reads=[B_v], writes=[B_VS[c]], kind="store")
                    P.barrier()
                if stop_after == "G":
                    return
                with ExitStack() as s2:
                    VSv = VS.rearrange("(c p) t -> p c t", p=128)
                    w2 = sb(s2, "h_w2", [128, 8, D], BF16)
                    B_w2 = Buf()
                    P.dma("pool", lambda e: e.dma_start(out=w2[:], in_=pw2.rearrange("(k p) n -> p k n", p=128)), reads=[B_w], writes=[B_w2])
                    b2 = sb(s2, "h_b2", [1, D], BF16)
                    ones1 = sb(s2, "h_ones1", [1, 128], BF16)
                    B_b2 = Buf()
                    P.dma("pool", lambda e: e.dma_start(out=b2[:], in_=bpw2), reads=[B_w], writes=[B_b2])
                    P.op("pool", lambda e: e.memset(ones1[:], 1.0), writes=[B_b2])
                    vt = Ring([(sb(s2, "h_vt%d" % i, [128, 8, 512], BF16), Buf()) for i in range(2)])
                    sq = Ring([(sb(s2, "h_sq%d" % i, [128, 512], BF16), Buf()) for i in range(3)])
                    pst = Ring([(ps(s2, "h_pst%d" % i, [128, 512], F32), PBuf()) for i in range(4)])
                    mur = Ring([(sb(s2, "h_mu%d" % i, [128, 512], F32), Buf()) for i in range(2)])
                    rsr = Ring([(sb(s2, "h_rs%d" % i, [128, 512], F32), Buf()) for i in range(2)])
                    tn = Ring([(sb(s2, "h_tn%d" % i, [128, 512], F32), Buf()) for i in range(3)])
                    sTr = Ring([(sb(s2, "h_sT%d" % i, [128, 8, 512], BF16), Buf()) for i in range(2)])
                    po = Ring([(ps(s2, "h_po%d" % i, [128, 512], F32), PBuf()) for i in range(3)])
                    xr = Ring([(sb(s2, "h_x%d" % i, [128, D], F32), Buf()) for i in range(2)])
                    ho = Ring([(sb(s2, "h_ho%d" % i, [128, D], F32), Buf()) for i in range(2)])
                    hctx = {}

                    def h_front(i):
                        v_, B_v = vt.next()
                        mu, B_mu = mur.next()
                        rs, B_rs = rsr.next()
                        sT, B_sT = sTr.next()
                        P.dma("sp", lambda e, v_=v_, i=i, mu=mu, rs=rs, sT=sT: e.dma_start(out=v_[:], in_=VSv[:, :, i * 512:(i + 1) * 512]), reads=B_VS, writes=[B_v])
                        p_s, B_ps = pst.next()
                        p_q, B_pq2 = pst.next()
                        for c in range(8):
                            P.op("pe", lambda e, p_s=p_s, v_=v_, c=c, mu=mu, rs=rs, sT=sT: e.matmul(p_s[:], cst_b[:, CI_ONE, :], v_[:, c, :], start=(c == 0), stop=(c == 7)),
                                 reads=[B_cst, B_v], writes=[B_ps])
                        for c in range(8):
                            s_, B_s = sq.next()
                            P.op("pool" if c % 2 else "dve", lambda e, s_=s_, v_=v_, c=c, mu=mu, rs=rs, sT=sT: e.tensor_tensor(out=s_[:], in0=v_[:, c, :], in1=v_[:, c, :], op=ALU.mult),
                                 reads=[B_v], writes=[B_s])
                            P.op("pe", lambda e, p_q=p_q, s_=s_, c=c, mu=mu, rs=rs, sT=sT: e.matmul(p_q[:], cst_b[:, CI_ONE, :], s_[:], start=(c == 0), stop=(c == 7)),
                                 reads=[B_cst, B_s], writes=[B_pq2])
                        P.op("act", lambda e, p_s=p_s, mu=mu, rs=rs, sT=sT: e.activation(out=mu[:], in_=p_s[:], func=AF.Copy, scale=1.0 / D), reads=[B_ps], writes=[B_mu])
                        P.op("dve", lambda e, mu=mu, rs=rs, sT=sT: e.tensor_tensor(out=rs[:], in0=mu[:], in1=mu[:], op=ALU.mult), reads=[B_mu], writes=[B_rs])
                        P.op("dve", lambda e, p_q=p_q, mu=mu, rs=rs, sT=sT: e.scalar_tensor_tensor(out=rs[:], in0=p_q[:], scalar=1.0 / D, in1=rs[:], op0=ALU.mult, op1=ALU.subtract),
                             reads=[B_pq2, B_rs], writes=[B_rs])
                        P.op("act", lambda e, mu=mu, rs=rs, sT=sT: e.activation(out=rs[:], in_=rs[:], func=AF.Ln, bias=eps_t[:, 0:1], scale=1.0), reads=[B_rs, B_eps], writes=[B_rs])
                        P.op("act", lambda e, mu=mu, rs=rs, sT=sT: e.activation(out=rs[:], in_=rs[:], func=AF.Exp, scale=-0.5), reads=[B_rs], writes=[B_rs])
                        for c in range(8):
                            t_, B_t = tn.next()
                            P.op("dve", lambda e, t_=t_, v_=v_, c=c, mu=mu, rs=rs, sT=sT: e.tensor_tensor(out=t_[:], in0=v_[:, c, :], in1=mu[:], op=ALU.subtract),
                                 reads=[B_v, B_mu], writes=[B_t])
                            P.op("pool", lambda e, t_=t_, mu=mu, rs=rs, sT=sT: e.tensor_tensor(out=t_[:], in0=t_[:], in1=rs[:], op=ALU.mult), reads=[B_t, B_rs], writes=[B_t])
                            P.op("act", lambda e, t_=t_, c=c, mu=mu, rs=rs, sT=sT: e.activation(out=sT[:, c, :], in_=t_[:], func=AF.Silu, bias=cf[:, c, 4:5], scale=cf[:, c, 3:4]),
                                 reads=[B_t, B_cf], writes=[B_sT])
                        hctx[i] = (sT, B_sT)

                    def h_back(i):
                        sT, B_sT = hctx.pop(i)
                        mu = rs = None
                        for j in range(4):
                            t = i * 4 + j
                            xt, B_xt = xr.next()
                            P.dma("sp", lambda e, xt=xt, t=t, mu=mu, rs=rs, sT=sT: e.dma_start(out=xt[:], in_=HB[t * 128:(t + 1) * 128, :]), reads=[B_HB[t]], writes=[B_xt])
                            h_, B_h = ho.next()
                            for cb in range(2):
                                pq, B_pq = po.next()
                                for c in range(8):
                                    P.op("pe", lambda e, pq=pq, c=c, j=j, cb=cb, mu=mu, rs=rs, sT=sT: e.matmul(
                                        pq[:], sT[:, c, j * 128:(j + 1) * 128], w2[:, c, cb * 512:(cb + 1) * 512], start=(c == 0), stop=False),
                                        reads=[B_sT, B_w2], writes=[B_pq])
                                P.op("pe", lambda e, pq=pq, cb=cb, mu=mu, rs=rs, sT=sT: e.matmul(pq[:], ones1[:], b2[:, cb * 512:(cb + 1) * 512], start=False, stop=True),
                                     reads=[B_b2], writes=[B_pq])
                                P.op("dve", lambda e, h_=h_, pq=pq, cb=cb, mu=mu, rs=rs, sT=sT: e.tensor_tensor(
                                    out=h_[:, cb * 512:(cb + 1) * 512], in0=pq[:], in1=mod[:, 2 * D + cb * 512:2 * D + (cb + 1) * 512], op=ALU.mult),
                                    reads=[B_pq, B_mod], writes=[B_h])
                            P.op("pool", lambda e, h_=h_, xt=xt, mu=mu, rs=rs, sT=sT: e.tensor_tensor(out=h_[:], in0=h_[:], in1=xt[:], op=ALU.add),
                                 reads=[B_h, B_xt], writes=[B_h])
                            P.dma("sp", lambda e, h_=h_, t=t, mu=mu, rs=rs, sT=sT: e.dma_start(out=HA[t * 128:(t + 1) * 128, :], in_=h_[:]), reads=[B_h], writes=[B_HA[t]], kind="store")

                    h_front(0)
                    for i in range(8):
                        if i + 1 < 8:
                            h_front(i + 1)
                        h_back(i)
                    P.barrier()

        ssd_layer()
        if stop_after in (None, "D", "E", "G", "H"):
            if stop_after != "D":
                ffn_pass(0, HA, B_HA, HB, B_HB, final=False)
            if stop_after in (None, "G", "H"):
                conf_layer()
            if stop_after is None:
                ffn_pass(1, HA, B_HA, out, B_out, final=True)
        for name, (src, bufs) in {"HA": (HA, B_HA), "HB": (HB, B_HB)}.items():
            if name in dbg_ap:
                P.dma("sp", lambda e, name=name, src=src: e.dma_start(out=dbg_ap[name], in_=src), reads=bufs, writes=[B_dbg])
        P.emit()
    build_nc.last_prog = P
    return nc


def _prep_inputs(inp, b):
    f = np.float32
    rep = lambda v: np.ascontiguousarray(np.broadcast_to(np.asarray(v, f).reshape(1, -1), (128, np.asarray(v).size)))
    m = {}
    m["xin"] = np.ascontiguousarray(np.concatenate([inp["ctx"][b], inp["x"][b]], axis=0), dtype=f)
    cc = np.stack([inp["c"][b].reshape(8, 128).T, inp["c_ctx"].reshape(8, 128).T], axis=1)
    m["ccT"] = np.ascontiguousarray(cc, dtype=f)
    m["ada_w"] = inp["ada_w"]
    m["ada_b"] = np.ascontiguousarray(inp["ada_b"].reshape(1, -1), dtype=f)
    ng = np.stack([inp["norm_mix_g"][0], inp["norm_mix_g"][1], inp["norm_ffn_g"][0], inp["norm_ffn_g"][1], inp["final_norm_g"]], axis=0)
    m["normg"] = np.ascontiguousarray(np.broadcast_to(ng[None], (128, 5, D)), dtype=f)
    m["consts"] = _consts()
    m["ssd_w_in"] = inp["ssd_w_in"][0]
    m["ssd_convw"] = np.ascontiguousarray(inp["ssd_conv_w"][0].reshape(5, 32, 128).transpose(2, 1, 0), dtype=f)
    m["ssd_convb"] = np.ascontiguousarray(inp["ssd_conv_b"][0].reshape(32, 128).T, dtype=f)
    sv = np.concatenate([inp["ssd_dt_bias_f"][0], inp["ssd_dt_bias_b"][0], inp["ssd_a_log_f"][0], inp["ssd_a_log_b"][0], inp["ssd_d_skip"][0]])
    m["ssdv"] = rep(sv)
    m["ssd_nw"] = rep(inp["ssd_norm_w"][0])
    m["ssd_w_out"] = inp["ssd_w_out"][0]
    m["conf_w_pw1"] = inp["conf_w_pw1"][0]
    cfv = np.zeros((128, 8, 37), f)
    col = lambda v: np.asarray(v, f).reshape(8, 128).T
    cfv[:, :, 0] = col(inp["conf_b_pw1"][0][:D])
    cfv[:, :, 1] = col(inp["conf_b_pw1"][0][D:])
    cfv[:, :, 2] = col(inp["conf_dw_b"][0])
    cfv[:, :, 3] = col(inp["conf_ln_g"][0])
    cfv[:, :, 4] = col(inp["conf_ln_b"][0])
    cfv[:, :, 5:36] = inp["conf_dw_w"][0].reshape(CK, 8, 128).transpose(2, 1, 0)
    m["cfv"] = cfv
    m["conf_w_pw2"] = inp["conf_w_pw2"][0]
    m["conf_b_pw2"] = np.ascontiguousarray(inp["conf_b_pw2"][0].reshape(1, -1), dtype=f)
    m["ffn_w_in"] = inp["ffn_w_in"]
    m["ffn_w_out"] = inp["ffn_w_out"]
    return m


def kernel(**inputs):
    inp = {k: np.asarray(v) for k, v in inputs.items()}
    nc = build_nc()
    in_maps = [_prep_inputs(inp, b) for b in range(8)]
    res = run_bass_kernel_spmd(nc, in_maps, core_ids=list(range(8)))
    return np.stack([r["out"] for r in res.results], axis=0).astype(np.float32)
```

```python
import contextlib
from contextlib import ExitStack
import numpy as np
import concourse.bass as bass
import concourse.mybir as mybir
from concourse.bass_utils import run_bass_kernel_spmd

F32 = mybir.dt.float32
BF16 = mybir.dt.bfloat16
AF = mybir.ActivationFunctionType
ALU = mybir.AluOpType
AX = mybir.AxisListType

D = 1024
L = 4096
CTX = 256
LT = L + CTX
NT = LT // 128
DI = 2048
NH = 32
HD = 64
NG = 8
NS = 128
CONVD = 4096
PROJ = 6208
FF = 2816
NFT = FF // 128
EPS = 1e-6
CK = 31

ENGS = ("pe", "act", "dve", "pool", "sp")
DBG = {}
EMIT_LOG = None
N_DMA_SEMS = 44
N_HW_SEMS = 28


class Buf:
    __slots__ = ("name", "w", "r", "rd", "excl")

    def __init__(self, name="", excl=False):
        self.name = name
        self.w = None
        self.r = {}
        self.rd = []
        self.excl = excl


def PBuf():
    return Buf("psum", True)


class Ins:
    __slots__ = ("eng", "fn", "deps", "is_dma", "need_inc", "val", "sem", "idx", "kind")

    def __init__(self, eng, fn, is_dma):
        self.eng = eng
        self.fn = fn
        self.deps = []
        self.is_dma = is_dma
        self.need_inc = False
        self.val = None
        self.sem = None
        self.idx = None
        self.kind = "load"


class Prog:
    def __init__(self, nc):
        self.nc = nc
        self.q = {e: [] for e in ENGS}
        self.dma_rr = 0
        self.dma_rr_sw = 0
        self.dma_last = [None] * N_DMA_SEMS
        self.dma_cnt = [0] * N_DMA_SEMS

    def _collect(self, ins, reads, writes):
        ex = [b for b in reads if b.excl]
        if ex:
            reads = [b for b in reads if not b.excl]
            writes = list(writes) + [b for b in ex if b not in writes]
        deps = {}

        def add(d):
            if d is None or d is ins:
                return
            deps[id(d)] = d
        for b in reads:
            add(b.w)
        for b in writes:
            add(b.w)
            for d in b.r.values():
                add(d)
            for d in b.rd:
                add(d)
        out = []
        for d in deps.values():
            if (not d.is_dma) and (not ins.is_dma) and d.eng == "pe" and ins.eng == "pe":
                continue
            out.append(d)
        ins.deps = out
        for d in out:
            d.need_inc = True
        for b in reads:
            if ins.is_dma:
                b.rd.append(ins)
            else:
                b.r[ins.eng] = ins
        for b in writes:
            b.w = ins
            b.r = {}
            b.rd = []

    budget = None

    def _spend(self):
        if self.budget is None:
            return True
        if self.budget <= 0:
            return False
        self.budget -= 1
        return True

    def op(self, eng, fn, reads=(), writes=()):
        if not self._spend():
            return None
        ins = Ins(eng, fn, False)
        self._collect(ins, reads, writes)
        ins.idx = len(self.q[eng])
        self.q[eng].append(ins)
        return ins

    def dma(self, eng, fn, reads=(), writes=(), kind="load"):
        if not self._spend():
            return None
        ins = Ins(eng, fn, True)
        ins.kind = kind
        self._collect(ins, reads, writes)
        if eng == "pool":
            slot = N_HW_SEMS + self.dma_rr_sw
            self.dma_rr_sw = (self.dma_rr_sw + 1) % (N_DMA_SEMS - N_HW_SEMS)
        else:
            slot = self.dma_rr
            self.dma_rr = (self.dma_rr + 1) % N_HW_SEMS
        prev = self.dma_last[slot]
        if prev is not None:
            ins.deps.append(prev)
        self.dma_cnt[slot] += 1
        ins.sem = slot
        ins.val = 16 * self.dma_cnt[slot]
        ins.need_inc = True
        self.dma_last[slot] = ins
        ins.idx = len(self.q[eng])
        self.q[eng].append(ins)
        return ins

    def barrier(self):
        b = Ins("sp", lambda e: e.nop(), False)
        deps = []
        for e in ENGS:
            for ins in reversed(self.q[e]):
                if not ins.is_dma:
                    deps.append(ins)
                    ins.need_inc = True
                    break
        for d in self.dma_last:
            if d is not None:
                deps.append(d)
        b.deps = deps
        b.need_inc = True
        b.idx = len(self.q["sp"])
        self.q["sp"].append(b)
        for e in ENGS:
            if e == "sp":
                continue
            w = Ins(e, lambda eng: eng.nop(), False)
            w.deps = [b]
            w.idx = len(self.q[e])
            self.q[e].append(w)

    def _hoist_loads(self):
        newq = []
        prev_load_pos = -1
        pos = {id(ins): k for k, ins in enumerate(self.q["sp"])}
        for k, ins in enumerate(self.q["sp"]):
            if ins.is_dma and ins.kind == "load":
                j = len(newq)
                while (j > 0 and newq[j - 1].is_dma and newq[j - 1].kind == "store"
                       and pos[id(newq[j - 1])] > prev_load_pos
                       and all(newq[j - 1] is not d for d in ins.deps) and len(newq) - j < 12):
                    j -= 1
                newq.insert(j, ins)
                prev_load_pos = k
            else:
                newq.append(ins)
        self.q["sp"] = newq

    def emit(self):
        nc = self.nc
        if DBG.get('hoist', True):
            self._hoist_loads()
        for e in ENGS:
            c = 0
            for ins in self.q[e]:
                if ins.is_dma:
                    continue
                if ins.need_inc:
                    c += 1
                    ins.val = c
        with ExitStack() as st:
            esem = {e: st.enter_context(nc.semaphore("s_" + e)) for e in ENGS}
            dsem = [st.enter_context(nc.semaphore("d_%d" % i)) for i in range(N_DMA_SEMS)]
            block = st.enter_context(nc.Block())

            def run(e, engobj):
                seen = {}
                for ins in self.q[e]:
                    for d in ins.deps:
                        if d.is_dma:
                            key = ("d", d.sem)
                            sem = dsem[d.sem]
                        else:
                            key = ("c", d.eng)
                            sem = esem[d.eng]
                        if seen.get(key, 0) >= d.val:
                            continue
                        seen[key] = d.val
                        engobj.wait_ge(sem, d.val)
                        if EMIT_LOG is not None:
                            EMIT_LOG.append((e, ins.idx, "wait", key, d.val))
                    if EMIT_LOG is not None:
                        EMIT_LOG.append((e, ins.idx, "ins", ins.is_dma, ins.val if (ins.need_inc or ins.is_dma) else None, ins.sem))
                    r = ins.fn(engobj)
                    if ins.is_dma:
                        r.then_inc(dsem[ins.sem], 16)
                    elif ins.need_inc:
                        r.then_inc(esem[e], 1)
                if e == "sp":
                    for slot in range(N_DMA_SEMS):
                        if self.dma_cnt[slot]:
                            v = 16 * self.dma_cnt[slot]
                            if seen.get(("d", slot), 0) < v:
                                engobj.wait_ge(dsem[slot], v)

            @block.tensor
            def _(pe):
                run("pe", pe)

            @block.scalar
            def _(act):
                run("act", act)

            @block.vector
            def _(dve):
                run("dve", dve)

            @block.gpsimd
            def _(pool):
                run("pool", pool)

            @block.sync
            def _(sp):
                run("sp", sp)


class Ring:
    def __init__(self, items):
        self.items = items
        self.i = 0

    def next(self):
        it = self.items[self.i % len(self.items)]
        self.i += 1
        return it


def _consts():
    i = np.arange(128)
    c = {}
    c["ident"] = np.eye(128, dtype=np.float32)
    c["Uf"] = (i[:, None] > i[None, :]).astype(np.float32)
    c["Ub"] = (i[:, None] < i[None, :]).astype(np.float32)
    c["Rf"] = (i[:, None] <= i[None, :]).astype(np.float32)
    c["Rb"] = (i[:, None] >= i[None, :]).astype(np.float32)
    c["ones"] = np.ones((128, 128), np.float32)
    return np.stack([c[k] for k in ("ident", "Uf", "Ub", "Rf", "Rb", "ones")], axis=1)


CI_ID, CI_UF, CI_UB, CI_RF, CI_RB, CI_ONE = range(6)


def build_nc(dbg=None, stop_after=None):
    nc = bass.Bass("TRN2", target_bir_lowering=False)
    dbg = dbg or {}

    def din(name, shape, dt=F32):
        return nc.dram_tensor(name, list(shape), dt, kind="ExternalInput").ap()

    def dscr(name, shape, dt):
        return nc.dram_tensor(name, list(shape), dt, kind="Internal").ap()

    xin = din("xin", [LT, D])
    ccT = din("ccT", [128, 2, 8])
    ada_w = din("ada_w", [2, D, 6 * D])
    ada_b = din("ada_b", [1, 2 * 6 * D])
    normg = din("normg", [128, 5, D])
    consts = din("consts", [128, 6, 128])
    w_in = din("ssd_w_in", [D, PROJ])
    convw = din("ssd_convw", [128, 32, 5])
    convb = din("ssd_convb", [128, 32])
    ssdv = din("ssdv", [128, 160])
    ssd_nw = din("ssd_nw", [128, DI])
    w_out = din("ssd_w_out", [DI, D])
    pw1 = din("conf_w_pw1", [D, 2 * D])
    cfv = din("cfv", [128, 8, 37])
    pw2 = din("conf_w_pw2", [D, D])
    bpw2 = din("conf_b_pw2", [1, D])
    ffn_wi = din("ffn_w_in", [2, D, 2 * FF])
    ffn_wo = din("ffn_w_out", [2, FF, D])
    out = nc.dram_tensor("out", [L, D], F32, kind="ExternalOutput").ap()
    dbg_ap = {k: nc.dram_tensor("dbg_" + k, list(s), F32, kind="ExternalOutput").ap() for k, s in dbg.items()}

    XBC = dscr("XBC", [CONVD, LT], BF16)
    ZS = dscr("ZS", [L, DI], BF16)
    YP = dscr("YP", [L, DI], F32)
    SBS = dscr("SBS", [NT, 128, DI], F32)
    HA = dscr("HA", [L, D], F32)
    HB = dscr("HB", [L, D], F32)
    VS = dscr("VS", [D, L], BF16)
    WSC = [dscr("WSC%d" % i, [NFT, 128, 8, 256], BF16) for i in range(2)]

    P = Prog(nc)
    B_xin = [Buf() for _ in range(NT)]
    B_XBC_c = [Buf() for _ in range(32)]
    B_ZS = [Buf() for _ in range(32)]
    B_YP = [Buf() for _ in range(32)]
    B_SBS = [Buf() for _ in range(NT)]
    B_HA = [Buf() for _ in range(32)]
    B_HB = [Buf() for _ in range(32)]
    B_VS = [Buf() for _ in range(8)]
    B_out = [Buf() for _ in range(32)]
    B_w = Buf("weights")
    B_wsc = [[Buf() for _ in range(NFT)] for _ in range(2)]
    B_dbg = Buf("dbg")

    def dump(name, src_ap, src_bufs, dst_slice=None):
        if name not in dbg_ap:
            return
        dst = dbg_ap[name] if dst_slice is None else dst_slice(dbg_ap[name])
        P.dma("pool", lambda e: e.dma_start(out=dst, in_=src_ap), reads=src_bufs, writes=[B_dbg])

    with ExitStack() as top:
        uid = [0]

        def sb(st, name, shape, dt):
            uid[0] += 1
            return st.enter_context(nc.sbuf_tensor("%s_%d" % (name, uid[0]), list(shape), dt))

        def ps(st, name, shape, dt):
            uid[0] += 1
            return st.enter_context(nc.psum_tensor("%s_%d" % (name, uid[0]), list(shape), dt))

        cst_f = sb(top, "cst_f", [128, 6, 128], F32)
        cst_b = sb(top, "cst_b", [128, 6, 128], BF16)
        mod = sb(top, "mod", [128, 6 * D], F32)
        B_cst = Buf("cst")
        B_mod = Buf("mod")
        P.dma("sp", lambda e: e.dma_start(out=cst_f[:], in_=consts), reads=[B_w], writes=[B_cst])
        P.dma("pool", lambda e: e.dma_start(out=cst_b[:], in_=consts), reads=[B_w], writes=[B_cst])
        ident = cst_b[:, CI_ID, :]

        def cast_ffn_weights(li):
            for f in range(NFT):
                for two in range(2):
                    P.dma("pool", lambda e, f=f, two=two: e.dma_start(
                        out=WSC[li][f, :, :, two * 128:(two + 1) * 128],
                        in_=ffn_wi[li][:, two * FF + f * 128:two * FF + (f + 1) * 128].rearrange("(k p) n -> p k n", p=128)),
                        reads=[B_w], writes=[B_wsc[li][f]])

        def mod_pass(li, modc=None, B_modc=None):
            with ExitStack() as st:
                cc = sb(st, "cc", [128, 2, 8], F32)
                scT = sb(st, "scT", [128, 2, 8], F32)
                screp = sb(st, "screp", [128, 2, 8, 128], F32)
                ones1 = sb(st, "ones1", [1, 128], F32)
                wr = Ring([(sb(st, "adw%d" % i, [128, 8, 512], F32), Buf()) for i in range(2)])
                br = Ring([(sb(st, "adb%d" % i, [1, 512], F32), Buf()) for i in range(2)])
                pr = Ring([(ps(st, "adp%d" % i, [128, 512], F32), PBuf()) for i in range(2)])
                gt = sb(st, "gtmp", [128, 2, D], F32)
                B_cc, B_sc, B_rep, B_o1, B_gt = Buf(), Buf(), Buf(), Buf(), Buf()
                P.dma("sp", lambda e: e.dma_start(out=cc[:], in_=ccT), reads=[B_w], writes=[B_cc])
                P.op("act", lambda e: e.activation(out=scT[:], in_=cc[:], func=AF.Silu), reads=[B_cc], writes=[B_sc])
                P.op("dve", lambda e: e.tensor_copy(screp[:], scT[:].unsqueeze(3).to_broadcast([128, 2, 8, 128])),
                     reads=[B_sc], writes=[B_rep])
                P.op("pool", lambda e: e.memset(ones1[:], 1.0), writes=[B_o1])
                P.dma("sp", lambda e: e.dma_start(out=gt[:, 0, :], in_=normg[:, li, :]), reads=[B_w], writes=[B_gt])
                P.dma("sp", lambda e: e.dma_start(out=gt[:, 1, :], in_=normg[:, 2 + li, :]), reads=[B_w], writes=[B_gt])
                for blk in range(12):
                    wt, B_wt = wr.next()
                    bt, B_bt = br.next()
                    P.dma("sp", lambda e, wt=wt, blk=blk: e.dma_start(
                        out=wt[:], in_=ada_w[li, :, blk * 512:(blk + 1) * 512].rearrange("(k p) n -> p k n", p=128)),
                        reads=[B_w], writes=[B_wt])
                    P.dma("sp", lambda e, bt=bt, blk=blk: e.dma_start(
                        out=bt[:], in_=ada_b[:, li * 6 * D + blk * 512: li * 6 * D + (blk + 1) * 512]),
                        reads=[B_w], writes=[B_bt])
                    for who in ((0, 1) if (modc is not None and blk < 4) else (0,)):
                        pt, B_pt = pr.next()
                        for k in range(8):
                            P.op("pe", lambda e, pt=pt, wt=wt, k=k, who=who: e.matmul(
                                pt[:], screp[:, who, k, :], wt[:, k, :], start=(k == 0), stop=False),
                                reads=[B_rep, B_wt], writes=[B_pt])
                        P.op("pe", lambda e, pt=pt, bt=bt: e.matmul(pt[:], ones1[:], bt[:], start=False, stop=True),
                             reads=[B_o1, B_bt], writes=[B_pt])
                        dst = mod if who == 0 else modc
                        Bd = B_mod if who == 0 else B_modc
                        P.op("act", lambda e, pt=pt, dst=dst, blk=blk: e.activation(
                            out=dst[:, blk * 512:(blk + 1) * 512], in_=pt[:], func=AF.Copy),
                            reads=[B_pt], writes=[Bd])
                for (slot, gi) in ((1, 0), (4, 1)):
                    P.op("dve", lambda e, slot=slot, gi=gi: e.scalar_tensor_tensor(
                        out=mod[:, slot * D:(slot + 1) * D], in0=mod[:, slot * D:(slot + 1) * D], scalar=1.0,
                        in1=gt[:, gi, :], op0=ALU.add, op1=ALU.mult), reads=[B_mod, B_gt], writes=[B_mod])
                if modc is not None:
                    P.op("dve", lambda e: e.scalar_tensor_tensor(
                        out=modc[:, D:2 * D], in0=modc[:, D:2 * D], scalar=1.0,
                        in1=gt[:, 0, :], op0=ALU.add, op1=ALU.mult), reads=[B_modc, B_gt], writes=[B_modc])
                P.barrier()

        def norm_mod_T(st_tiles, xt, B_xt, gs_ap, sh_ap, B_g, dstT_ap, B_dst, evac_eng="act", part="both"):
            junk, B_junk, ss, B_ss, xn, B_xn, pT, B_pT = st_tiles
            if part == "back":
                for k in range(8):
                    P.op("pe", lambda e, k=k: e.transpose(pT[:, k, :], xn[:, k * 128:(k + 1) * 128], ident),
                         reads=[B_xn, B_cst], writes=[B_pT])
                if evac_eng == "act":
                    P.op("act", lambda e: e.activation(out=dstT_ap, in_=pT[:], func=AF.Copy), reads=[B_pT], writes=[B_dst])
                else:
                    P.op("dve", lambda e: e.tensor_copy(dstT_ap, pT[:]), reads=[B_pT], writes=[B_dst])
                return
            P.op("act", lambda e: e.activation(out=junk[:], in_=xt, func=AF.Square, accum_out=ss[:, 0:1]),
                 reads=[B_xt], writes=[B_junk, B_ss])
            P.op("act", lambda e: e.activation(out=ss[:, 1:2], in_=ss[:, 0:1], func=AF.Ln, bias=eps_t[:, 0:1], scale=1.0 / D),
                 reads=[B_ss, B_eps], writes=[B_ss])
            P.op("act", lambda e: e.activation(out=ss[:, 2:3], in_=ss[:, 1:2], func=AF.Exp, scale=-0.5),
                 reads=[B_ss], writes=[B_ss])
            P.op("dve", lambda e: e.scalar_tensor_tensor(out=junk[:], in0=xt, scalar=ss[:, 2:3], in1=gs_ap,
                                                         op0=ALU.mult, op1=ALU.mult),
                 reads=[B_xt, B_ss, B_g], writes=[B_junk])
            P.op("dve", lambda e: e.tensor_tensor(out=xn[:], in0=junk[:], in1=sh_ap, op=ALU.add),
                 reads=[B_junk, B_g], writes=[B_xn])
            if part == "front":
                return
            for k in range(8):
                P.op("pe", lambda e, k=k: e.transpose(pT[:, k, :], xn[:, k * 128:(k + 1) * 128], ident),
                     reads=[B_xn, B_cst], writes=[B_pT])
            if evac_eng == "act":
                P.op("act", lambda e: e.activation(out=dstT_ap, in_=pT[:], func=AF.Copy), reads=[B_pT], writes=[B_dst])
            else:
                P.op("dve", lambda e: e.tensor_copy(dstT_ap, pT[:]), reads=[B_pT], writes=[B_dst])

        eps_t = sb(top, "eps_t", [128, 1], F32)
        B_eps = Buf()
        P.op("pool", lambda e: e.memset(eps_t[:], EPS), writes=[B_eps])

        def norm_tiles(st, pfx):
            return (sb(st, pfx + "junk", [128, D], F32), Buf(), sb(st, pfx + "ss", [128, 4], F32), Buf(),
                    sb(st, pfx + "xn", [128, D], BF16), Buf(), ps(st, pfx + "pT", [128, 8, 128], BF16), PBuf())

        def ffn_pass(li, hin, B_hin, hout, B_hout, final):
            TB = 512
            with ExitStack() as st:
                xnTr = [(sb(st, "f_xnT%d" % i, [128, 8, TB], BF16), [Buf() for _ in range(TB // 128)]) for i in range(2)]
                hidT = sb(st, "f_hidT", [128, NFT, TB], BF16)
                B_hid = [Buf() for _ in range(NFT)]
                wo = sb(st, "f_wo", [128, NFT, D], BF16)
                B_wo = Buf()
                hresr = Ring([(sb(st, "f_hres%d" % i, [128, TB // 128, D], F32), [Buf() for _ in range(TB // 128)]) for i in range(2)])
                nts = [norm_tiles(st, "f_a"), norm_tiles(st, "f_b")]
                wr = Ring([(sb(st, "f_wi%d" % i, [128, 8, 256], BF16), Buf()) for i in range(4)])
                pu = Ring([(ps(st, "f_pu%d" % i, [128, 512], F32), PBuf()) for i in range(4)])
                po = Ring([(ps(st, "f_po%d" % i, [128, 512], F32), PBuf()) for i in range(2)])
                sg = Ring([(sb(st, "f_sg%d" % i, [128, 512], BF16), Buf()) for i in range(2)])
                ot = Ring([(sb(st, "f_ot%d" % i, [128, D], F32), Buf()) for i in range(2)])
                fjunk = sb(st, "f_fjunk", [128, D], F32)
                fss = sb(st, "f_fss", [128, 4], F32)
                B_fj, B_fss = Buf(), Buf()
                fg = sb(st, "f_fg", [128, D], F32)
                B_fg = Buf()
                if final:
                    P.dma("sp", lambda e: e.dma_start(out=fg[:], in_=normg[:, 4, :]), reads=[B_w], writes=[B_fg])
                P.dma("pool", lambda e: e.dma_start(out=wo[:], in_=ffn_wo[li].rearrange("(k p) n -> p k n", p=128)),
                      reads=[B_w], writes=[B_wo])
                NB = L // TB
                hres_of = {}

                def norm_tile(tb, j, part="both"):
                    if j == 0 and part != "back":
                        hres_of[tb] = hresr.next()
                    hres, B_hres = hres_of[tb]
                    xnT, B_xnT = xnTr[tb % 2]
                    t = tb * (TB // 128) + j
                    if part != "back":
                        P.dma("sp", lambda e: e.dma_start(out=hres[:, j, :], in_=hin[t * 128:(t + 1) * 128, :]),
                              reads=[B_hin[t]], writes=[B_hres[j]])
                    norm_mod_T(nts[j % 2], hres[:, j, :], B_hres[j], mod[:, 4 * D:5 * D], mod[:, 3 * D:4 * D], B_mod,
                               xnT[:, :, j * 128:(j + 1) * 128], B_xnT[j], evac_eng=("act" if j % 2 else "dve"), part=part)

                def up(tb):
                    xnT, B_xnT = xnTr[tb % 2]
                    for f in range(NFT):
                        wt, B_wt = wr.next()
                        P.dma("sp", lambda e, wt=wt, f=f: e.dma_start(out=wt[:], in_=WSC[li][f]),
                              reads=[B_wsc[li][f]], writes=[B_wt])
                        p1, B_p1 = pu.next()
                        p2, B_p2 = pu.next()
                        for k in range(8):
                            P.op("pe", lambda e, p1=p1, wt=wt, k=k: e.matmul(
                                p1[:], wt[:, k, 0:128], xnT[:, k, :], start=(k == 0), stop=(k == 7)),
                                reads=[B_wt] + B_xnT, writes=[B_p1])
                        for k in range(8):
                            P.op("pe", lambda e, p2=p2, wt=wt, k=k: e.matmul(
                                p2[:], wt[:, k, 128:256], xnT[:, k, :], start=(k == 0), stop=(k == 7)),
                                reads=[B_wt] + B_xnT, writes=[B_p2])
                        s1, B_s1 = sg.next()
                        P.op("act", lambda e, s1=s1, p1=p1: e.activation(out=s1[:], in_=p1[:], func=AF.Silu),
                             reads=[B_p1], writes=[B_s1])
                        P.op("dve", lambda e, s1=s1, p2=p2, f=f: e.tensor_tensor(
                            out=hidT[:, f, :], in0=s1[:], in1=p2[:], op=ALU.mult),
                            reads=[B_s1, B_p2], writes=[B_hid[f]])
                        if f == NFT - 6 and tb + 1 < NB:
                            norm_tile(tb + 1, 0, "front")
                        if f == NFT - 1 and tb + 1 < NB:
                            norm_tile(tb + 1, 0, "back")

                def out_tile(tb, j):
                    hres, B_hres = hres_of[tb]
                    t = tb * (TB // 128) + j
                    o, B_o = ot.next()
                    for cb in range(2):
                        pq, B_pq = po.next()
                        for f in range(NFT):
                            P.op("pe", lambda e, pq=pq, f=f, cb=cb: e.matmul(
                                pq[:], hidT[:, f, j * 128:(j + 1) * 128], wo[:, f, cb * 512:(cb + 1) * 512],
                                start=(f == 0), stop=(f == NFT - 1)), reads=[B_hid[f], B_wo], writes=[B_pq])
                        P.op("dve", lambda e, pq=pq, cb=cb: e.tensor_tensor(
                            out=o[:, cb * 512:(cb + 1) * 512], in0=pq[:], in1=mod[:, 5 * D + cb * 512:5 * D + (cb + 1) * 512],
                            op=ALU.mult), reads=[B_pq, B_mod], writes=[B_o])
                    P.op("pool", lambda e: e.tensor_tensor(out=o[:], in0=o[:], in1=hres[:, j, :], op=ALU.add),
                         reads=[B_o, B_hres[j]], writes=[B_o])
                    if final:
                        P.op("act", lambda e: e.activation(out=fjunk[:], in_=o[:], func=AF.Square, accum_out=fss[:, 0:1]),
                             reads=[B_o], writes=[B_fj, B_fss])
                        P.op("act", lambda e: e.activation(out=fss[:, 1:2], in_=fss[:, 0:1], func=AF.Ln, bias=eps_t[:, 0:1], scale=1.0 / D),
                             reads=[B_fss, B_eps], writes=[B_fss])
                        P.op("act", lambda e: e.activation(out=fss[:, 2:3], in_=fss[:, 1:2], func=AF.Exp, scale=-0.5),
                             reads=[B_fss], writes=[B_fss])
                        P.op("dve", lambda e: e.scalar_tensor_tensor(out=o[:], in0=o[:], scalar=fss[:, 2:3], in1=fg[:],
                                                                     op0=ALU.mult, op1=ALU.mult),
                             reads=[B_o, B_fss, B_fg], writes=[B_o])
                    P.dma("sp", lambda e: e.dma_start(out=hout[t * 128:(t + 1) * 128, :], in_=o[:]),
                          reads=[B_o], writes=[B_hout[t]], kind="store")

                for j in range(TB // 128):
                    norm_tile(0, j)
                for tb in range(NB):
                    up(tb)
                    for j in range(TB // 128):
                        nxt = tb + 1 < NB and j + 1 < TB // 128
                        if nxt:
                            norm_tile(tb + 1, j + 1, "front")
                        out_tile(tb, j)
                        if nxt:
                            norm_tile(tb + 1, j + 1, "back")
                P.barrier()

        def ssd_layer():
            with ExitStack() as s0:
                dt_all = sb(s0, "dt_all", [128, NT, 64], F32)
                a_hl = sb(s0, "a_hl", [128, NT, 2, 64], BF16)
                B_dt = [Buf() for _ in range(NT)]
                sv = sb(s0, "sv", [128, 160], F32)
                B_sv = Buf()
                P.dma("sp", lambda e: e.dma_start(out=sv[:], in_=ssdv), reads=[B_w], writes=[B_sv])
                P.op("act", lambda e: e.activation(out=sv[:, 64:128], in_=sv[:, 64:128], func=AF.Exp), reads=[B_sv], writes=[B_sv])
                P.op("dve", lambda e: e.tensor_scalar(out=sv[:, 64:128], in0=sv[:, 64:128], scalar1=-1.0, scalar2=None, op0=ALU.mult),
                     reads=[B_sv], writes=[B_sv])
                with ExitStack() as st:
                    a_all = sb(st, "a_all", [128, NT, 64], F32)
                    modc = sb(st, "modc", [128, 2 * D], F32)
                    B_modc = Buf()
                    mod_pass(0, modc, B_modc)
                    xnT = sb(st, "xnT", [128, 8, LT], BF16)
                    B_xnT = [Buf() for _ in range(NT)]
                    nts = [norm_tiles(st, "a_a"), norm_tiles(st, "a_b")]
                    xr = Ring([(sb(st, "a_x%d" % i, [128, D], F32), Buf()) for i in range(3)])
                    wdt = sb(st, "a_wdt", [128, 8, 64], BF16)
                    B_wdt = Buf()
                    pdt = ps(st, "a_pdt", [128, 512], F32)
                    B_pdt = PBuf()
                    dttr = Ring([(sb(st, "a_dtt%d" % i, [128, 64], F32), Buf()) for i in range(2)])
                    P.dma("pool", lambda e: e.dma_start(out=wdt[:], in_=w_in[:, DI + CONVD:PROJ].rearrange("(k p) n -> p k n", p=128)),
                          reads=[B_w], writes=[B_wdt])
                    xts = {}

                    def a_norm(t, part):
                        if part != "back":
                            xts[t] = xr.next()
                            xt, B_xt = xts[t]
                            P.dma("sp", lambda e: e.dma_start(out=xt[:], in_=xin[t * 128:(t + 1) * 128, :]),
                                  reads=[B_xin[t]], writes=[B_xt])
                        xt, B_xt = xts[t]
                        if t < 2:
                            norm_mod_T(nts[t % 2], xt[:], B_xt, modc[:, D:2 * D], modc[:, 0:D], B_modc, xnT[:, :, t * 128:(t + 1) * 128], B_xnT[t],
                                       part=part)
                        else:
                            norm_mod_T(nts[t % 2], xt[:], B_xt, mod[:, D:2 * D], mod[:, 0:D], B_mod, xnT[:, :, t * 128:(t + 1) * 128], B_xnT[t],
                                       evac_eng=("act" if t % 2 else "dve"), part=part)

                    def a_dt(t):
                        dtt, B_dtt = dttr.next()
                        for k in range(8):
                            P.op("pe", lambda e, k=k: e.matmul(pdt[:, 0:64], xnT[:, k, t * 128:(t + 1) * 128], wdt[:, k, :],
                                                               start=(k == 0), stop=(k == 7)),
                                 reads=[B_xnT[t], B_wdt], writes=[B_pdt])
                        P.op("dve", lambda e: e.tensor_tensor(out=dtt[:], in0=pdt[:, 0:64], in1=sv[:, 0:64], op=ALU.add),
                             reads=[B_pdt, B_sv], writes=[B_dtt])
                        P.op("act", lambda e: e.activation(out=dtt[:], in_=dtt[:], func=AF.Exp), reads=[B_dtt], writes=[B_dtt])
                        P.op("act", lambda e: e.activation(out=dt_all[:, t, :], in_=dtt[:], func=AF.Ln, bias=1.0, scale=1.0),
                             reads=[B_dtt], writes=[B_dt[t]])
                        P.op("dve", lambda e: e.tensor_tensor(out=a_all[:, t, :], in0=dt_all[:, t, :], in1=sv[:, 64:128], op=ALU.mult),
                             reads=[B_dt[t], B_sv], writes=[B_dt[t]])
                        P.op("dve", lambda e: e.tensor_copy(a_hl[:, t, 0, :], a_all[:, t, :]), reads=[B_dt[t]], writes=[B_dt[t]])
                        P.op("dve", lambda e: e.tensor_tensor(out=a_hl[:, t, 1, :], in0=a_all[:, t, :], in1=a_hl[:, t, 0, :], op=ALU.subtract),
                             reads=[B_dt[t]], writes=[B_dt[t]])

                    a_norm(0, "front")
                    for t in range(NT):
                        if t + 1 < NT:
                            a_norm(t + 1, "front")
                        a_norm(t, "back")
                        if t >= 1:
                            a_dt(t - 1)
                    a_dt(NT - 1)
                    cast_ffn_weights(0)
                    dump("xnT", xnT[:, :, 0:LT], B_xnT)
                    dump("dt_all", dt_all[:], B_dt)
                    with ExitStack() as s2:
                        wz = Ring([(sb(s2, "a_wz%d" % i, [128, 8, 512], BF16), Buf()) for i in range(2)])
                        pz = Ring([(ps(s2, "a_pz%d" % i, [128, 512], F32), PBuf()) for i in range(2)])
                        zt = Ring([(sb(s2, "a_zt%d" % i, [128, 512], BF16), Buf()) for i in range(3)])
                        for cbk in range(4):
                            w, B_wz = wz.next()
                            P.dma("pool", lambda e, w=w, cbk=cbk: e.dma_start(
                                out=w[:], in_=w_in[:, cbk * 512:(cbk + 1) * 512].rearrange("(k p) n -> p k n", p=128)),
                                reads=[B_w], writes=[B_wz])
                            for t in range(32):
                                pzz, B_pz = pz.next()
                                for k in range(8):
                                    P.op("pe", lambda e, pzz=pzz, w=w, k=k, t=t: e.matmul(
                                        pzz[:], xnT[:, k, CTX + t * 128:CTX + (t + 1) * 128], w[:, k, :], start=(k == 0), stop=(k == 7)),
                                        reads=[B_xnT[t + 2], B_wz], writes=[B_pz])
                                z, B_z = zt.next()
                                P.op("act", lambda e, z=z, pzz=pzz: e.activation(out=z[:], in_=pzz[:], func=AF.Silu),
                                     reads=[B_pz], writes=[B_z])
                                P.dma("sp", lambda e, z=z, t=t, cbk=cbk: e.dma_start(
                                    out=ZS[t * 128:(t + 1) * 128, cbk * 512:(cbk + 1) * 512], in_=z[:]),
                                    reads=[B_z], writes=[B_ZS[t]], kind="store")
                        P.barrier()
                    with ExitStack() as s2:
                        cw = sb(s2, "b_cw", [128, 32, 5], F32)
                        cb_ = sb(s2, "b_cb", [128, 32], F32)
                        B_cw = Buf()
                        P.dma("sp", lambda e: e.dma_start(out=cw[:], in_=convw), reads=[B_w], writes=[B_cw])
                        P.dma("sp", lambda e: e.dma_start(out=cb_[:], in_=convb), reads=[B_w], writes=[B_cw])
                        PW = LT + 8
                        pre = Ring([(sb(s2, "b_pre%d" % i, [128, PW], BF16), Buf()) for i in range(2)])
                        post = Ring([(sb(s2, "b_post%d" % i, [128, LT], BF16), Buf()) for i in range(2)])
                        dg = Ring([(sb(s2, "b_dg%d" % i, [128, 5, 128], BF16), Buf()) for i in range(2)])
                        wb = Ring([(sb(s2, "b_w%d" % i, [128, 8, 128], BF16), Buf()) for i in range(3)])
                        pb = Ring([(ps(s2, "b_pb%d" % i, [128, 512], F32), PBuf()) for i in range(3)])
                        pc = Ring([(ps(s2, "b_pc%d" % i, [128, 512], F32), PBuf()) for i in range(2)])
                        for (pt_, B_p) in pre.items:
                            P.op("pool", lambda e, pt_=pt_: e.memset(pt_[:], 0.0), writes=[B_p])
                        segs = [(0, 256, 2)] + [(CTX + i * 512, 512, 262 + i * 512) for i in range(8)]
                        for c in range(32):
                            w, B_wb = wb.next()
                            P.dma("pool", lambda e, w=w, c=c: e.dma_start(
                                out=w[:], in_=w_in[:, DI + c * 128:DI + (c + 1) * 128].rearrange("(k p) n -> p k n", p=128)),
                                reads=[B_w], writes=[B_wb])
                            pr_, B_pre = pre.next()
                            po_, B_post = post.next()
                            d_, B_dg = dg.next()
                            for k in range(5):
                                P.op("pool", lambda e, d_=d_, k=k, c=c: e.tensor_scalar(
                                    out=d_[:, k, :], in0=ident, scalar1=cw[:, c, k:k + 1], scalar2=None, op0=ALU.mult),
                                    reads=[B_cst, B_cw], writes=[B_dg])
                            for si, (t0, n, off) in enumerate(segs):
                                p_, B_pb = pb.next()
                                rb = B_xnT[t0 // 128:(t0 + n) // 128]
                                for k in range(8):
                                    P.op("pe", lambda e, p_=p_, w=w, k=k, t0=t0, n=n: e.matmul(
                                        p_[:, 0:n], w[:, k, :], xnT[:, k, t0:t0 + n], start=(k == 0), stop=(k == 7)),
                                        reads=[B_wb] + rb, writes=[B_pb])
                                if si % 2 == 0:
                                    P.op("dve", lambda e, p_=p_, pr_=pr_, off=off, n=n: e.tensor_copy(pr_[:, off:off + n], p_[:, 0:n]),
                                         reads=[B_pb], writes=[B_pre])
                                else:
                                    P.op("act", lambda e, p_=p_, pr_=pr_, off=off, n=n: e.activation(
                                        out=pr_[:, off:off + n], in_=p_[:, 0:n], func=AF.Copy), reads=[B_pb], writes=[B_pre])
                            for si, (t0, n, off) in enumerate(segs):
                                q_, B_pc = pc.next()
                                for k in range(5):
                                    P.op("pe", lambda e, q_=q_, d_=d_, pr_=pr_, k=k, off=off, n=n: e.matmul(
                                        q_[:, 0:n], d_[:, k, :], pr_[:, off - 2 + k:off - 2 + k + n], start=(k == 0), stop=(k == 4)),
                                        reads=[B_dg, B_pre], writes=[B_pc])
                                P.op("act", lambda e, q_=q_, po_=po_, t0=t0, n=n, c=c: e.activation(
                                    out=po_[:, t0:t0 + n], in_=q_[:, 0:n], func=AF.Silu, bias=cb_[:, c:c + 1], scale=1.0),
                                    reads=[B_pc, B_cw], writes=[B_post])
                            P.dma("sp", lambda e, po_=po_, c=c: e.dma_start(out=XBC[c * 128:(c + 1) * 128, :], in_=po_[:]),
                                  reads=[B_post], writes=[B_XBC_c[c]], kind="store")
                            if c == 0:
                                dump("post0", po_[:], [B_post])
                        P.barrier()
                if stop_after == "B":
                    return
                ssd_scan(dt_all, a_hl, B_dt, sv, B_sv)

        def ssd_scan(dt_all, a_hl, B_dt, sv, B_sv):
            XBCv = XBC.rearrange("(c p) t -> p c t", p=128)
            cast_ffn_weights(1)
            with ExitStack() as st:
                hTb = sb(st, "c_hTb", [128, DI], F32)
                B_hTb = Buf()
                with ExitStack() as s2:
                    xbc = Ring([(sb(s2, "c_xbc%d" % i, [128, 32, 128], BF16), Buf()) for i in range(3)])
                    pTx = Ring([(ps(s2, "c_pTx%d" % i, [128, 1024], BF16), PBuf()) for i in range(2)])
                    pXr = Ring([(ps(s2, "c_pX%d" % i, [128, 512], F32), PBuf()) for i in range(2)])
                    pScr = Ring([(ps(s2, "c_pSc%d" % i, [128, 512], F32), PBuf()) for i in range(1)])
                    pYr = Ring([(ps(s2, "c_pY%d" % i, [128, 512], F32), PBuf()) for i in range(1)])
                    pMr = Ring([(ps(s2, "c_pM%d" % i, [128, 512], F32), PBuf()) for i in range(2)])
                    xdtr = Ring([(sb(s2, "c_xdt%d" % i, [128, 2, DI], BF16), Buf()) for i in range(2)])
                    xddr = Ring([(sb(s2, "c_xdd%d" % i, [128, 2, DI], BF16), [Buf() for _ in range(8)]) for i in range(2)])
                    btmr = Ring([(sb(s2, "c_btm%d" % i, [128, 8, 128], BF16), Buf()) for i in range(2)])
                    expor = Ring([(sb(s2, "c_expo%d" % i, [128, 192], F32), Buf()) for i in range(2)])
                    Lbuf = [[(sb(s2, "c_L%d_%d" % (par, i), [128, 4, 128], BF16), Buf()) for i in range(16)] for par in range(2)]

                    def buildL(tt, i):
                        g, dr = i // 2, i % 2
                        hb = dr * 32 + g * 4
                        L_, B_L = Lbuf[tt % 2][i]
                        P.op("pool", lambda e, L_=L_, dr=dr, hb=hb, tt=tt: e.tensor_tensor(
                            out=L_[:],
                            in0=cst_b[:, CI_UF + dr, :].unsqueeze(1).to_broadcast([128, 4, 128]),
                            in1=a_hl[:, tt, 0, hb:hb + 4].unsqueeze(2).to_broadcast([128, 4, 128]),
                            op=ALU.mult), reads=[B_cst, B_dt[tt]], writes=[B_L])
                    dec = Ring([(sb(s2, "c_dec%d" % i, [128, 4, 128], BF16), Buf()) for i in range(3)])
                    Mt = Ring([(sb(s2, "c_M%d" % i, [128, 4, 128], BF16), Buf()) for i in range(2)])
                    sm = Ring([(sb(s2, "c_sm%d" % i, [128, 2, 128], BF16), Buf()) for i in range(2)])
                    yp = Ring([(sb(s2, "c_yp%d" % i, [128, DI], F32), [Buf() for _ in range(8)]) for i in range(2)])
                    sbst = Ring([(sb(s2, "c_sbst%d" % i, [128, DI], F32), [Buf() for _ in range(8)]) for i in range(2)])
                    hTf = sb(s2, "c_hTf", [128, DI], F32)
                    hTf16 = sb(s2, "c_hTf16", [128, DI], BF16)
                    B_hTf = [Buf() for _ in range(8)]
                    B_hTf16 = [Buf() for _ in range(8)]
                    tmpg = Ring([(sb(s2, "c_tmpg%d" % i, [128, 256], F32), Buf()) for i in range(2)])
                    P.op("pool", lambda e: e.memset(hTf[:], 0.0), writes=B_hTf)
                    P.op("pool", lambda e: e.memset(hTf16[:], 0.0), writes=B_hTf16)
                    for i in range(16):
                        buildL(2, i)
                    def make_tile(t):
                        lat = t >= 2
                        xb_, B_xb = xbc.next()
                        xdt, B_xdt = xdtr.next()
                        xdd, B_xdd = xddr.next()
                        btm, B_btm = btmr.next()
                        expo, B_expo = expor.next()
                        y_, B_y = yp.next() if lat else (None, None)
                        st_dec, st_sm = {}, {}
                        box = {}

                        def loads():
                            for q4 in range(4):
                                P.dma("sp", lambda e, q4=q4: e.dma_start(
                                    out=xb_[:, q4 * 8:(q4 + 1) * 8, :], in_=XBCv[:, q4 * 8:(q4 + 1) * 8, t * 128:(t + 1) * 128]),
                                    reads=B_XBC_c, writes=[B_xb])

                        def stageABC(i):
                            g, dr = i // 2, i % 2
                            L_, B_L = Lbuf[t % 2][i]
                            pX, B_pX = pXr.next()
                            for h4 in range(4):
                                P.op("pe", lambda e, pX=pX, L_=L_, h4=h4, dr=dr: e.matmul(
                                    pX[:, h4 * 128:(h4 + 1) * 128], L_[:, h4, :], cst_b[:, CI_RF + dr, :], start=True, stop=True),
                                    reads=[B_L, B_cst], writes=[B_pX])
                            dc, B_dc = dec.next()
                            P.op("act", lambda e, dc=dc, pX=pX: e.activation(
                                out=dc[:].rearrange("p h q -> p (h q)"), in_=pX[:], func=AF.Exp),
                                reads=[B_pX], writes=[B_dc])
                            st_dec[i] = (dc, B_dc)

                        def stageD(g):
                            pSc, B_pSc = pScr.next()
                            P.op("pe", lambda e, pSc=pSc, g=g: e.matmul(
                                pSc[:, 0:128], xb_[:, 16 + g, :], xb_[:, 24 + g, :], start=True, stop=True),
                                reads=[B_xb], writes=[B_pSc])
                            sm_, B_sm = sm.next()
                            P.op("dve", lambda e, sm_=sm_, pSc=pSc: e.tensor_tensor(
                                out=sm_[:], in0=pSc[:, 0:128].unsqueeze(1).to_broadcast([128, 2, 128]),
                                in1=cst_b[:, CI_RF:CI_RB + 1, :], op=ALU.mult),
                                reads=[B_pSc, B_cst], writes=[B_sm])
                            st_sm[g] = (sm_, B_sm)

                        def pro():
                            pts = []
                            for hf in range(2):
                                pt_, B_pt = pTx.next()
                                for j in range(8):
                                    P.op("pe", lambda e, pt_=pt_, hf=hf, j=j: e.transpose(
                                        pt_[:, j * 128:(j + 1) * 128], xb_[:, hf * 8 + j, :], ident),
                                        reads=[B_xb, B_cst], writes=[B_pt])
                                pts.append((pt_, B_pt))
                            pS, B_pS = pMr.next()
                            sm_specs = [(CI_RF, 0, 0), (CI_RB, 32, 32), (CI_UF, 0, 64), (CI_UB, 32, 96)]
                            for (ci, ac, oc) in sm_specs:
                                for hl in range(2):
                                    P.op("pe", lambda e, ci=ci, ac=ac, oc=oc, hl=hl: e.matmul(
                                        pS[:, oc:oc + 32], cst_b[:, ci, :], a_hl[:, t, hl, ac:ac + 32], start=(hl == 0), stop=(hl == 1)),
                                        reads=[B_cst, B_dt[t]], writes=[B_pS])
                            for hl in range(2):
                                P.op("pe", lambda e, hl=hl: e.matmul(pS[:, 128:192], cst_b[:, CI_ONE, :], a_hl[:, t, hl, :],
                                                                     start=(hl == 0), stop=(hl == 1)),
                                     reads=[B_cst, B_dt[t]], writes=[B_pS])
                            P.op("act", lambda e: e.activation(out=expo[:], in_=pS[:, 0:192], func=AF.Exp),
                                 reads=[B_pS], writes=[B_expo])
                            for dr in range(2):
                                for hf in range(2):
                                    pt_, B_pt = pts[hf]
                                    P.op("dve", lambda e, pt_=pt_, dr=dr, hf=hf: e.tensor_tensor(
                                        out=xdt[:, dr, hf * 1024:(hf + 1) * 1024].rearrange("p (h j) -> p h j", j=64),
                                        in0=pt_[:].rearrange("p (h j) -> p h j", j=64),
                                        in1=dt_all[:, t, dr * 32 + hf * 16:dr * 32 + hf * 16 + 16].unsqueeze(2).to_broadcast([128, 16, 64]),
                                        op=ALU.mult), reads=[B_pt, B_dt[t]], writes=[B_xdt])
                            if lat:
                                for hf in range(2):
                                    pt_, B_pt = pts[hf]
                                    P.op("dve", lambda e, pt_=pt_, hf=hf: e.tensor_tensor(
                                        out=y_[:, hf * 1024:(hf + 1) * 1024].rearrange("p (h j) -> p h j", j=64),
                                        in0=pt_[:].rearrange("p (h j) -> p h j", j=64),
                                        in1=sv[:, 128 + hf * 16:128 + hf * 16 + 16].unsqueeze(2).to_broadcast([128, 16, 64]),
                                        op=ALU.mult), reads=[B_pt, B_sv], writes=B_y[hf * 4:(hf + 1) * 4])
                            ptb, B_ptb = pTx.next()
                            for g in range(8):
                                P.op("pe", lambda e, g=g: e.transpose(ptb[:, g * 128:(g + 1) * 128], xb_[:, 16 + g, :], ident),
                                     reads=[B_xb, B_cst], writes=[B_ptb])
                            P.op("act", lambda e: e.activation(out=btm[:].rearrange("p g n -> p (g n)"), in_=ptb[:], func=AF.Copy),
                                 reads=[B_ptb], writes=[B_btm])

                        def stageXdd(g):
                            gs = slice(g * 256, (g + 1) * 256)
                            for dr in range(2):
                                P.op("pool", lambda e, g=g, gs=gs, dr=dr: e.tensor_tensor(
                                    out=xdd[:, dr, gs].rearrange("p (h j) -> p h j", j=64),
                                    in0=xdt[:, dr, gs].rearrange("p (h j) -> p h j", j=64),
                                    in1=expo[:, 64 + dr * 32 + g * 4:64 + dr * 32 + g * 4 + 4].unsqueeze(2).to_broadcast([128, 4, 64]),
                                    op=ALU.mult), reads=[B_xdt, B_expo], writes=[B_xdd[g]])

                        def stageEF(i):
                            g, dr = i // 2, i % 2
                            if dr == 0:
                                box["pY"] = pYr.next()
                            pY, B_pY = box["pY"]
                            dc, B_dc = st_dec[i]
                            sm_, B_sm = st_sm[g]
                            M_, B_M = Mt.next()
                            P.op("dve", lambda e, M_=M_, dc=dc, sm_=sm_, dr=dr: e.tensor_tensor(
                                out=M_[:], in0=dc[:], in1=sm_[:, dr, :].unsqueeze(1).to_broadcast([128, 4, 128]), op=ALU.mult),
                                reads=[B_dc, B_sm], writes=[B_M])
                            for h4 in range(4):
                                h = g * 4 + h4
                                P.op("pe", lambda e, pY=pY, M_=M_, h4=h4, h=h, dr=dr: e.matmul(
                                    pY[:, h4 * 64:(h4 + 1) * 64], M_[:, h4, :], xdt[:, dr, h * 64:(h + 1) * 64],
                                    start=(dr == 0 and h4 == 0), stop=(dr == 1 and h4 == 3)), reads=[B_M, B_xdt], writes=[B_pY])
                            return pY, B_pY

                        def stageG1(g, pYt):
                            gs = slice(g * 256, (g + 1) * 256)
                            pY, B_pY = pYt
                            P.op("dve", lambda e, pY=pY, gs=gs: e.tensor_tensor(out=y_[:, gs], in0=y_[:, gs], in1=pY[:, 0:256], op=ALU.add),
                                 reads=[B_y[g], B_pY], writes=[B_y[g]])

                        def stageG2(g):
                            sbt, B_sbt = box["sbt"]
                            gs = slice(g * 256, (g + 1) * 256)
                            P.op("pool", lambda e, g=g, gs=gs: e.tensor_tensor(
                                out=hTf[:, gs].rearrange("p (h j) -> p h j", j=64),
                                in0=hTf[:, gs].rearrange("p (h j) -> p h j", j=64),
                                in1=expo[:, 128 + g * 4:128 + g * 4 + 4].unsqueeze(2).to_broadcast([128, 4, 64]), op=ALU.mult),
                                reads=[B_hTf[g], B_expo, B_hTf16[g]], writes=[B_hTf[g]])
                            if lat:
                                pO, B_pO = pMr.next()
                                P.op("pe", lambda e, pO=pO, g=g, gs=gs: e.matmul(
                                    pO[:, 0:256], xb_[:, 24 + g, :], hTf16[:, gs], start=True, stop=True),
                                    reads=[B_xb, B_hTf16[g]], writes=[B_pO])
                            pSt, B_pSt = pMr.next()
                            for dr in range(2):
                                P.op("pe", lambda e, pSt=pSt, g=g, dr=dr, gs=gs: e.matmul(
                                    pSt[:, dr * 256:(dr + 1) * 256], btm[:, g, :], xdd[:, dr, gs], start=True, stop=True),
                                    reads=[B_btm, B_xdd[g]], writes=[B_pSt])
                            if lat:
                                tg, B_tg = tmpg.next()
                                P.op("dve", lambda e, tg=tg, pO=pO, g=g: e.tensor_tensor(
                                    out=tg[:].rearrange("p (h j) -> p h j", j=64),
                                    in0=pO[:, 0:256].rearrange("p (h j) -> p h j", j=64),
                                    in1=expo[:, g * 4:g * 4 + 4].unsqueeze(2).to_broadcast([128, 4, 64]), op=ALU.mult),
                                    reads=[B_pO, B_expo], writes=[B_tg])
                            P.op("dve", lambda e, pSt=pSt, gs=gs: e.tensor_tensor(out=hTf[:, gs], in0=hTf[:, gs], in1=pSt[:, 0:256], op=ALU.add),
                                 reads=[B_hTf[g], B_pSt], writes=[B_hTf[g]])
                            if lat:
                                P.op("pool", lambda e, tg=tg, gs=gs: e.tensor_tensor(out=y_[:, gs], in0=y_[:, gs], in1=tg[:], op=ALU.add),
                                     reads=[B_y[g], B_tg], writes=[B_y[g]])
                            P.op("act", lambda e, pSt=pSt, gs=gs: e.activation(out=sbt[:, gs], in_=pSt[:, 256:512], func=AF.Copy),
                                 reads=[B_pSt], writes=[B_sbt[g]])
                            P.op("act", lambda e, gs=gs: e.activation(out=hTf16[:, gs], in_=hTf[:, gs], func=AF.Copy),
                                 reads=[B_hTf[g]], writes=[B_hTf16[g]])

                        def body(inject):
                            box["sbt"] = sbst.next()
                            sbt, B_sbt = box["sbt"]
                            if lat:
                                stageABC(0)
                                stageABC(1)
                                stageD(0)
                                pend = None
                                for i in range(16):
                                    g, dr = i // 2, i % 2
                                    if i + 2 < 16:
                                        stageABC(i + 2)
                                    if dr == 0 and g + 1 < 8:
                                        stageD(g + 1)
                                    if dr == 0:
                                        stageXdd(g)
                                    pYt = stageEF(i)
                                    if dr == 1:
                                        stageG1(g, pYt)
                                        if pend is not None:
                                            stageG2(pend)
                                            if t + 1 < NT:
                                                buildL(t + 1, 2 * pend)
                                                buildL(t + 1, 2 * pend + 1)
                                        pend = g
                                        if g == 3:
                                            inject()
                                stageG2(pend)
                                if t + 1 < NT:
                                    buildL(t + 1, 2 * pend)
                                    buildL(t + 1, 2 * pend + 1)
                            else:
                                for g in range(8):
                                    stageXdd(g)
                                    stageG2(g)
                                    if g == 3:
                                        inject()
                            P.dma("sp", lambda e: e.dma_start(out=SBS[t], in_=sbt[:]), reads=B_sbt, writes=[B_SBS[t]], kind="store")
                            if lat:
                                P.dma("sp", lambda e: e.dma_start(out=YP[(t - 2) * 128:(t - 1) * 128, :], in_=y_[:]),
                                      reads=B_y, writes=[B_YP[t - 2]], kind="store")
                            if t == 1:
                                dump("hf_ctx", hTf[:], B_hTf)

                        return loads, pro, body

                    tiles = {0: make_tile(0), 1: make_tile(1)}
                    tiles[0][0]()
                    tiles[1][0]()
                    tiles[0][1]()
                    for t in range(NT):
                        def inject(t=t):
                            if t + 1 < NT:
                                tiles[t + 1][1]()
                            if t + 2 < NT:
                                tiles[t + 2] = make_tile(t + 2)
                                tiles[t + 2][0]()
                        tiles[t][2](inject)
                        del tiles[t]
                    P.barrier()
                if stop_after == "C":
                    return
                with ExitStack() as s2:
                    wo = sb(s2, "d_wo", [128, 16, D], BF16)
                    B_wo = Buf()
                    P.dma("pool", lambda e: e.dma_start(out=wo[:], in_=w_out.rearrange("(k p) n -> p k n", p=128)), reads=[B_w], writes=[B_wo])
                    nw = sb(s2, "d_nw", [128, DI], F32)
                    B_nw = Buf()
                    P.dma("sp", lambda e: e.dma_start(out=nw[:], in_=ssd_nw), reads=[B_w], writes=[B_nw])
                    hb16 = sb(s2, "d_hb16", [128, DI], BF16)
                    B_hb16 = Buf()
                    P.op("pool", lambda e: e.memset(hTb[:], 0.0), writes=[B_hTb])
                    P.op("pool", lambda e: e.memset(hb16[:], 0.0), writes=[B_hb16])
                    ct = Ring([(sb(s2, "d_ct%d" % i, [128, 8, 128], BF16), Buf()) for i in range(2)])
                    sbl = Ring([(sb(s2, "d_sbl%d" % i, [128, DI], F32), Buf()) for i in range(2)])
                    ypl = Ring([(sb(s2, "d_ypl%d" % i, [128, DI], F32), [Buf() for _ in range(8)]) for i in range(3)])
                    zl = Ring([(sb(s2, "d_zl%d" % i, [128, DI], BF16), Buf()) for i in range(3)])
                    xl = Ring([(sb(s2, "d_xl%d" % i, [128, D], F32), Buf()) for i in range(3)])
                    pG = Ring([(ps(s2, "d_pG%d" % i, [128, 512], F32), PBuf()) for i in range(6)])
                    pTy = Ring([(ps(s2, "d_pTy%d" % i, [128, 1024], BF16), PBuf()) for i in range(2)])
                    expr = Ring([(sb(s2, "d_expd%d" % i, [128, 64], F32), Buf()) for i in range(2)])
                    tgr = Ring([(sb(s2, "d_tga%d" % i, [128, DI], F32), Buf()) for i in range(1)])
                    ynr = Ring([(sb(s2, "d_yn%d" % i, [128, DI], BF16), Buf()) for i in range(1)])
                    ynTr = Ring([(sb(s2, "d_ynT%d" % i, [128, 16, 128], BF16), Buf()) for i in range(2)])
                    dssr = Ring([(sb(s2, "d_ss%d" % i, [128, 4], F32), Buf()) for i in range(2)])
                    djr = Ring([(sb(s2, "d_junk%d" % i, [128, DI], BF16), Buf()) for i in range(1)])
                    ho = Ring([(sb(s2, "d_ho%d" % i, [128, D], F32), Buf()) for i in range(2)])
                    order = [1, 0] + list(range(NT - 1, 1, -1))

                    def front(t):
                        lat = t >= 2
                        s_, B_s = sbl.next()
                        P.dma("sp", lambda e: e.dma_start(out=s_[:], in_=SBS[t]), reads=[B_SBS[t]], writes=[B_s])
                        pS, B_pS = pG.next()
                        ex, B_ex = expr.next()
                        for ci, oc in ((CI_RB, 0), (CI_ONE, 32)):
                            for hl in range(2):
                                P.op("pe", lambda e, ci=ci, oc=oc, hl=hl: e.matmul(
                                    pS[:, oc:oc + 32], cst_b[:, ci, :], a_hl[:, t, hl, 32:64], start=(hl == 0), stop=(hl == 1)),
                                    reads=[B_cst, B_dt[t]], writes=[B_pS])
                        P.op("act", lambda e: e.activation(out=ex[:, 0:64], in_=pS[:, 0:64], func=AF.Exp), reads=[B_pS], writes=[B_ex])
                        C = None
                        if lat:
                            tl = t - 2
                            c_, B_c = ct.next()
                            P.dma("sp", lambda e: e.dma_start(out=c_[:], in_=XBCv[:, 24:32, t * 128:(t + 1) * 128]),
                                  reads=B_XBC_c, writes=[B_c])
                            y_, B_y = ypl.next()
                            P.dma("sp", lambda e: e.dma_start(out=y_[:], in_=YP[tl * 128:(tl + 1) * 128, :]),
                                  reads=[B_YP[tl]], writes=B_y)
                            z_, B_z = zl.next()
                            P.dma("sp", lambda e: e.dma_start(out=z_[:], in_=ZS[tl * 128:(tl + 1) * 128, :]),
                                  reads=[B_ZS[tl]], writes=[B_z])
                            x_, B_x = xl.next()
                            P.dma("sp", lambda e: e.dma_start(out=x_[:], in_=xin[t * 128:(t + 1) * 128, :]),
                                  reads=[B_xin[t]], writes=[B_x])
                            banks = []
                            for k in range(4):
                                pO, B_pO = pG.next()
                                banks.append((pO, B_pO))
                                for h2 in range(2):
                                    g = 2 * k + h2
                                    gs = slice(g * 256, (g + 1) * 256)
                                    P.op("pe", lambda e, pO=pO, g=g, gs=gs, h2=h2: e.matmul(
                                        pO[:, h2 * 256:(h2 + 1) * 256], c_[:, g, :], hb16[:, gs], start=True, stop=True),
                                        reads=[B_c, B_hb16], writes=[B_pO])
                            C = (tl, y_, B_y, z_, B_z, x_, B_x)
                        P.op("dve", lambda e: e.tensor_tensor(
                            out=hTb[:].rearrange("p (h j) -> p h j", j=64), in0=hTb[:].rearrange("p (h j) -> p h j", j=64),
                            in1=ex[:, 32:64].unsqueeze(2).to_broadcast([128, 32, 64]), op=ALU.mult),
                            reads=[B_hTb, B_ex, B_hb16], writes=[B_hTb])
                        P.op("dve", lambda e: e.tensor_tensor(out=hTb[:], in0=hTb[:], in1=s_[:], op=ALU.add),
                             reads=[B_hTb, B_s], writes=[B_hTb])
                        P.op("act", lambda e: e.activation(out=hb16[:], in_=hTb[:], func=AF.Copy), reads=[B_hTb], writes=[B_hb16])
                        def front_b():
                            if not lat:
                                return
                            tga, B_tga = tgr.next()
                            for k in range(4):
                                pO, B_pO = banks[k]
                                P.op("dve", lambda e, pO=pO, k=k: e.tensor_tensor(
                                    out=tga[:, k * 512:(k + 1) * 512].rearrange("p (h j) -> p h j", j=64),
                                    in0=pO[:].rearrange("p (h j) -> p h j", j=64),
                                    in1=ex[:, k * 8:k * 8 + 8].unsqueeze(2).to_broadcast([128, 8, 64]), op=ALU.mult),
                                    reads=[B_pO, B_ex], writes=[B_tga])
                            P.op("dve", lambda e: e.tensor_tensor(out=y_[:], in0=y_[:], in1=tga[:], op=ALU.add),
                                 reads=B_y + [B_tga], writes=B_y)
                        if t == 0:
                            dump("hb_ctx", hTb[:], [B_hTb])
                        return C, front_b

                    def back(C):
                        tl, y_, B_y, z_, B_z, x_, B_x = C
                        yn, B_yn = ynr.next()
                        ynT, B_ynT = ynTr.next()
                        djunk, B_dj = djr.next()
                        dss, B_dss = dssr.next()
                        if tl == 5:
                            dump("y5", y_[:], B_y)
                        P.op("dve", lambda e: e.tensor_tensor(out=y_[:], in0=y_[:], in1=z_[:], op=ALU.mult),
                             reads=B_y + [B_z], writes=B_y)
                        P.op("act", lambda e: e.activation(out=djunk[:], in_=y_[:], func=AF.Square, accum_out=dss[:, 0:1]),
                             reads=B_y, writes=[B_dj, B_dss])
                        P.op("act", lambda e: e.activation(out=dss[:, 1:2], in_=dss[:, 0:1], func=AF.Ln, bias=eps_t[:, 0:1], scale=1.0 / DI),
                             reads=[B_dss, B_eps], writes=[B_dss])
                        P.op("act", lambda e: e.activation(out=dss[:, 2:3], in_=dss[:, 1:2], func=AF.Exp, scale=-0.5), reads=[B_dss], writes=[B_dss])
                        P.op("dve", lambda e: e.scalar_tensor_tensor(out=yn[:], in0=y_[:], scalar=dss[:, 2:3], in1=nw[:],
                                                                     op0=ALU.mult, op1=ALU.mult),
                             reads=B_y + [B_dss, B_nw], writes=[B_yn])
                        for hf in range(2):
                            pt_, B_pt = pTy.next()
                            for j in range(8):
                                P.op("pe", lambda e, pt_=pt_, hf=hf, j=j: e.transpose(
                                    pt_[:, j * 128:(j + 1) * 128], yn[:, (hf * 8 + j) * 128:(hf * 8 + j + 1) * 128], ident),
                                    reads=[B_yn, B_cst], writes=[B_pt])
                            if hf == 0:
                                P.op("act", lambda e, pt_=pt_, hf=hf: e.activation(
                                    out=ynT[:, hf * 8:(hf + 1) * 8, :].rearrange("p k t -> p (k t)"), in_=pt_[:], func=AF.Copy),
                                    reads=[B_pt], writes=[B_ynT])
                            else:
                                P.op("dve", lambda e, pt_=pt_, hf=hf: e.tensor_copy(
                                    ynT[:, hf * 8:(hf + 1) * 8, :].rearrange("p k t -> p (k t)"), pt_[:]),
                                    reads=[B_pt], writes=[B_ynT])
                        h_, B_h = ho.next()
                        for cb in range(2):
                            pq, B_pq = pG.next()
                            for k in range(16):
                                P.op("pe", lambda e, pq=pq, k=k, cb=cb: e.matmul(
                                    pq[:], ynT[:, k, :], wo[:, k, cb * 512:(cb + 1) * 512], start=(k == 0), stop=(k == 15)),
                                    reads=[B_ynT, B_wo], writes=[B_pq])
                            P.op("dve", lambda e, pq=pq, cb=cb: e.tensor_tensor(
                                out=h_[:, cb * 512:(cb + 1) * 512], in0=pq[:], in1=mod[:, 2 * D + cb * 512:2 * D + (cb + 1) * 512], op=ALU.mult),
                                reads=[B_pq, B_mod], writes=[B_h])
                        P.op("pool", lambda e: e.tensor_tensor(out=h_[:], in0=h_[:], in1=x_[:], op=ALU.add),
                             reads=[B_h, B_x], writes=[B_h])
                        P.dma("sp", lambda e: e.dma_start(out=HA[tl * 128:(tl + 1) * 128, :], in_=h_[:]),
                              reads=[B_h], writes=[B_HA[tl]], kind="store")

                    pend = None
                    for t in order:
                        C, fb = front(t)
                        if pend is not None and DBG.get("d_mid", True):
                            back(pend)
                            pend = None
                        fb()
                        if pend is not None:
                            back(pend)
                        pend = C
                    back(pend)
                    P.barrier()

        def conf_layer():
            with ExitStack() as st:
                mod_pass(1)
                xnT = sb(st, "g_xnT", [128, 8, L], BF16)
                B_xnT = [Buf() for _ in range(32)]
                cf = sb(st, "g_cf", [128, 8, 37], F32)
                B_cf = Buf()
                P.dma("sp", lambda e: e.dma_start(out=cf[:], in_=cfv), reads=[B_w], writes=[B_cf])
                with ExitStack() as s2:
                    nts = [norm_tiles(s2, "g_a"), norm_tiles(s2, "g_b")]
                    xr = Ring([(sb(s2, "g_x%d" % i, [128, D], F32), Buf()) for i in range(3)])
                    xts = {}

                    def g_norm(t, part):
                        if part != "back":
                            xts[t] = xr.next()
                            xt, B_xt = xts[t]
                            P.dma("sp", lambda e: e.dma_start(out=xt[:], in_=HB[t * 128:(t + 1) * 128, :]),
                                  reads=[B_HB[t]], writes=[B_xt])
                        xt, B_xt = xts[t]
                        norm_mod_T(nts[t % 2], xt[:], B_xt, mod[:, D:2 * D], mod[:, 0:D], B_mod, xnT[:, :, t * 128:(t + 1) * 128], B_xnT[t],
                                   evac_eng=("act" if t % 2 else "dve"), part=part)

                    g_norm(0, "front")
                    for t in range(32):
                        if t + 1 < 32:
                            g_norm(t + 1, "front")
                        g_norm(t, "back")
                    P.barrier()
                with ExitStack() as s2:
                    HW_ = 94
                    ub = Ring([(sb(s2, "g_u%d" % i, [128, 64 * HW_], BF16), Buf()) for i in range(2)])
                    vb = Ring([(sb(s2, "g_v%d" % i, [128, L], BF16), Buf()) for i in range(2)])
                    dg = Ring([(sb(s2, "g_dg%d" % i, [128, CK, 128], BF16), Buf()) for i in range(2)])
                    wa = Ring([(sb(s2, "g_wa%d" % i, [128, 8, 256], BF16), Buf()) for i in range(2)])
                    pa = Ring([(ps(s2, "g_pa%d" % i, [128, 512], F32), PBuf()) for i in range(4)])
                    pc = Ring([(ps(s2, "g_pc%d" % i, [128, 512], F32), PBuf()) for i in range(3)])
                    sgm = Ring([(sb(s2, "g_sg%d" % i, [128, 512], F32), Buf()) for i in range(2)])
                    for (u_, B_u) in ub.items:
                        P.op("pool", lambda e, u_=u_: e.memset(u_[:], 0.0), writes=[B_u])
                    for c in range(8):
                        hor = c < 4
                        w, B_wa = wa.next()
                        for two in range(2):
                            P.dma("pool", lambda e, w=w, c=c, two=two: e.dma_start(
                                out=w[:, :, two * 128:(two + 1) * 128],
                                in_=pw1[:, two * D + c * 128:two * D + (c + 1) * 128].rearrange("(k p) n -> p k n", p=128)),
                                reads=[B_w], writes=[B_wa])
                        u_, B_u = ub.next()
                        v_, B_v = vb.next()
                        d_, B_dg = dg.next()
                        if c in (4, 5):
                            P.op("pool", lambda e, u_=u_: e.memset(u_[:], 0.0), writes=[B_u])
                        for k in range(CK):
                            P.op("pool" if k % 2 else "dve", lambda e, d_=d_, k=k, c=c: e.tensor_scalar(
                                out=d_[:, k, :], in0=ident, scalar1=cf[:, c, 5 + k:6 + k], scalar2=None, op0=ALU.mult),
                                reads=[B_cst, B_cf], writes=[B_dg])
                        if hor:
                            uv = u_[:].rearrange("p (r w) -> p r w", w=HW_)
                        else:
                            uv = u_[:].rearrange("p (r w) -> p r w", w=64)
                        for i in range(8):
                            p1, B_p1 = pa.next()
                            p2, B_p2 = pa.next()
                            rb = B_xnT[i * 4:(i + 1) * 4]
                            for k in range(8):
                                P.op("pe", lambda e, p1=p1, w=w, k=k, i=i: e.matmul(
                                    p1[:], w[:, k, 0:128], xnT[:, k, i * 512:(i + 1) * 512], start=(k == 0), stop=(k == 7)),
                                    reads=[B_wa] + rb, writes=[B_p1])
                            for k in range(8):
                                P.op("pe", lambda e, p2=p2, w=w, k=k, i=i: e.matmul(
                                    p2[:], w[:, k, 128:256], xnT[:, k, i * 512:(i + 1) * 512], start=(k == 0), stop=(k == 7)),
                                    reads=[B_wa] + rb, writes=[B_p2])
                            s_, B_s = sgm.next()
                            P.op("act", lambda e, s_=s_, p2=p2, c=c: e.activation(out=s_[:], in_=p2[:], func=AF.Sigmoid, bias=cf[:, c, 1:2], scale=1.0),
                                 reads=[B_p2, B_cf], writes=[B_s])
                            if hor:
                                dst = uv[:, i * 8:(i + 1) * 8, 15:79]
                            else:
                                dst = uv[:, 15 + i * 8:15 + (i + 1) * 8, :]
                            P.op("dve", lambda e, dst=dst, p1=p1, s_=s_, c=c: e.scalar_tensor_tensor(
                                out=dst, in0=p1[:].rearrange("p (r w) -> p r w", w=64), scalar=cf[:, c, 0:1],
                                in1=s_[:].rearrange("p (r w) -> p r w", w=64), op0=ALU.add, op1=ALU.mult),
                                reads=[B_p1, B_s, B_cf], writes=[B_u])
                        for i in range(8):
                            q_, B_q = pc.next()
                            for k in range(CK):
                                if hor:
                                    src = uv[:, i * 8:(i + 1) * 8, k:k + 64]
                                else:
                                    src = uv[:, i * 8 + k:i * 8 + k + 8, :]
                                P.op("pe", lambda e, q_=q_, d_=d_, k=k, src=src: e.matmul(
                                    q_[:].rearrange("p (r w) -> p r w", w=64), d_[:, k, :], src, start=(k == 0), stop=(k == CK - 1)),
                                    reads=[B_dg, B_u], writes=[B_q])
                            P.op("act", lambda e, q_=q_, v_=v_, i=i, c=c: e.activation(
                                out=v_[:, i * 512:(i + 1) * 512], in_=q_[:], func=AF.Identity, bias=cf[:, c, 2:3], scale=1.0),
                                reads=[B_q, B_cf], writes=[B_v])
                        P.dma("sp", lambda e, v_=v_, c=c: e.dma_start(out=VS[c * 128:(c + 1) * 128, :], in_=v_[:]),
                              reads=[B_v], writes=[B_VS[c]], kind="store")
                    P.barrier()
                if stop_after == "G":
                    return
                with ExitStack() as s2:
                    VSv = VS.rearrange("(c p) t -> p c t", p=128)
                    w2 = sb(s2, "h_w2", [128, 8, D], BF16)
                    B_w2 = Buf()
                    P.dma("pool", lambda e: e.dma_start(out=w2[:], in_=pw2.rearrange("(k p) n -> p k n", p=128)), reads=[B_w], writes=[B_w2])
                    b2 = sb(s2, "h_b2", [1, D], BF16)
                    ones1 = sb(s2, "h_ones1", [1, 128], BF16)
                    B_b2 = Buf()
                    P.dma("pool", lambda e: e.dma_start(out=b2[:], in_=bpw2), reads=[B_w], writes=[B_b2])
                    P.op("pool", lambda e: e.memset(ones1[:], 1.0), writes=[B_b2])
                    vt = Ring([(sb(s2, "h_vt%d" % i, [128, 8, 512], BF16), Buf()) for i in range(2)])
                    sq = Ring([(sb(s2, "h_sq%d" % i, [128, 512], BF16), Buf()) for i in range(3)])
                    pst = Ring([(ps(s2, "h_pst%d" % i, [128, 512], F32), PBuf()) for i in range(4)])
                    mur = Ring([(sb(s2, "h_mu%d" % i, [128, 512], F32), Buf()) for i in range(2)])
                    rsr = Ring([(sb(s2, "h_rs%d" % i, [128, 512], F32), Buf()) for i in range(2)])
                    tn = Ring([(sb(s2, "h_tn%d" % i, [128, 512], F32), Buf()) for i in range(3)])
                    sTr = Ring([(sb(s2, "h_sT%d" % i, [128, 8, 512], BF16), Buf()) for i in range(2)])
                    po = Ring([(ps(s2, "h_po%d" % i, [128, 512], F32), PBuf()) for i in range(3)])
                    xr = Ring([(sb(s2, "h_x%d" % i, [128, D], F32), Buf()) for i in range(2)])
                    ho = Ring([(sb(s2, "h_ho%d" % i, [128, D], F32), Buf()) for i in range(2)])
                    hctx = {}

                    def h_front(i):
                        v_, B_v = vt.next()
                        mu, B_mu = mur.next()
                        rs, B_rs = rsr.next()
                        sT, B_sT = sTr.next()
                        P.dma("sp", lambda e, v_=v_, i=i, mu=mu, rs=rs, sT=sT: e.dma_start(out=v_[:], in_=VSv[:, :, i * 512:(i + 1) * 512]), reads=B_VS, writes=[B_v])
                        p_s, B_ps = pst.next()
                        p_q, B_pq2 = pst.next()
                        for c in range(8):
                            P.op("pe", lambda e, p_s=p_s, v_=v_, c=c, mu=mu, rs=rs, sT=sT: e.matmul(p_s[:], cst_b[:, CI_ONE, :], v_[:, c, :], start=(c == 0), stop=(c == 7)),
                                 reads=[B_cst, B_v], writes=[B_ps])
                        for c in range(8):
                            s_, B_s = sq.next()
                            P.op("pool" if c % 2 else "dve", lambda e, s_=s_, v_=v_, c=c, mu=mu, rs=rs, sT=sT: e.tensor_tensor(out=s_[:], in0=v_[:, c, :], in1=v_[:, c, :], op=ALU.mult),
                                 reads=[B_v], writes=[B_s])
                            P.op("pe", lambda e, p_q=p_q, s_=s_, c=c, mu=mu, rs=rs, sT=sT: e.matmul(p_q[:], cst_b[:, CI_ONE, :], s_[:], start=(c == 0), stop=(c == 7)),
                                 reads=[B_cst, B_s], writes=[B_pq2])
                        P.op("act", lambda e, p_s=p_s, mu=mu, rs=rs, sT=sT: e.activation(out=mu[:], in_=p_s[:], func=AF.Copy, scale=1.0 / D), reads=[B_ps], writes=[B_mu])
                        P.op("dve", lambda e, mu=mu, rs=rs, sT=sT: e.tensor_tensor(out=rs[:], in0=mu[:], in1=mu[:], op=ALU.mult), reads=[B_mu], writes=[B_rs])
                        P.op("dve", lambda e, p_q=p_q, mu=mu, rs=rs, sT=sT: e.scalar_tensor_tensor(out=rs[:], in0=p_q[:], scalar=1.0 / D, in1=rs[:], op0=ALU.mult, op1=ALU.subtract),
                             reads=[B_pq2, B_rs], writes=[B_rs])
                        P.op("act", lambda e, mu=mu, rs=rs, sT=sT: e.activation(out=rs[:], in_=rs[:], func=AF.Ln, bias=eps_t[:, 0:1], scale=1.0), reads=[B_rs, B_eps], writes=[B_rs])
                        P.op("act", lambda e, mu=mu, rs=rs, sT=sT: e.activation(out=rs[:], in_=rs[:], func=AF.Exp, scale=-0.5), reads=[B_rs], writes=[B_rs])
                        for c in range(8):
                            t_, B_t = tn.next()
                            P.op("dve", lambda e, t_=t_, v_=v_, c=c, mu=mu, rs=rs, sT=sT: e.tensor_tensor(out=t_[:], in0=v_[:, c, :], in1=mu[:], op=ALU.subtract),
                                 reads=[B_v, B_mu], writes=[B_t])
                            P.op("pool", lambda e, t_=t_, mu=mu, rs=rs, sT=sT: e.tensor_tensor(out=t_[:], in0=t_[:], in1=rs[:], op=ALU.mult), reads=[B_t, B_rs], writes=[B_t])
                            P.op("act", lambda e, t_=t_, c=c, mu=mu, rs=rs, sT=sT: e.activation(out=sT[:, c, :], in_=t_[:], func=AF.Silu, bias=cf[:, c, 4:5], scale=cf[:, c, 3:4]),
                                 reads=[B_t, B_cf], writes=[B_sT])
                        hctx[i] = (sT, B_sT)

                    def h_back(i):
                        sT, B_sT = hctx.pop(i)
                        mu = rs = None
                        for j in range(4):
                            t = i * 4 + j
                            xt, B_xt = xr.next()
                            P.dma("sp", lambda e, xt=xt, t=t, mu=mu, rs=rs, sT=sT: e.dma_start(out=xt[:], in_=HB[t * 128:(t + 1) * 128, :]), reads=[B_HB[t]], writes=[B_xt])
                            h_, B_h = ho.next()
                            for cb in range(2):
                                pq, B_pq = po.next()
                                for c in range(8):
                                    P.op("pe", lambda e, pq=pq, c=c, j=j, cb=cb, mu=mu, rs=rs, sT=sT: e.matmul(
                                        pq[:], sT[:, c, j * 128:(j + 1) * 128], w2[:, c, cb * 512:(cb + 1) * 512], start=(c == 0), stop=False),
                                        reads=[B_sT, B_w2], writes=[B_pq])
                                P.op("pe", lambda e, pq=pq, cb=cb, mu=mu, rs=rs, sT=sT: e.matmul(pq[:], ones1[:], b2[:, cb * 512:(cb + 1) * 512], start=False, stop=True),
                                     reads=[B_b2], writes=[B_pq])
                                P.op("dve", lambda e, h_=h_, pq=pq, cb=cb, mu=mu, rs=rs, sT=sT: e.tensor_tensor(
                                    out=h_[:, cb * 512:(cb + 1) * 512], in0=pq[:], in1=mod[:, 2 * D + cb * 512:2 * D + (cb + 1) * 512], op=ALU.mult),
                                    reads=[B_pq, B_mod], writes=[B_h])
                            P.op("pool", lambda e, h_=h_, xt=xt, mu=mu, rs=rs, sT=sT: e.tensor_tensor(out=h_[:], in0=h_[:], in1=xt[:], op=ALU.add),
                                 reads=[B_h, B_xt], writes=[B_h])
                            P.dma("sp", lambda e, h_=h_, t=t, mu=mu, rs=rs, sT=sT: e.dma_start(out=HA[t * 128:(t + 1) * 128, :], in_=h_[:]), reads=[B_h], writes=[B_HA[t]], kind="store")

                    h_front(0)
                    for i in range(8):
                        if i + 1 < 8:
                            h_front(i + 1)
                        h_back(i)
                    P.barrier()

        ssd_layer()
        if stop_after in (None, "D", "E", "G", "H"):
            if stop_after != "D":
                ffn_pass(0, HA, B_HA, HB, B_HB, final=False)
            if stop_after in (None, "G", "H"):
                conf_layer()
            if stop_after is None:
                ffn_pass(1, HA, B_HA, out, B_out, final=True)
        for name, (src, bufs) in {"HA": (HA, B_HA), "HB": (HB, B_HB)}.items():
            if name in dbg_ap:
                P.dma("sp", lambda e, name=name, src=src: e.dma_start(out=dbg_ap[name], in_=src), reads=bufs, writes=[B_dbg])
        P.emit()
    build_nc.last_prog = P
    return nc


def _prep_inputs(inp, b):
    f = np.float32
    rep = lambda v: np.ascontiguousarray(np.broadcast_to(np.asarray(v, f).reshape(1, -1), (128, np.asarray(v).size)))
    m = {}
    m["xin"] = np.ascontiguousarray(np.concatenate([inp["ctx"][b], inp["x"][b]], axis=0), dtype=f)
    cc = np.stack([inp["c"][b].reshape(8, 128).T, inp["c_ctx"].reshape(8, 128).T], axis=1)
    m["ccT"] = np.ascontiguousarray(cc, dtype=f)
    m["ada_w"] = inp["ada_w"]
    m["ada_b"] = np.ascontiguousarray(inp["ada_b"].reshape(1, -1), dtype=f)
    ng = np.stack([inp["norm_mix_g"][0], inp["norm_mix_g"][1], inp["norm_ffn_g"][0], inp["norm_ffn_g"][1], inp["final_norm_g"]], axis=0)
    m["normg"] = np.ascontiguousarray(np.broadcast_to(ng[None], (128, 5, D)), dtype=f)
    m["consts"] = _consts()
    m["ssd_w_in"] = inp["ssd_w_in"][0]
    m["ssd_convw"] = np.ascontiguousarray(inp["ssd_conv_w"][0].reshape(5, 32, 128).transpose(2, 1, 0), dtype=f)
    m["ssd_convb"] = np.ascontiguousarray(inp["ssd_conv_b"][0].reshape(32, 128).T, dtype=f)
    sv = np.concatenate([inp["ssd_dt_bias_f"][0], inp["ssd_dt_bias_b"][0], inp["ssd_a_log_f"][0], inp["ssd_a_log_b"][0], inp["ssd_d_skip"][0]])
    m["ssdv"] = rep(sv)
    m["ssd_nw"] = rep(inp["ssd_norm_w"][0])
    m["ssd_w_out"] = inp["ssd_w_out"][0]
    m["conf_w_pw1"] = inp["conf_w_pw1"][0]
    cfv = np.zeros((128, 8, 37), f)
    col = lambda v: np.asarray(v, f).reshape(8, 128).T
    cfv[:, :, 0] = col(inp["conf_b_pw1"][0][:D])
    cfv[:, :, 1] = col(inp["conf_b_pw1"][0][D:])
    cfv[:, :, 2] = col(inp["conf_dw_b"][0])
    cfv[:, :, 3] = col(inp["conf_ln_g"][0])
    cfv[:, :, 4] = col(inp["conf_ln_b"][0])
    cfv[:, :, 5:36] = inp["conf_dw_w"][0].reshape(CK, 8, 128).transpose(2, 1, 0)
    m["cfv"] = cfv
    m["conf_w_pw2"] = inp["conf_w_pw2"][0]
    m["conf_b_pw2"] = np.ascontiguousarray(inp["conf_b_pw2"][0].reshape(1, -1), dtype=f)
    m["ffn_w_in"] = inp["ffn_w_in"]
    m["ffn_w_out"] = inp["ffn_w_out"]
    return m


def kernel(**inputs):
    inp = {k: np.asarray(v) for k, v in inputs.items()}
    nc = build_nc()
    in_maps = [_prep_inputs(inp, b) for b in range(8)]
    res = run_bass_kernel_spmd(nc, in_maps, core_ids=list(range(8)))
    return np.stack([r["out"] for r in res.results], axis=0).astype(np.float32)
```
